# Optimizing a Trainium2 kernel written in Bass

```python
import math, functools
import jax, jax.numpy as jnp
from jax import lax
import numpy as np

D_MODEL = 1024
BATCH = 8
SEQ = 2048
DEPTH = 1
DEC_BATCH = 16
DEC_SEQ = 64
PAST_LEN = 2048

CHUNK = 64
A_HEADS = 8
A_HEAD_DIM = 64
A_PAST_CHUNKS = 8
A_BAND = A_PAST_CHUNKS * CHUNK
A_REL_CLIP = 128
B_HEADS = 8
B_HEAD_DIM = 64
Q_BLOCK = 128
D_FF = 2816
CONV_W = 3
EPS = 1e-6
A_WIDTH = A_HEADS * A_HEAD_DIM
B_WIDTH = B_HEADS * B_HEAD_DIM
IN_SIZES = [A_WIDTH, A_WIDTH, A_WIDTH, B_WIDTH, B_WIDTH, B_WIDTH, B_HEADS, D_MODEL, D_MODEL]
IN_COLS = sum(IN_SIZES)
IN_SPLITS = [int(o) for o in np.cumsum(IN_SIZES)[:-1]]

kernel_name = 'streaming_band_fox_hybrid_step'


def rmsnorm(x, g):
    xf = x.astype(jnp.float32)
    y = xf * lax.rsqrt(jnp.mean(xf * xf, axis=-1, keepdims=True) + EPS) * g.astype(jnp.float32)
    return y.astype(x.dtype)


def split_in_proj(z, b_f):
    lead = z.shape[:-1]
    parts = jnp.split(z, IN_SPLITS, axis=-1)
    q_a, k_a, v_a = [p.reshape(lead + (A_HEADS, A_HEAD_DIM)) for p in parts[0:3]]
    q_b, k_b, v_b = [p.reshape(lead + (B_HEADS, B_HEAD_DIM)) for p in parts[3:6]]
    log_f = jax.nn.log_sigmoid((parts[6] + b_f).astype(jnp.float32))
    return q_a, k_a, v_a, q_b, k_b, v_b, log_f, parts[7], parts[8]


def rel_bias(rel_table, dist):
    idx = jnp.clip(dist, -A_REL_CLIP, A_REL_CLIP) + A_REL_CLIP
    return rel_table[:, idx].astype(jnp.float32)


def band_attn_prompt(q, k, v, rel_table):
    b, s, h, d = q.shape
    nc = s // CHUNK
    band = (A_PAST_CHUNKS + 1) * CHUNK
    pad = jnp.zeros((b, A_BAND, h, d), k.dtype)
    kp = jnp.concatenate([pad, k], axis=1).reshape(b, nc + A_PAST_CHUNKS, CHUNK, h, d)
    vp = jnp.concatenate([pad.astype(v.dtype), v], axis=1).reshape(b, nc + A_PAST_CHUNKS, CHUNK, h, d)
    kb = jnp.concatenate([kp[:, j:j + nc] for j in range(A_PAST_CHUNKS + 1)], axis=2)
    vb = jnp.concatenate([vp[:, j:j + nc] for j in range(A_PAST_CHUNKS + 1)], axis=2)
    qc = q.reshape(b, nc, CHUNK, h, d)
    scores = jnp.einsum('bcqhd,bckhd->bhcqk', qc, kb).astype(jnp.float32) * (d ** -0.5)
    koff = jnp.arange(band) - A_BAND
    dist = jnp.arange(CHUNK)[:, None] - koff[None, :]
    bias = rel_bias(rel_table, dist)
    kpos = jnp.arange(nc)[:, None] * CHUNK + koff[None, :]
    scores = jnp.where((kpos >= 0)[None, None, :, None, :], scores + bias[None, :, None], -jnp.inf)
    p = jax.nn.softmax(scores, axis=-1)
    out = jnp.einsum('bhcqk,bckhd->bcqhd', p.astype(v.dtype), vb)
    return out.reshape(b, s, h * d)


def band_attn_sample(q, k, v, k_cache, v_cache, rel_table):
    b, t, h, d = q.shape
    L = k_cache.shape[1]
    kk = jnp.concatenate([k_cache.astype(k.dtype), k], axis=1)
    vv = jnp.concatenate([v_cache.astype(v.dtype), v], axis=1)
    scores = jnp.einsum('bqhd,bkhd->bhqk', q, kk).astype(jnp.float32) * (d ** -0.5)
    kpos = jnp.concatenate([jnp.arange(-L, 0), jnp.arange(t)])
    dist = jnp.arange(t)[:, None] - kpos[None, :]
    p = jax.nn.softmax(scores + rel_bias(rel_table, dist)[None], axis=-1)
    out = jnp.einsum('bhqk,bkhd->bqhd', p.astype(v.dtype), vv)
    return out.reshape(b, t, h * d)


def forget_attn_prompt(q, k, v, log_f):
    b, s, h, d = q.shape
    nb = s // Q_BLOCK
    c = jnp.cumsum(log_f, axis=1).transpose(0, 2, 1)
    qb = q.reshape(b, nb, Q_BLOCK, h, d).transpose(1, 0, 2, 3, 4)
    cb = c.reshape(b, h, nb, Q_BLOCK).transpose(2, 0, 1, 3)
    kpos = jnp.arange(s)

    def one_block(args):
        qi, ci, i = args
        sc = jnp.einsum('bqhd,bkhd->bhqk', qi, k).astype(jnp.float32) * (d ** -0.5)
        sc = sc + ci[..., None] - c[:, :, None, :]
        qpos = i * Q_BLOCK + jnp.arange(Q_BLOCK)
        sc = jnp.where((kpos[None, :] <= qpos[:, None])[None, None], sc, -jnp.inf)
        p = jax.nn.softmax(sc, axis=-1)
        return jnp.einsum('bhqk,bkhd->bqhd', p.astype(v.dtype), v)

    out = lax.map(one_block, (qb, cb, jnp.arange(nb)))
    return out.transpose(1, 0, 2, 3, 4).reshape(b, s, h * d)


def forget_attn_sample(q, k, v, log_f, k_cache, v_cache, logf_cache):
    b, t, h, d = q.shape
    L = k_cache.shape[1]
    kk = jnp.concatenate([k_cache.astype(k.dtype), k], axis=1)
    vv = jnp.concatenate([v_cache.astype(v.dtype), v], axis=1)
    c = jnp.cumsum(jnp.concatenate([logf_cache.astype(jnp.float32), log_f], axis=1), axis=1).transpose(0, 2, 1)
    sc = jnp.einsum('bqhd,bkhd->bhqk', q, kk).astype(jnp.float32) * (d ** -0.5)
    sc = sc + c[:, :, L:, None] - c[:, :, None, :]
    kpos = jnp.arange(L + t)
    qpos = L + jnp.arange(t)
    sc = jnp.where((kpos[None, :] <= qpos[:, None])[None, None], sc, -jnp.inf)
    p = jax.nn.softmax(sc, axis=-1)
    out = jnp.einsum('bhqk,bkhd->bqhd', p.astype(v.dtype), vv)
    return out.reshape(b, t, h * d)


def conv_ffn(h, buf, w_up, conv_w, conv_b, w_down):
    t = h.shape[1]
    u = h @ w_up
    ext = jnp.concatenate([buf.astype(u.dtype), u], axis=1)
    y = conv_b
    for j in range(CONV_W):
        y = y + ext[:, j:j + t] * conv_w[j]
    gate, val = jnp.split(y, 2, axis=-1)
    return (jax.nn.gelu(gate) * val) @ w_down, ext[:, -(CONV_W - 1):]


def layer(x, attn_a, attn_b, conv_buf, g_pre_mix, g_post_mix, g_pre_ffn, g_post_ffn,
          w_in, b_f, w_proj_a, w_proj_b, w_out, w_up, conv_w, conv_b, w_down):
    h = rmsnorm(x, g_pre_mix)
    q_a, k_a, v_a, q_b, k_b, v_b, log_f, g_a, g_b = split_in_proj(h @ w_in, b_f)
    o_a = attn_a(q_a, k_a, v_a)
    o_b = attn_b(q_b, k_b, v_b, log_f)
    merged = jax.nn.sigmoid(g_a) * (o_a @ w_proj_a) + jax.nn.sigmoid(g_b) * (o_b @ w_proj_b)
    x = x + rmsnorm(merged @ w_out, g_post_mix)
    f, new_buf = conv_ffn(rmsnorm(x, g_pre_ffn), conv_buf, w_up, conv_w, conv_b, w_down)
    x = x + rmsnorm(f, g_post_ffn)
    return x, (k_a, v_a, k_b, v_b, log_f, new_buf)


def setup_inputs(seed: int = 0) -> dict:
    key = jax.random.key(seed)
    ks = jax.random.split(key, 24)
    nrm = jax.random.normal
    a_len = min(A_BAND, PAST_LEN)
    f32 = jnp.float32
    return {
        'x_prompt': nrm(ks[0], (BATCH, SEQ, D_MODEL), f32),
        'x_sample': nrm(ks[1], (DEC_BATCH, DEC_SEQ, D_MODEL), f32),
        'cache_k_a': nrm(ks[2], (DEPTH, DEC_BATCH, a_len, A_HEADS, A_HEAD_DIM), f32),
        'cache_v_a': nrm(ks[3], (DEPTH, DEC_BATCH, a_len, A_HEADS, A_HEAD_DIM), f32),
        'cache_k_b': nrm(ks[4], (DEPTH, DEC_BATCH, PAST_LEN, B_HEADS, B_HEAD_DIM), f32),
        'cache_v_b': nrm(ks[5], (DEPTH, DEC_BATCH, PAST_LEN, B_HEADS, B_HEAD_DIM), f32),
        'cache_logf_b': jax.nn.log_sigmoid(2.0 + nrm(ks[6], (DEPTH, DEC_BATCH, PAST_LEN, B_HEADS), f32)),
        'state_conv_ffn': nrm(ks[7], (DEPTH, DEC_BATCH, CONV_W - 1, 2 * D_FF), f32),
        'g_pre_mix': 1.0 + 0.05 * nrm(ks[8], (DEPTH, D_MODEL), f32),
        'g_post_mix': 1.0 + 0.05 * nrm(ks[9], (DEPTH, D_MODEL), f32),
        'g_pre_ffn': 1.0 + 0.05 * nrm(ks[10], (DEPTH, D_MODEL), f32),
        'g_post_ffn': 1.0 + 0.05 * nrm(ks[11], (DEPTH, D_MODEL), f32),
        'w_in': nrm(ks[12], (DEPTH, D_MODEL, IN_COLS), f32) * D_MODEL ** -0.5,
        'b_f': 2.0 + 0.5 * nrm(ks[13], (DEPTH, B_HEADS), f32),
        'rel_table': 0.5 * nrm(ks[14], (DEPTH, A_HEADS, 2 * A_REL_CLIP + 1), f32),
        'w_proj_a': nrm(ks[15], (DEPTH, A_WIDTH, D_MODEL), f32) * A_WIDTH ** -0.5,
        'w_proj_b': nrm(ks[16], (DEPTH, B_WIDTH, D_MODEL), f32) * B_WIDTH ** -0.5,
        'w_out': nrm(ks[17], (DEPTH, D_MODEL, D_MODEL), f32) * D_MODEL ** -0.5,
        'w_up': nrm(ks[18], (DEPTH, D_MODEL, 2 * D_FF), f32) * D_MODEL ** -0.5,
        'conv_w': nrm(ks[19], (DEPTH, CONV_W, 2 * D_FF), f32) * CONV_W ** -0.5,
        'conv_b': 0.02 * nrm(ks[20], (DEPTH, 2 * D_FF), f32),
        'w_down': nrm(ks[21], (DEPTH, D_FF, D_MODEL), f32) * D_FF ** -0.5,
    }


def reference(x_prompt, x_sample, cache_k_a, cache_v_a, cache_k_b, cache_v_b, cache_logf_b,
              state_conv_ffn, g_pre_mix, g_post_mix, g_pre_ffn, g_post_ffn, w_in, b_f, rel_table,
              w_proj_a, w_proj_b, w_out, w_up, conv_w, conv_b, w_down):
    x_p, x_s = x_prompt, x_sample
    p_states, s_states = [], []
    for l in range(DEPTH):
        w = (g_pre_mix[l], g_post_mix[l], g_pre_ffn[l], g_post_ffn[l], w_in[l], b_f[l],
             w_proj_a[l], w_proj_b[l], w_out[l], w_up[l], conv_w[l], conv_b[l], w_down[l])
        buf0 = jnp.zeros((x_p.shape[0], CONV_W - 1, 2 * D_FF), x_p.dtype)
        x_p, (ka, va, kb, vb, lf, cv) = layer(
            x_p, functools.partial(band_attn_prompt, rel_table=rel_table[l]),
            forget_attn_prompt, buf0, *w)
        keep = min(A_BAND, x_p.shape[1])
        p_states.append((ka[:, -keep:], va[:, -keep:], kb, vb, lf, cv))
        x_s, st = layer(
            x_s,
            functools.partial(band_attn_sample, k_cache=cache_k_a[l], v_cache=cache_v_a[l],
                              rel_table=rel_table[l]),
            functools.partial(forget_attn_sample, k_cache=cache_k_b[l], v_cache=cache_v_b[l],
                              logf_cache=cache_logf_b[l]),
            state_conv_ffn[l], *w)
        s_states.append(st)
    ka_p, va_p, kb_p, vb_p, lf_p, cv_p = [jnp.stack(s) for s in zip(*p_states)]
    ka_s, va_s, kb_s, vb_s, lf_s, cv_s = [jnp.stack(s) for s in zip(*s_states)]
    return (x_p, x_s, ka_p, va_p, kb_p, vb_p, lf_p, cv_p, ka_s, va_s, kb_s, vb_s, lf_s, cv_s)
```

```python
import contextlib
import numpy as np
import concourse.bass as bass
import concourse.mybir as mybir
from concourse.bass_utils import run_bass_kernel_spmd

F32 = mybir.dt.float32
BF16 = mybir.dt.bfloat16
AF = mybir.ActivationFunctionType
ALU = mybir.AluOpType

ENGS = ("pe", "act", "dve", "pool", "sp")
N_DMA_SEMS = 40
N_POOL_SEMS = 8


class Op:
    __slots__ = ("eng", "fn", "dma", "deps", "has_dep", "ticket", "sem")

    def __init__(self, eng, fn, dma):
        self.eng = eng
        self.fn = fn
        self.dma = dma
        self.deps = []
        self.has_dep = False
        self.ticket = None
        self.sem = None


class Prog:
    def __init__(self):
        self.ops = {e: [] for e in ENGS}
        self.last_w = {}
        self.readers = {}
        self.dma_count = 0
        self.pool_dma_count = 0
        self.dma_last = {}

    def op(self, eng, fn, reads=(), writes=(), dma=False):
        o = Op(eng, fn, dma)
        deps = {}

        def add(d, kind):
            if d is o:
                return
            if d.eng == eng and not d.dma and not dma:
                if eng == "pe":
                    return
            deps[id(d)] = d

        for k in reads:
            if isinstance(k, str) and k.startswith("ps"):
                k = k.split(".")[0]
                w = self.last_w.get(k)
                if w is not None:
                    add(w, "raw")
                for rd in self.readers.get(k, ()):
                    if rd.eng != eng:
                        add(rd, "war")
                self.readers.setdefault(k, []).append(o)
            else:
                w = self.last_w.get(k)
                if w is not None:
                    add(w, "raw")
                self.readers.setdefault(k, []).append(o)
        for k in writes:
            if isinstance(k, str) and k.startswith("ps"):
                k = k.split(".")[0]
            w = self.last_w.get(k)
            if w is not None:
                add(w, "waw")
            for rd in self.readers.get(k, ()):
                add(rd, "war")
            self.last_w[k] = o
            self.readers[k] = []
        if dma:
            if eng == "pool":
                slot = self.pool_dma_count % N_POOL_SEMS
                self.pool_dma_count += 1
            else:
                slot = N_POOL_SEMS + self.dma_count % (N_DMA_SEMS - N_POOL_SEMS)
                self.dma_count += 1
            prev = self.dma_last.get(slot)
            if prev is not None:
                deps[id(prev)] = prev
            self.dma_last[slot] = o
            o.sem = slot
        for d in deps.values():
            o.deps.append(d)
            d.has_dep = True
        self.ops[eng].append(o)
        return o

    def emit(self, nc, final_wait_eng="sp"):
        counts = {e: 0 for e in ENGS}
        dma_vals = [0] * N_DMA_SEMS
        for e in ENGS:
            for o in self.ops[e]:
                if o.dma:
                    dma_vals[o.sem] += 16
                    o.ticket = dma_vals[o.sem]
                elif o.has_dep:
                    counts[e] += 1
                    o.ticket = counts[e]
        with contextlib.ExitStack() as st:
            esem = {e: st.enter_context(nc.semaphore("s_" + e)) for e in ENGS if e != "sp"}
            dsem = [st.enter_context(nc.semaphore("d%d" % i)) for i in range(N_DMA_SEMS)]
            block = st.enter_context(nc.Block())

            def run(e, eng):
                waited = {}
                for o in self.ops[e]:
                    need = {}
                    for d in o.deps:
                        key = ("d", d.sem) if d.dma else ("e", d.eng)
                        if d.ticket > need.get(key, 0):
                            need[key] = d.ticket
                    for key, tk in need.items():
                        if waited.get(key, 0) >= tk:
                            continue
                        s = dsem[key[1]] if key[0] == "d" else esem[key[1]]
                        eng.wait_ge(s, tk)
                        waited[key] = tk
                    ins = o.fn(eng)
                    if o.dma:
                        ins.then_inc(dsem[o.sem], 16)
                    elif o.has_dep:
                        ins.then_inc(esem[e], 1)
                if e == final_wait_eng:
                    for i in range(N_DMA_SEMS):
                        if dma_vals[i] and waited.get(("d", i), 0) < dma_vals[i]:
                            eng.wait_ge(dsem[i], dma_vals[i])
                    for e2 in esem:
                        if counts[e2]:
                            eng.wait_ge(esem[e2], counts[e2])

            block.tensor(lambda eng: run("pe", eng))
            block.scalar(lambda eng: run("act", eng))
            block.vector(lambda eng: run("dve", eng))
            block.gpsimd(lambda eng: run("pool", eng))
            block.sync(lambda eng: run("sp", eng))


D = 1024
KD = 8
S = 2048
NTP = 16
INC = 5128
FF = 2816
NJ = 22
QA, KA, VA, QB, KB, VB, FB, GA, GB = 0, 512, 1024, 1536, 2048, 2560, 3072, 3080, 4104
EPS = 1e-6
NWB = 4

IN_SPECS = [
    ("xp", [2048, 1024]), ("xs", [128, 1024]),
    ("cka", [2, 512, 512]), ("cva", [2, 512, 512]), ("ckb", [2, 2048, 512]), ("cvb", [2, 2048, 512]),
    ("clf", [2, 2048, 8]), ("scv", [4, 5632]),
    ("g1", [1, 1024]), ("g2", [1, 1024]), ("g3", [1, 1024]), ("g4", [1, 1024]),
    ("w_in", [1024, INC]), ("b_f", [1, 8]), ("rel", [8, 257]),
    ("wpa", [512, 1024]), ("wpb", [512, 1024]), ("wout", [1024, 1024]),
    ("wup", [1024, 2 * FF]), ("convw", [3, 2 * FF]), ("convb", [1, 2 * FF]), ("wdn", [FF, 1024]),
]
OUT_SPECS = [
    ("yp", [2048, 1024]), ("ys", [128, 1024]),
    ("kap", [512, 512]), ("vap", [512, 512]), ("kbp", [2048, 512]), ("vbp", [2048, 512]),
    ("lfp", [2048, 8]), ("cvp", [2, 2 * FF]),
    ("kas", [128, 512]), ("vas", [128, 512]), ("kbs", [128, 512]), ("vbs", [128, 512]),
    ("lfs", [128, 8]), ("cvs", [4, 2 * FF]),
]


class _Stop(Exception):
    pass


def build(stop=None, ngroups=5):
    nc = bass.Bass("TRN2", target_bir_lowering=False)
    T = {}
    for n, sh in IN_SPECS:
        T[n] = nc.dram_tensor(n, sh, F32, kind="ExternalInput")
    for n, sh in OUT_SPECS:
        T[n] = nc.dram_tensor(n, sh, F32, kind="ExternalOutput")
    eext = nc.dram_tensor("eext", [8, 384], F32)
    stash = nc.dram_tensor("wstash", [40, 128, 4096], BF16)

    P = Prog()
    st = contextlib.ExitStack()

    def sb(name, shape, dt):
        return st.enter_context(nc.sbuf_tensor(name, shape, dt))

    def dap(t, off, ap):
        return bass.AP(tensor=t, offset=off, ap=ap)

    def mm(out, lhsT, rhs, start, stop, reads, writes, skip=False):
        if skip:
            P.op("pe", lambda e: e.matmul(out, lhsT=lhsT, rhs=rhs, start=start, stop=stop, skip_group_check=True), reads, writes)
        else:
            P.op("pe", lambda e: e.matmul(out, lhsT=lhsT, rhs=rhs, start=start, stop=stop), reads, writes)

    def tr(out, in_, ident, reads, writes):
        P.op("pe", lambda e: e.transpose(out, in_, ident), reads, writes)

    def act(out, in_, func, reads, writes, **kw):
        P.op("act", lambda e: e.activation(out=out, in_=in_, func=func, **kw), reads, writes)

    def tt(eng, out, in0, in1, op, reads, writes):
        P.op(eng, lambda e: e.tensor_tensor(out=out, in0=in0, in1=in1, op=op), reads, writes)

    def ts(eng, out, in0, s1, s2, op0, op1, reads, writes):
        if s2 is None:
            P.op(eng, lambda e: e.tensor_scalar(out=out, in0=in0, scalar1=s1, scalar2=None, op0=op0), reads, writes)
        else:
            P.op(eng, lambda e: e.tensor_scalar(out=out, in0=in0, scalar1=s1, scalar2=s2, op0=op0, op1=op1), reads, writes)

    def stt(eng, out, in0, scalar, in1, op0, op1, reads, writes):
        P.op(eng, lambda e: e.scalar_tensor_tensor(out=out, in0=in0, scalar=scalar, in1=in1, op0=op0, op1=op1), reads, writes)

    def cp(eng, out, in_, reads, writes):
        if eng == "act":
            act(out, in_, AF.Copy, reads, writes)
        else:
            P.op(eng, lambda e: e.tensor_copy(out=out, in_=in_), reads, writes)

    def memset(eng, ap, val, writes):
        P.op(eng, lambda e: e.memset(ap, val), (), writes)

    def asel(out, pattern, cmp, fill, base, cm, key):
        P.op("pool", lambda e: e.affine_select(out=out, in_=out, pattern=pattern, compare_op=cmp, fill=fill,
                                               base=base, channel_multiplier=cm), [key], [key])

    import os
    SKIP = os.environ.get("DBG_SKIP", "")

    def dma(eng, out, in_, reads, writes, slow=False):
        if "dmaout" in SKIP and not writes and not slow:
            return
        if slow:
            P.op(eng, lambda e: e.dma_start(out=out, in_=in_, allow_slow_non_contiguous=True), reads, writes, dma=True)
        else:
            P.op(eng, lambda e: e.dma_start(out=out, in_=in_), reads, writes, dma=True)

    def bc_last(ap, n):
        return bass.AP(tensor=ap.tensor, offset=ap.offset, ap=[list(x) for x in ap.ap] + [[0, n]])

    psb = [st.enter_context(nc.psum_tensor("ps%d" % i, [128, 512], F32)) for i in range(8)]

    def bk(b):
        return ["ps%d" % b]

    class Rot:
        def __init__(self, items):
            self.items = list(items)
            self.i = 0

        def next(self):
            x = self.items[self.i % len(self.items)]
            self.i += 1
            return x

    rotT = Rot([0, 1])
    rotM = Rot([2, 3, 4, 5, 6, 7])
    rotAll = Rot(range(8))

    def psbf(b):
        return psb[b][:, :].bitcast(BF16)

    kbT = sb("kbT", [128, 4, S], BF16)
    kaT = sb("kaT", [128, 4, 1024], BF16)
    vbA = sb("vbA", [128, 16, 8, 65], BF16)
    vaA = sb("vaA", [128, 8, 8, 65], BF16)
    cc = sb("cc", [128, 16, 16], F32)
    rc = sb("rc", [128, 17, 8], F32)
    identb = sb("identb", [128, 128], BF16)
    identf = sb("identf", [128, 128], F32)
    tri = sb("tri", [128, 128], F32)
    onesf = sb("onesf", [128, 128], F32)
    Jm = sb("Jm", [128, 128], F32)
    SU = sb("SU", [128, 128], F32)
    SUb = sb("SUb", [128, 128], F32)
    oseq = [sb("oseq%d" % s, [128, 128], F32) for s in range(2)]
    cmask = sb("cmask", [128, 128], BF16)
    cmask_s = sb("cmask_s", [128, 128], BF16)
    expB = sb("expB", [128, 8, 256], F32)
    expBs = sb("expBs", [128, 8, 128], F32)
    cbias = sb("cbias", [128, 8], F32)
    gbb = sb("gbb", [128, 1024], F32)
    g1T = sb("g1T", [128, 8], F32)
    g3T = sb("g3T", [128, 8], F32)
    bfb = sb("bfb", [128, 8], F32)
    zt = sb("zt", [128, 16], F32)
    onec = sb("onec", [128, 1], F32)
    epsc = sb("epsc", [128, 1], F32)
    cw = sb("cw", [128, 44, 4], F32)
    sprev = sb("sprev", [128, 44, 4], F32)
    carry = [sb("carry%d" % i, [128, 44, 4], F32) for i in range(2)]
    clfT = sb("clfT", [128, 2, 16, 8], F32)
    rs = sb("rs", [128, 2, 17, 8], F32)
    bC = sb("bC", [128, 2, 16, 8], F32)
    bN = sb("bN", [128, 8], F32)
    xres = sb("xres", [128, 4, 1024], F32)
    actT = sb("actT", [128, 8, 512], BF16)
    U = sb("U", [128, 24, 512], BF16)
    kTn = [sb("kTn%d" % i, [128, 4, 128], BF16) for i in range(2)]
    vAn = [sb("vAn%d" % i, [128, 8, 65], BF16) for i in range(2)]
    lfg = sb("lfg", [128, 4, 8], F32)
    sm = [sb("sm%d" % i, [128, 16], F32) for i in range(4)]
    wbuf = [sb("wb%d" % i, [128, 4096], BF16) for i in range(NWB)]
    rotW = Rot(range(NWB))
    xh = [sb("xh%d" % i, [128, 1024], BF16) for i in range(2)]
    rotXH = Rot(range(2))
    stg = [sb("stg%d" % i, [128, 512], F32) for i in range(2)]
    rotStg = Rot(range(2))
    ptl = [sb("pt%d" % i, [128, 128], BF16) for i in range(16)]
    rotPt = Rot(range(16))
    ptw = [sb("ptw%d" % i, [128, 384], BF16) for i in range(3)]
    rotPtw = Rot(range(3))
    pn32 = [sb("pn32_%d" % i, [128, 256], F32) for i in range(2)]
    rotPn = Rot(range(2))
    ptn = [sb("ptn%d" % i, [128, 256], BF16) for i in range(4)]
    rotPtn = Rot(range(4))
    ot = sb("ot", [128, 1024], BF16)
    ptq = [sb("ptq%d" % i, [128, 512], BF16) for i in range(5)]
    rotPq = Rot(range(5))
    facT = sb("facT", [128, 4, 8], F32)
    nb = [sb("nb%d" % i, [128, 16, 8], F32) for i in range(2)]
    rotNb = Rot(range(2))
    mt = [sb("mt%d" % i, [128, 512], BF16) for i in range(2)]
    rotMt = Rot(range(2))
    uext = [[sb("uext%d_%d" % (a, i), [128, 516], F32) for i in range(1)] for a in range(2)]
    yy = [[sb("yy%d_%d" % (a, i), [128, 512], F32) for i in range(2)] for a in range(2)]
    rotC = Rot(range(2))
    kcT = [sb("kcT%d" % i, [128, 4, 128], BF16) for i in range(2)]
    rotKc = Rot(range(2))
    hk = [sb("hk%d" % i, [128, 128], F32) for i in range(2)]

    def Ukeys(rows, tiles):
        return [("U", r, t) for r in rows for t in tiles]

    memset("pool", identb[:, :], 0.0, ["identb"])
    asel(identb[:, :], [[-1, 128]], ALU.not_equal, 1.0, 0, 1, "identb")
    memset("pool", identf[:, :], 0.0, ["identf"])
    asel(identf[:, :], [[-1, 128]], ALU.not_equal, 1.0, 0, 1, "identf")
    memset("pool", onesf[:, :], 1.0, ["onesf"])
    memset("pool", tri[:, :], 1.0, ["tri"])
    asel(tri[:, :], [[1, 128]], ALU.is_ge, 0.0, 0, -1, "tri")
    memset("pool", Jm[:, :], 0.0, ["Jm"])
    asel(Jm[:, :], [[1, 128]], ALU.not_equal, 1.0, -127, 1, "Jm")
    memset("pool", SU[:, :], 1.0, ["SU"])
    asel(SU[:, :], [[-1, 128]], ALU.is_gt, 0.0, 0, 1, "SU")
    memset("pool", SUb[:, :], 1.0, ["SUb"])
    asel(SUb[:, :], [[-1, 128]], ALU.is_gt, 0.0, 0, 1, "SUb")
    memset("pool", SUb[64:128, 0:64], 0.0, ["SUb"])
    for s in range(2):
        memset("pool", oseq[s][:, :], 0.0, ["oseq%d" % s])
        memset("pool", oseq[s][s * 64:(s + 1) * 64, :], 1.0, ["oseq%d" % s])
    memset("pool", cmask[:, :], 1.0, ["cmask"])
    asel(cmask[:, :], [[1, 128]], ALU.is_ge, 0.0, 0, -1, "cmask")
    memset("pool", cmask_s[:, :], 1.0, ["cmask_s"])
    asel(cmask_s[:, :], [[1, 128]], ALU.is_ge, 0.0, 0, -1, "cmask_s")
    memset("pool", cmask_s[0:64, 64:128], 0.0, ["cmask_s"])
    memset("pool", onec[:, :], 1.0, ["onec"])
    memset("pool", epsc[:, :], EPS, ["epsc"])
    memset("pool", vbA[:, :, :, :], 1.0, [("vbA", j) for j in range(16)])
    memset("pool", vaA[:, :, :, :], 1.0, [("vaA", j) for j in range(8)])
    for i in range(2):
        memset("pool", vAn[i][:, :, :], 1.0, ["vAn%d" % i])
        memset("pool", carry[i][:, :, :], 0.0, ["carry%d" % i])
    memset("pool", rc[:, 0, :], 0.0, [("rc", 0)])

    misc8 = sb("misc8", [8, 640], F32)
    dma("sp", bfb[:, :], dap(T["b_f"], 0, [[0, 128], [1, 8]]), [], ["bfb"])
    dma("sp", misc8[:, 0:257], T["rel"].ap(), [], ["m8rel"])
    dma("sp", misc8[:, 384:512], T["g1"].ap().rearrange("o (c p) -> (o c) p", p=128), [], ["m8g1"])
    dma("sp", misc8[:, 512:640], T["g3"].ap().rearrange("o (c p) -> (o c) p", p=128), [], ["m8g3"])
    m8c = misc8[:, 256:257]
    cp("dve", misc8[:, 257:384], bass.AP(tensor=m8c.tensor, offset=m8c.offset, ap=[list(m8c.ap[0]), [0, 127]]), ["m8rel"], ["m8ext"])
    for gi_, (gt_, col_) in enumerate(((g1T, 384), (g3T, 512))):
        b = rotM.next()
        mm(psb[b][:, 0:8], misc8[0:8, col_:col_ + 128], identf[0:8, 0:8], True, True, ["m8g1", "m8g3", "identf"], bk(b))
        cp("dve", gt_[:, :], psb[b][:, 0:8], bk(b), ["g1T" if gi_ == 0 else "g3T"])
    ts("dve", zt[0:8, 0:8], identf[0:8, 0:8], misc8[0:8, 256:257], None, ALU.mult, None, ["identf", "m8rel"], ["zt"])
    b = rotM.next()
    mm(psb[b][:, 0:8], onesf[0:8, :], zt[0:8, 0:8], True, True, ["onesf", "zt"], bk(b))
    cp("dve", cbias[:, :], psb[b][:, 0:8], bk(b), ["cbias"])
    dma("sp", clfT[:, 0, :, :], T["clf"].ap()[0].rearrange("(t p) h -> p t h", p=128), [], ["clfT"])
    dma("sp", clfT[:, 1, :, :], T["clf"].ap()[1].rearrange("(t p) h -> p t h", p=128), [], ["clfT1"])
    for s_ in range(2):
        ck = "clfT" if s_ == 0 else "clfT1"
        memset("dve", rs[:, s_, 16, :], 0.0, [("rs", s_, 16)])
        for j in range(15, -1, -1):
            tt("dve", rs[:, s_, j, :], rs[:, s_, j + 1, :], clfT[:, s_, j, :], ALU.add, [("rs", s_, j + 1), ck], [("rs", s_, j)])
        b2 = rotM.next()
        for j in range(16):
            mm(psb[b2][:, j * 8:(j + 1) * 8], SU[:, :], clfT[:, s_, j, :], True, False, ["SU", ck], bk(b2))
            mm(psb[b2][:, j * 8:(j + 1) * 8], onesf[:, :], rs[:, s_, j + 1, :], False, True, ["onesf", ("rs", s_, j + 1)], bk(b2))
        cp("dve", bC[:, s_, :, :], psb[b2][:, 0:128].rearrange("p (j h) -> p j h", h=8), bk(b2), [("bC", s_)])
    dma("sp", eext.ap(), misc8[:, 0:384], ["m8rel", "m8ext"], ["eext0", "eext1"])
    for h in range(8):
        for dlt in range(2):
            base = 129 if dlt == 0 else 1
            hb = (h * 2 + dlt) % 2
            dma("sp", hk[hb][:, :], dap(eext, h * 384 + base, [[1, 128], [1, 128]]), ["eext0", "eext1"], ["hk%d" % hb])
            b = rotM.next()
            mm(psb[b][:, 0:128], Jm[:, :], hk[hb][:, :], True, True, ["Jm", "hk%d" % hb], bk(b))
            act(expB[:, h, dlt * 128:(dlt + 1) * 128], psb[b][:, 0:128], AF.Exp, bk(b), ["expB"])
    cp("dve", expBs[:, :, :], expB[:, :, 128:256], ["expB"], ["expBs"])
    memset("pool", expB[64:128, :, 128:192], 0.0, ["expB"])
    memset("pool", expBs[64:128, :, 0:64], 0.0, ["expBs"])
    memset("pool", expBs[0:64, :, 64:128], 0.0, ["expBs"])
    Uf = U[:, :, :].bitcast(F32)
    Uflat = Uf.rearrange("p a b -> p (a b)")
    allU = Ukeys(range(24), range(4))
    dma("sp", Uflat[0:3, 0:5632], T["convw"].ap(), [], allU)
    dma("sp", Uflat[3:4, 0:5632], T["convb"].ap(), [], ["ustg1"])
    dma("sp", Uflat[32:36, 0:5632], T["scv"].ap(), [], ["ustg2"])
    bA = rotM.next()
    for blk in range(44):
        mm(psb[bA][:, blk * 4:blk * 4 + 4], Uflat[0:4, blk * 128:(blk + 1) * 128], identf[0:4, 0:4], True, True,
           allU + ["ustg1", "identf"], bk(bA))
    cp("dve", cw[:, :, :], psb[bA][:, 0:176].rearrange("p (a b) -> p a b", b=4), bk(bA), ["cw"])
    bA = rotM.next()
    for blk in range(44):
        mm(psb[bA][:, blk * 4:blk * 4 + 4], Uflat[32:36, blk * 128:(blk + 1) * 128], identf[32:36, 32:36], True, True,
           allU + ["ustg2", "identf"], bk(bA))
    cp("dve", sprev[:, :, :], psb[bA][:, 0:176].rearrange("p (a b) -> p a b", b=4), bk(bA), ["sprev"])

    w_in_v = T["w_in"].ap().rearrange("(kc p) n -> p kc n", p=128)
    wup_v = T["wup"].ap().rearrange("(kc p) n -> p kc n", p=128)
    wout_v = T["wout"].ap().rearrange("(kc p) n -> p kc n", p=128)
    wpa_v = T["wpa"].ap().rearrange("(kc p) n -> p kc n", p=128)
    wpb_v = T["wpb"].ap().rearrange("(kc p) n -> p kc n", p=128)
    wdn_v = T["wdn"].ap().rearrange("(j p) n -> p j n", p=128)

    def wview8(i):
        return wbuf[i][:, :].rearrange("p (a b) -> p a b", a=8)

    def wview4(i):
        return wbuf[i][:, :].rearrange("p (a b) -> p a b", a=4)

    def wkeys(i):
        return [("wb", i, 0), ("wb", i, 1)]

    stash_ids = {}
    cur_gi = [0]

    def stash_idx(bid):
        if bid not in stash_ids:
            stash_ids[bid] = len(stash_ids)
        return stash_ids[bid]

    def stash_store(i, bid):
        idx = stash_idx(bid)
        dma("sp", stash.ap()[idx], wbuf[i][:, :], wkeys(i), [("stash", idx)])

    def stash_load(i, bid):
        idx = stash_idx(bid)
        dma("pool", wbuf[i][:, :], stash.ap()[idx], [("stash", idx)], wkeys(i))

    def load_w8(src, ncols, bid):
        i = rotW.next()
        if cur_gi[0] == 0:
            dma("pool", wview8(i)[:, :, 0:ncols], src, [], wkeys(i))
            stash_store(i, bid)
        else:
            stash_load(i, bid)
        return i

    evac_flip = [0]

    def evac_eng():
        evac_flip[0] ^= 1
        return "act" if evac_flip[0] else "dve"

    def rstd_from_ss(smt, smk, c_ss, c_out, n):
        act(smt[:, c_out:c_out + 1], smt[:, c_ss:c_ss + 1], AF.Ln, [(smk, c_ss), "epsc"], [(smk, c_out)],
            scale=1.0 / n, bias=epsc[:, 0:1])
        act(smt[:, c_out:c_out + 1], smt[:, c_out:c_out + 1], AF.Exp, [(smk, c_out)], [(smk, c_out)], scale=-0.5)

    def norm_tiles(lis, gT, gkey, c0):
        for li in lis:
            memset("dve", sm[li][:, c0:c0 + 1], 0.0, [("sm%d" % li, c0)])
        for li in lis:
            act(ot[:, :], xres[:, li, :], AF.Square, [("xres", li)], ["ot", ("sm%d" % li, c0)], accum_out=sm[li][:, c0:c0 + 1])
        for li in lis:
            act(sm[li][:, c0 + 1:c0 + 2], sm[li][:, c0:c0 + 1], AF.Ln, [("sm%d" % li, c0), "epsc"], [("sm%d" % li, c0 + 1)],
                scale=1.0 / 1024.0, bias=epsc[:, 0:1])
        for li in lis:
            act(sm[li][:, c0 + 1:c0 + 2], sm[li][:, c0 + 1:c0 + 2], AF.Exp, [("sm%d" % li, c0 + 1)], [("sm%d" % li, c0 + 1)], scale=-0.5)
        for li in lis:
            xb = rotXH.next()
            ts("dve", xh[xb][:, :], xres[:, li, :], sm[li][:, c0 + 1:c0 + 2], None, ALU.mult, None,
               [("xres", li), ("sm%d" % li, c0 + 1)], ["xh%d" % xb])
            b = rotT.next()
            pv = psbf(b)
            for kc in range(8):
                tr(pv[:, kc * 128:(kc + 1) * 128], xh[xb][:, kc * 128:(kc + 1) * 128], identb[:, :],
                   ["xh%d" % xb, "identb"], bk(b))
            tt("dve", actT[:, :, li * 128:(li + 1) * 128], pv[:, 0:1024].rearrange("p (c t) -> p c t", c=8),
               bc_last(gT[:, :], 128), ALU.mult, bk(b) + [gkey], [("actT", li)])

    def transpose_rows_to_U(src, srckey, nchunk, row0, li):
        b = rotT.next()
        pv = psbf(b)
        for c in range(nchunk):
            tr(pv[:, c * 128:(c + 1) * 128], src[:, c * 128:(c + 1) * 128], identb[:, :], [srckey, "identb"], bk(b))
        cp(evac_eng(), U[:, row0:row0 + nchunk, li * 128:(li + 1) * 128],
           pv[:, 0:nchunk * 128].rearrange("p (c t) -> p c t", c=nchunk), bk(b),
           Ukeys(range(row0, row0 + nchunk), [li]))

    groups = [("p", [0, 1, 2, 3]), ("p", [4, 5, 6, 7]), ("p", [8, 9, 10, 11]), ("p", [12, 13, 14, 15]), ("s", [0])]

    def chk(stage, gi):
        if stop == (stage, gi):
            raise _Stop()

    try:
      chk("setup", 0)
      for gi, (kind, tiles) in enumerate(groups[:ngroups] if ngroups > 0 else groups[ngroups:]):
          cur_gi[0] = gi
          NT = len(tiles)
          TK = NT * 128
          isS = kind == "s"
          lts = list(range(NT))

          for li, at in enumerate(tiles):
              src = T["xs"].ap() if isS else T["xp"].ap()[at * 128:(at + 1) * 128, :]
              dma("sp", xres[:, li, :], src, [], [("xres", li)])
          norm_tiles(lts, g1T, "g1T", 0)

          actT_keys = [("actT", li) for li in lts]

          chk("S1", gi)
          def fm_block(wi, dest_fn):
              w8 = wview8(wi)
              for c in range(4):
                  b = rotM.next()
                  for kc in range(8):
                      mm(psb[b][:, 0:TK], w8[:, kc, c * 128:(c + 1) * 128], actT[:, kc, 0:TK], kc == 0, kc == 7,
                         wkeys(wi) + actT_keys, bk(b))
                  out, wk = dest_fn(c)
                  cp(evac_eng(), out, psb[b][:, 0:TK], bk(b), wk)

          def tm_block(wi, li, ncols=512):
              b = rotM.next()
              w8 = wview8(wi)
              for kc in range(8):
                  mm(psb[b][:, 0:ncols], actT[:, kc, li * 128:(li + 1) * 128], w8[:, kc, 0:ncols], kc == 0, kc == 7,
                     wkeys(wi) + [("actT", li)], bk(b))
              return b

          def out_rows(name_p, name_s, at):
              if isS:
                  return T[name_s].ap()
              return T[name_p].ap()[at * 128:(at + 1) * 128, :]

          slot0 = (tiles[0] % 8)
          wi = load_w8(w_in_v[:, :, KA:KA + 512], 512, "KA")
          if isS:
              fm_block(wi, lambda c: (kTn[0][:, c, :], ["kTn0"]))
          else:
              fm_block(wi, lambda c: (kaT[:, c, slot0 * 128:slot0 * 128 + TK], [("kaT", (slot0 + t)) for t in lts]))
          for li, at in enumerate(tiles):
              if isS or at >= 12:
                  b = tm_block(wi, li)
                  s = rotStg.next()
                  cp(evac_eng(), stg[s][:, :], psb[b][:, :], bk(b), ["stg%d" % s])
                  dst = T["kas"].ap() if isS else T["kap"].ap()[(at - 12) * 128:(at - 11) * 128, :]
                  dma("sp", dst, stg[s][:, :], ["stg%d" % s], [])
          chk("S2a", gi)
          wi = load_w8(w_in_v[:, :, VA:VA + 512], 512, "VA")
          for li, at in enumerate(tiles):
              b = tm_block(wi, li)
              if isS:
                  vdst, vk = vAn[0][:, :, 0:64], "vAn0"
              else:
                  vdst, vk = vaA[:, at % 8, :, 0:64], ("vaA", at % 8)
              cp("dve", vdst, psb[b][:, :].rearrange("p (h d) -> p h d", h=8), bk(b), [vk])
              if isS or at >= 12:
                  s = rotStg.next()
                  cp("act", stg[s][:, :], psb[b][:, :], bk(b), ["stg%d" % s])
                  dst = T["vas"].ap() if isS else T["vap"].ap()[(at - 12) * 128:(at - 11) * 128, :]
                  dma("sp", dst, stg[s][:, :], ["stg%d" % s], [])
          chk("S2b", gi)
          wi = load_w8(w_in_v[:, :, KB:KB + 512], 512, "KB")
          if isS:
              fm_block(wi, lambda c: (kTn[1][:, c, :], ["kTn1"]))
          else:
              t0 = tiles[0] * 128
              fm_block(wi, lambda c: (kbT[:, c, t0:t0 + TK], [("kbT", at) for at in tiles]))
          chk("S2b1", gi)
          for li, at in enumerate(tiles):
              b = tm_block(wi, li)
              s = rotStg.next()
              cp(evac_eng(), stg[s][:, :], psb[b][:, :], bk(b), ["stg%d" % s])
              dma("sp", out_rows("kbp", "kbs", at), stg[s][:, :], ["stg%d" % s], [])
          chk("S2b2", gi)
          wi = load_w8(w_in_v[:, :, VB:VB + 512], 512, "VB")
          for li, at in enumerate(tiles):
              b = tm_block(wi, li)
              if isS:
                  vdst, vk = vAn[1][:, :, 0:64], "vAn1"
              else:
                  vdst, vk = vbA[:, at, :, 0:64], ("vbA", at)
              cp("dve", vdst, psb[b][:, :].rearrange("p (h d) -> p h d", h=8), bk(b), [vk])
              s = rotStg.next()
              cp("act", stg[s][:, :], psb[b][:, :], bk(b), ["stg%d" % s])
              dma("sp", out_rows("vbp", "vbs", at), stg[s][:, :], ["stg%d" % s], [])
          chk("S2c", gi)
          wi = load_w8(w_in_v[:, :, QA:QA + 512], 512, "QA")
          chk("S2c0", gi)
          fm_block(wi, lambda c: (U[:, c, 0:TK], Ukeys([c], lts)))
          chk("S2c1", gi)
          wi = load_w8(w_in_v[:, :, QB:QB + 512], 512, "QB")
          fm_block(wi, lambda c: (U[:, 4 + c, 0:TK], Ukeys([4 + c], lts)))
          chk("S2d", gi)
          wi = load_w8(w_in_v[:, :, FB:FB + 8], 8, "FB")
          for li, at in enumerate(tiles):
              b = tm_block(wi, li, ncols=8)
              tt("dve", zt[:, 0:8], psb[b][:, 0:8], bfb[:, :], ALU.add, bk(b) + ["bfb"], ["zt"])
              act(zt[:, 0:8], zt[:, 0:8], AF.Exp, ["zt"], ["zt"], scale=-1.0)
              act(zt[:, 0:8], zt[:, 0:8], AF.Ln, ["zt", "onec"], ["zt"], bias=onec[:, 0:1])
              ts("dve", lfg[:, li, :], zt[:, 0:8], -1.0, None, ALU.mult, None, ["zt"], [("lfg", li)])
              if not isS:
                  b2 = rotM.next()
                  mm(psb[b2][:, 0:8], tri[:, :], lfg[:, li, :], True, at == 0, ["tri", ("lfg", li)], bk(b2))
                  if at > 0:
                      mm(psb[b2][:, 0:8], onesf[:, :], rc[:, at, :], False, True, ["onesf", ("rc", at)], bk(b2))
                  tt("dve", rc[:, at + 1, :], rc[:, at, :], lfg[:, li, :], ALU.add, [("rc", at), ("lfg", li)], [("rc", at + 1)])
                  mm(psb[b2][:, 8:16], onesf[:, :], rc[:, at + 1, :], True, True, ["onesf", ("rc", at + 1)], bk(b2))
                  cp("dve", cc[:, at, :], psb[b2][:, 0:16], bk(b2), [("cc", at)])
          lf_dst = T["lfs"].ap() if isS else T["lfp"].ap()[tiles[0] * 128:tiles[0] * 128 + TK, :].rearrange("(t p) h -> p t h", p=128)
          if isS:
              dma("sp", lf_dst, lfg[:, 0, :], [("lfg", 0)], [])
          else:
              dma("sp", lf_dst, lfg[:, 0:NT, :], [("lfg", li) for li in lts], [])

          chk("S2", gi)
          rotST = {0: Rot([2, 3]), 64: Rot([4, 5])}

          def oreg(bank, r):
              return psb[bank][:, r * 65:(r + 1) * 65], ["ps%d" % bank]

          def normalize(h, o, ok, col0, smt, smk):
              P.op("dve", lambda e: e.reciprocal(out=smt[:, 8 + (h % 8):9 + (h % 8)], in_=o[:, 64:65]), ok, [(smk, 8 + h % 8)])
              ts("dve", ot[:, col0 + h * 64:col0 + (h + 1) * 64], o[:, 0:64], smt[:, 8 + (h % 8):9 + (h % 8)], None,
                 ALU.mult, None, ok + [(smk, 8 + h % 8)], ["ot"])

          def run_pipeline(units, skew=2):
              nun = len(units)
              for t_ in range(nun + skew):
                  if t_ < nun:
                      units[t_][0]()
                  if t_ - skew >= 0:
                      units[t_ - skew][1]()

          if not isS:
              units = []
              for li, at in enumerate(tiles):
                  smt, smk = sm[li], "sm%d" % li
                  qc = slice(li * 128, (li + 1) * 128)
                  n = rotNb.next()
                  nbv = nb[n]

                  def mk_bias(at=at, n=n, nbv=nbv, g=gi):
                      csrc = cc[:, at, 8:16]
                      j0 = 4 * g
                      tt("dve", nbv[:, j0:at + 1, :],
                         bass.AP(tensor=csrc.tensor, offset=csrc.offset, ap=[list(csrc.ap[0]), [0, at + 1 - j0], list(csrc.ap[1])]),
                         cc[:, j0:at + 1, 0:8], ALU.subtract, [("cc", j) for j in range(at + 1)], ["nb%d" % n])
                      if g > 0:
                          esrc = cc[:, 3, 8:16]
                          tt("dve", nbv[:, 0:g, :],
                             bass.AP(tensor=csrc.tensor, offset=csrc.offset, ap=[list(csrc.ap[0]), [0, g], list(csrc.ap[1])]),
                             bass.AP(tensor=esrc.tensor, offset=esrc.offset, ap=[list(esrc.ap[0]), [64, g], list(esrc.ap[1])]),
                             ALU.subtract, [("cc", j) for j in range(at + 1)] + ["nb%d" % n], ["nb%d" % n])

                  firstB = True
                  for hp in range(4):
                      bo = 6 + hp % 2
                      chunks = [(c0, hh) for c0 in range(0, at + 1, 4) for hh in range(2)]
                      for ci, (c0, hh) in enumerate(chunks):
                          js = list(range(c0, min(c0 + 4, at + 1)))
                          st_ = {}

                          def front(js=js, hh=hh, hp=hp, li=li, at=at, qc=qc, n=n, nbv=nbv, st_=st_, need_bias=firstB, mk_bias=mk_bias, gi_=gi):
                              if need_bias:
                                  mk_bias()
                              h = 2 * hp + hh
                              rlo = hh * 64
                              b = rotST[rlo].next()
                              for x, j in enumerate(js):
                                  mm(psb[b][:, x * 128:(x + 1) * 128], kbT[rlo:rlo + 64, hp, j * 128:(j + 1) * 128],
                                     U[rlo:rlo + 64, 4 + hp, qc], True, True, [("kbT", j), ("U", 4 + hp, li)], ["ps%d" % b])
                              if js[0] < 4 * gi_:
                                  w_ = rotPq.next()
                                  act(ptq[w_][:, :], psb[b][:, :], AF.Exp, ["ps%d" % b, "nb%d" % n], ["ptq%d" % w_],
                                      scale=0.125, bias=nbv[:, js[0] // 4, h:h + 1])
                                  st_["wide"] = w_
                              else:
                                  pl = []
                                  for x, j in enumerate(js):
                                      p = rotPt.next()
                                      pl.append(p)
                                      act(ptl[p][:, :], psb[b][:, x * 128:(x + 1) * 128], AF.Exp, ["ps%d" % b, "nb%d" % n], ["pt%d" % p],
                                          scale=0.125, bias=nbv[:, j, h:h + 1])
                                      if j == at:
                                          tt("dve", ptl[p][:, :], ptl[p][:, :], cmask[:, :], ALU.mult, ["pt%d" % p, "cmask"], ["pt%d" % p])
                                  st_["pl"] = pl

                          def back(js=js, hh=hh, hp=hp, bo=bo, st_=st_, first=(ci == 0), last=(ci == len(chunks) - 1),
                                   smt=smt, smk=smk):
                              h = 2 * hp + hh
                              o, ok = oreg(bo, hh)
                              for x, j in enumerate(js):
                                  if "wide" in st_:
                                      w_ = st_["wide"]
                                      mm(o, ptq[w_][:, x * 128:(x + 1) * 128], vbA[:, j, h, :], first and x == 0, False,
                                         ["ptq%d" % w_, ("vbA", j)], ok, skip=True)
                                  else:
                                      mm(o, ptl[st_["pl"][x]][:, :], vbA[:, j, h, :], first and x == 0, False,
                                         ["pt%d" % st_["pl"][x], ("vbA", j)], ok, skip=True)
                              if last:
                                  for h2 in range(2):
                                      o2, ok2 = oreg(bo, h2)
                                      normalize(2 * hp + h2, o2, ok2, 512, smt, smk)

                          units.append((front, back))
                          firstB = False
                  far = [j for j in (at - 4, at - 3, at - 2) if j >= 0]
                  near = [j for j in (at - 1, at) if j >= 0]
                  for hp in range(4):
                      bo = 6 + hp % 2
                      sub = []
                      for hh in range(2):
                          if far:
                              sub.append((hh, "far"))
                          sub.append((hh, "near"))
                      for ci, (hh, kind_) in enumerate(sub):
                          st_ = {}

                          def front(hh=hh, hp=hp, li=li, at=at, qc=qc, kind_=kind_, st_=st_, far=far, near=near):
                              h = 2 * hp + hh
                              rlo = hh * 64
                              jl = far if kind_ == "far" else near
                              b = rotST[rlo].next()
                              for x, j in enumerate(jl):
                                  mm(psb[b][:, x * 128:(x + 1) * 128], kaT[rlo:rlo + 64, hp, (j % 8) * 128:(j % 8 + 1) * 128],
                                     U[rlo:rlo + 64, hp, qc], True, True, [("kaT", j % 8), ("U", hp, li)], ["ps%d" % b])
                              nl = len(jl)
                              if kind_ == "far":
                                  w = rotPtw.next()
                                  act(ptw[w][:, 0:nl * 128], psb[b][:, 0:nl * 128], AF.Exp, ["ps%d" % b, "cbias"], ["ptw%d" % w],
                                      scale=0.125, bias=cbias[:, h:h + 1])
                                  if at - 4 >= 0:
                                      memset("dve", ptw[w][0:64, 64:128], 0.0, ["ptw%d" % w])
                                  st_["src"] = (ptw[w], "ptw%d" % w)
                              else:
                                  q = rotPn.next()
                                  act(pn32[q][:, 0:nl * 128], psb[b][:, 0:nl * 128], AF.Exp, ["ps%d" % b], ["pn32_%d" % q], scale=0.125)
                                  r = rotPtn.next()
                                  tt("dve", ptn[r][:, 0:nl * 128], pn32[q][:, 0:nl * 128], expB[:, h, 256 - nl * 128:256], ALU.mult,
                                     ["pn32_%d" % q, "expB"], ["ptn%d" % r])
                                  st_["src"] = (ptn[r], "ptn%d" % r)

                          def back(hh=hh, hp=hp, bo=bo, kind_=kind_, st_=st_, far=far, near=near, first=(ci == 0),
                                   last=(ci == len(sub) - 1), smt=smt, smk=smk, li=li, lastpair=(hp == 3)):
                              h = 2 * hp + hh
                              jl = far if kind_ == "far" else near
                              o, ok = oreg(bo, hh)
                              srcT, srck = st_["src"]
                              for x, j in enumerate(jl):
                                  mm(o, srcT[:, x * 128:(x + 1) * 128], vaA[:, j % 8, h, :], first and x == 0, False,
                                     [srck, ("vaA", j % 8)], ok, skip=True)
                              if last:
                                  for h2 in range(2):
                                      o2, ok2 = oreg(bo, h2)
                                      normalize(2 * hp + h2, o2, ok2, 0, smt, smk)
                                  if lastpair:
                                      transpose_rows_to_U(ot, "ot", 8, 8, li)

                          units.append((front, back))
              run_pipeline(units, skew=3)
              if gi < 3:
                  j0 = 4 * gi
                  esrc = cc[:, j0 + 3, 8:16]
                  tt("dve", facT[:, :, :],
                     bass.AP(tensor=esrc.tensor, offset=esrc.offset, ap=[list(esrc.ap[0]), [0, 4], list(esrc.ap[1])]),
                     cc[:, j0:j0 + 4, 0:8], ALU.subtract, [("cc", j0 + t) for t in range(4)], ["facT"])
                  act(facT[:, :, :], facT[:, :, :], AF.Exp, ["facT"], ["facT"])
                  for t in range(4):
                      tt("dve", vbA[:, j0 + t, :, :], vbA[:, j0 + t, :, :], bc_last(facT[:, t, :], 65), ALU.mult,
                         [("vbA", j0 + t), "facT"], [("vbA", j0 + t)])
          else:
              smt, smk = sm[0], "sm0"
              b2 = rotM.next()
              for s in range(2):
                  mm(psb[b2][:, s * 8:(s + 1) * 8], oseq[s][:, :], lfg[:, 0, :], True, True, ["oseq%d" % s, ("lfg", 0)], bk(b2))
              for s in range(2):
                  tsrc = psb[b2][:, s * 8:(s + 1) * 8]
                  tt("dve", bC[:, s, :, :], bC[:, s, :, :],
                     bass.AP(tensor=tsrc.tensor, offset=tsrc.offset, ap=[list(tsrc.ap[0]), [0, 16], list(tsrc.ap[1])]),
                     ALU.add, bk(b2) + [("bC", s)], [("bC", s)])
              b2 = rotM.next()
              mm(psb[b2][:, 0:8], SUb[:, :], lfg[:, 0, :], True, True, ["SUb", ("lfg", 0)], bk(b2))
              cp("dve", bN[:, :], psb[b2][:, 0:8], bk(b2), ["bN"])

              def sreg(h):
                  return oreg(6 + h // 4, h % 4)

              def sample_branch(br):
                  ncache = 4 if br == 0 else 16
                  ksrc = T["cka"] if br == 0 else T["ckb"]
                  vsrc = T["cva"] if br == 0 else T["cvb"]
                  qrow0 = 0 if br == 0 else 4
                  col0 = 0 if br == 0 else 512
                  if br == 1:
                      kcv = kbT[:, :, :].rearrange("p a b -> p (a b)").rearrange("p (j c) -> p j c", j=16)
                      vv, vname = vbA, "vbA"
                      allk = [("kbT", t) for t in range(16)]
                  else:
                      kcv = kaT[:, :, :].rearrange("p a b -> p (a b)").rearrange("p (j c) -> p j c", j=8)
                      vv, vname = vaA, "vaA"
                      allk = [("kaT", t) for t in range(8)]

                  def tile_of(s, j):
                      return j if br == 1 else 4 * s + j

                  def load_chunk(s, cj):
                      jj0 = tile_of(s, 4 * cj)
                      r0 = 4 * cj * 128
                      dma("pool", kcv[:, jj0:jj0 + 4, :], ksrc.ap()[s, r0:r0 + 512, :].rearrange("(j p) c -> p j c", p=128),
                          [], allk + [("skc", br, jj0 + x) for x in range(4)])
                      for x in range(4):
                          dma("pool", vv[:, jj0 + x, :, 0:64],
                              vsrc.ap()[s, r0 + x * 128:r0 + (x + 1) * 128, :].rearrange("p (h d) -> p h d", h=8),
                              [], [(vname, jj0 + x)])

                  if br == 1:
                      for cj in range(4):
                          load_chunk(0, cj)
                  else:
                      load_chunk(0, 0)
                      load_chunk(1, 0)
                  for s in range(2):
                      units = []
                      started = {6: False, 7: False}
                      kbst = {}
                      for j in range(ncache + 1):
                          for half in range(2):
                              st_ = {}

                              def front(j=j, half=half, s=s, st_=st_, kbst=kbst):
                                  isnew = (j == ncache)
                                  if half == 0 and not isnew:
                                      kb_ = rotKc.next()
                                      kbst[j] = kb_
                                      jj = tile_of(s, j)
                                      b = rotT.next()
                                      pv = psbf(b)
                                      for c in range(4):
                                          tr(pv[:, c * 128:(c + 1) * 128], kcv[:, jj, c * 128:(c + 1) * 128], identb[:, :],
                                             [("skc", br, jj), "identb"], bk(b))
                                      cp(evac_eng(), kcT[kb_][:, :, :], pv[:, 0:512].rearrange("p (c t) -> p c t", c=4), bk(b),
                                         ["kcT%d" % kb_])
                                  hs = list(range(half * 4, half * 4 + 4))
                                  bl = {0: rotST[0].next(), 64: rotST[64].next()}
                                  info = []
                                  for h in hs:
                                      hp, rlo = h // 2, (h % 2) * 64
                                      x = (h - half * 4) // 2
                                      b = bl[rlo]
                                      if isnew:
                                          stv = psb[b][:, x * 128:(x + 1) * 128]
                                          mm(stv, kTn[br][rlo:rlo + 64, hp, :], U[rlo:rlo + 64, qrow0 + hp, 0:128], True, True,
                                             ["kTn%d" % br, ("U", qrow0 + hp, 0)], ["ps%d" % b])
                                      else:
                                          stv = psb[b][:, x * 64:(x + 1) * 64]
                                          kb_ = kbst[j]
                                          mm(stv, kcT[kb_][rlo:rlo + 64, hp, :], U[rlo:rlo + 64, qrow0 + hp, s * 64:(s + 1) * 64], True, True,
                                             ["kcT%d" % kb_, ("U", qrow0 + hp, 0)], ["ps%d" % b])
                                      info.append((h, b, stv))
                                  pl = []
                                  for h, b, stv in info:
                                      p = rotPt.next()
                                      pk = "pt%d" % p
                                      if isnew:
                                          if br == 1:
                                              act(ptl[p][:, :], stv, AF.Exp, ["ps%d" % b, "bN"], [pk], scale=0.125, bias=bN[:, h:h + 1])
                                              tt("dve", ptl[p][:, :], ptl[p][:, :], cmask_s[:, :], ALU.mult, [pk, "cmask_s"], [pk])
                                          else:
                                              q = rotPn.next()
                                              act(pn32[q][:, 0:128], stv, AF.Exp, ["ps%d" % b], ["pn32_%d" % q], scale=0.125)
                                              tt("dve", ptl[p][:, :], pn32[q][:, 0:128], expBs[:, h, :], ALU.mult,
                                                 ["pn32_%d" % q, "expBs"], [pk])
                                      elif br == 1:
                                          act(ptl[p][:, s * 64:(s + 1) * 64], stv, AF.Exp, ["ps%d" % b, ("bC", s)], [pk], scale=0.125,
                                              bias=bC[:, s, j, h:h + 1])
                                      elif j < 3:
                                          act(ptl[p][:, s * 64:(s + 1) * 64], stv, AF.Exp, ["ps%d" % b, "cbias"], [pk], scale=0.125,
                                              bias=cbias[:, h:h + 1])
                                      else:
                                          q = rotPn.next()
                                          act(pn32[q][:, 0:64], stv, AF.Exp, ["ps%d" % b], ["pn32_%d" % q], scale=0.125)
                                          tt("dve", ptl[p][:, s * 64:(s + 1) * 64], pn32[q][:, 0:64], expB[:, h, 0:64],
                                             ALU.mult, ["pn32_%d" % q, "expB"], [pk])
                                      pl.append((h, p))
                                  st_["pl"] = pl

                              def back(j=j, half=half, s=s, st_=st_, kbst=kbst, started=started):
                                  isnew = (j == ncache)
                                  for h, p in st_["pl"]:
                                      o, ok = sreg(h)
                                      bo = 6 + h // 4
                                      if isnew:
                                          mm(o, ptl[p][:, :], vAn[br][:, h, :], not started[bo], False, ["pt%d" % p, "vAn%d" % br], ok, skip=True)
                                      else:
                                          jj = tile_of(s, j)
                                          mm(o, ptl[p][:, :], vv[:, jj, h, :], not started[bo], False, ["pt%d" % p, (vname, jj)], ok, skip=True)
                                      started[bo] = True
                                  if br == 1 and s == 0 and half == 1 and (not isnew) and j % 4 == 3:
                                      load_chunk(1, j // 4)

                              units.append((front, back))
                      run_pipeline(units, skew=2)
                      rs_ = slice(s * 64, (s + 1) * 64)
                      for h in range(8):
                          o, ok = sreg(h)
                          P.op("dve", lambda e, o=o, h=h, rs_=rs_: e.reciprocal(out=smt[rs_, 8 + h:9 + h], in_=o[rs_, 64:65]),
                               ok, [(smk, 8 + h)])
                          ts("dve", ot[rs_, col0 + h * 64:col0 + (h + 1) * 64], o[rs_, 0:64], smt[rs_, 8 + h:9 + h], None,
                             ALU.mult, None, ok + [(smk, 8 + h)], ["ot"])

              sample_branch(1)
              sample_branch(0)
              transpose_rows_to_U(ot, "ot", 8, 8, 0)

          chk("S3", gi)
          s4units = []
          for cb in range(2):
              wst = {}
              for li in lts:
                  st_ = {}

                  def front(cb=cb, li=li, wst=wst, st_=st_, first=(li == 0)):
                      if first:
                          wab_ = rotW.next()
                          if cur_gi[0] == 0:
                              dma("pool", wview8(wab_)[:, 0:4, :], wpa_v[:, :, cb * 512:(cb + 1) * 512], [], [("wb", wab_, 0)])
                              dma("pool", wview8(wab_)[:, 4:8, :], wpb_v[:, :, cb * 512:(cb + 1) * 512], [], [("wb", wab_, 1)])
                              stash_store(wab_, ("wab", cb))
                          else:
                              stash_load(wab_, ("wab", cb))
                          wst["wab"] = wab_
                          wst["wga"] = load_w8(w_in_v[:, :, GA + cb * 512:GA + (cb + 1) * 512], 512, ("wga", cb))
                          wst["wgb"] = load_w8(w_in_v[:, :, GB + cb * 512:GB + (cb + 1) * 512], 512, ("wgb", cb))
                      wab, wga, wgb = wst["wab"], wst["wga"], wst["wgb"]
                      tsl = slice(li * 128, (li + 1) * 128)
                      bpa, bpb, bga, bgb = rotAll.next(), rotAll.next(), rotAll.next(), rotAll.next()
                      for c in range(4):
                          mm(psb[bpa][:, :], U[:, 8 + c, tsl], wview8(wab)[:, c, :], c == 0, c == 3,
                             [("U", 8 + c, li), ("wb", wab, 0)], bk(bpa))
                      for c in range(4):
                          mm(psb[bpb][:, :], U[:, 12 + c, tsl], wview8(wab)[:, 4 + c, :], c == 0, c == 3,
                             [("U", 12 + c, li), ("wb", wab, 1)], bk(bpb))
                      for kc in range(8):
                          mm(psb[bga][:, :], actT[:, kc, tsl], wview8(wga)[:, kc, :], kc == 0, kc == 7,
                             [("actT", li)] + wkeys(wga), bk(bga))
                      for kc in range(8):
                          mm(psb[bgb][:, :], actT[:, kc, tsl], wview8(wgb)[:, kc, :], kc == 0, kc == 7,
                             [("actT", li)] + wkeys(wgb), bk(bgb))
                      sA, sB = yy[0][0], yy[1][0]
                      act(sA[:, :], psb[bga][:, :], AF.Sigmoid, bk(bga), ["yy0_0"])
                      act(sB[:, :], psb[bgb][:, :], AF.Sigmoid, bk(bgb), ["yy1_0"])
                      tt("dve", sA[:, :], psb[bpa][:, :], sA[:, :], ALU.mult, bk(bpa) + ["yy0_0"], ["yy0_0"])
                      tt("dve", sB[:, :], psb[bpb][:, :], sB[:, :], ALU.mult, bk(bpb) + ["yy1_0"], ["yy1_0"])
                      m = rotMt.next()
                      tt("dve", mt[m][:, :], sA[:, :], sB[:, :], ALU.add, ["yy0_0", "yy1_0"], ["mt%d" % m])
                      st_["m"] = m

                  def back(cb=cb, li=li, st_=st_):
                      m = st_["m"]
                      transpose_rows_to_U(mt[m], "mt%d" % m, 4, 16 + cb * 4, li)

                  s4units.append((front, back))
          for t_ in range(len(s4units) + 1):
              if t_ < len(s4units):
                  s4units[t_][0]()
              if t_ >= 1:
                  s4units[t_ - 1][1]()

          chk("S4", gi)
          dma("sp", gbb[:, :], dap(T["g2"], 0, [[0, 128], [1, 1024]]), [], ["gbb"])
          wo = [load_w8(wout_v[:, :, hf * 512:(hf + 1) * 512], 512, ("wo", hf)) for hf in range(2)]
          bms = {}
          for li in lts:
              tsl = slice(li * 128, (li + 1) * 128)
              bms[li] = [2 * li, 2 * li + 1]
              for hf in range(2):
                  for kc in range(8):
                      mm(psb[bms[li][hf]][:, :], U[:, 16 + kc, tsl], wview8(wo[hf])[:, kc, :], kc == 0, kc == 7,
                         [("U", 16 + kc, li)] + wkeys(wo[hf]), bk(bms[li][hf]))
              smt, smk = sm[li], "sm%d" % li
              memset("dve", smt[:, 2:4], 0.0, [(smk, 2), (smk, 3)])
              for hf in range(2):
                  act(ot[:, hf * 512:(hf + 1) * 512], psb[bms[li][hf]][:, :], AF.Square, bk(bms[li][hf]), ["ot", (smk, 2 + hf)],
                      accum_out=smt[:, 2 + hf:3 + hf])
              tt("dve", smt[:, 2:3], smt[:, 2:3], smt[:, 3:4], ALU.add, [(smk, 2), (smk, 3)], [(smk, 2)])
          for li in lts:
              smt, smk = sm[li], "sm%d" % li
              act(smt[:, 3:4], smt[:, 2:3], AF.Ln, [(smk, 2), "epsc"], [(smk, 3)], scale=1.0 / 1024.0, bias=epsc[:, 0:1])
          for li in lts:
              smt, smk = sm[li], "sm%d" % li
              act(smt[:, 3:4], smt[:, 3:4], AF.Exp, [(smk, 3)], [(smk, 3)], scale=-0.5)
          for li in lts:
              smt, smk = sm[li], "sm%d" % li
              for hf in range(2):
                  stt("dve", yy[hf][1][:, :], psb[bms[li][hf]][:, :], smt[:, 3:4], gbb[:, hf * 512:(hf + 1) * 512],
                      ALU.mult, ALU.mult, bk(bms[li][hf]) + [(smk, 3), "gbb"], ["yy%d_1" % hf])
                  tt("dve", xres[:, li, hf * 512:(hf + 1) * 512], xres[:, li, hf * 512:(hf + 1) * 512], yy[hf][1][:, :], ALU.add,
                     [("xres", li), "yy%d_1" % hf], [("xres", li)])
              norm_tiles([li], g3T, "g3T", 4)

          chk("S5", gi)
          nseg = 2 if isS else 1
          seglen = 64 if isS else TK
          cr = carry[1] if isS else carry[0]
          crk = "carry1" if isS else "carry0"
          for blk in range(6):
              ncol = 512 if blk < 5 else 256
              wg = load_w8(wup_v[:, :, blk * 512:blk * 512 + ncol], ncol, ("upg", blk))
              wv = load_w8(wup_v[:, :, FF + blk * 512:FF + blk * 512 + ncol], ncol, ("upv", blk))
              for jj in range(ncol // 128):
                  j = blk * 4 + jj
                  ci = rotC.next()
                  ys = []
                  for a, wi_ in ((0, wg), (1, wv)):
                      fidx = j if a == 0 else 22 + j
                      b = rotAll.next()
                      for kc in range(8):
                          mm(psb[b][:, 0:TK], wview8(wi_)[:, kc, jj * 128:(jj + 1) * 128], actT[:, kc, 0:TK], kc == 0, kc == 7,
                             wkeys(wi_) + actT_keys, bk(b))
                      ue = uext[a][0]
                      uk = "uext%d_0" % a
                      y = yy[a][ci]
                      yk = "yy%d_%d" % (a, ci)
                      uev = ue[:, 0:nseg * (seglen + 2)].rearrange("p (s t) -> p s t", s=nseg)
                      psv = psb[b][:, 0:TK].rearrange("p (s t) -> p s t", s=nseg)
                      yv = y[:, 0:TK].rearrange("p (s t) -> p s t", s=nseg)
                      act(y[:, 0:TK], psb[b][:, 0:TK], AF.Identity, bk(b) + ["cw"], [yk], scale=cw[:, fidx, 2:3], bias=cw[:, fidx, 3:4])
                      cp("act", uev[:, :, 2:2 + seglen], psv, bk(b), [uk])
                      if isS:
                          cp("act", uev[:, :, 0:2], sprev[:, fidx, :].rearrange("p (s t) -> p s t", s=2), ["sprev"], [uk])
                      else:
                          cp("act", uev[:, :, 0:2], cr[:, fidx, 0:2].rearrange("p (s t) -> p s t", s=1), [crk], [uk])
                      ce = "dve"
                      stt(ce, yv, uev[:, :, 1:1 + seglen], cw[:, fidx, 1:2], yv, ALU.mult, ALU.add, [uk, "cw", yk], [yk])
                      stt(ce, yv, uev[:, :, 0:seglen], cw[:, fidx, 0:1], yv, ALU.mult, ALU.add, [uk, "cw", yk], [yk])
                      if isS:
                          cp("dve" if a == 0 else "act", cr[:, fidx, :].rearrange("p (s t) -> p s t", s=2), uev[:, :, seglen:seglen + 2], [uk], [crk])
                      else:
                          cp("dve" if a == 0 else "act", cr[:, fidx, 0:2].rearrange("p (s t) -> p s t", s=1), uev[:, :, seglen:seglen + 2], [uk], [crk])
                      ys.append((y, yk))
                  (yg, ygk), (yvv, yvk) = ys
                  act(yg[:, 0:TK], yg[:, 0:TK], AF.Gelu_apprx_tanh, [ygk], [ygk])
                  tt("dve", U[:, j, 0:TK], yg[:, 0:TK], yvv[:, 0:TK], ALU.mult, [ygk, yvk], Ukeys([j], lts))
          if isS or gi == 3:
              nr = 4 if isS else 2
              dstT = T["cvs"] if isS else T["cvp"]
              for q4 in range(11):
                  b = rotAll.next()
                  for x in range(4):
                      f = q4 * 4 + x
                      mm(psb[b][0:nr, x * 128:(x + 1) * 128], cr[:, f, 0:nr], identf[:, :], True, True, [crk, "identf"], bk(b))
                  cv = rotStg.next()
                  cp("dve", stg[cv][0:nr, :], psb[b][0:nr, :], bk(b), ["stg%d" % cv])
                  dma("sp", dstT.ap()[:, q4 * 512:(q4 + 1) * 512], stg[cv][0:nr, :], ["stg%d" % cv], [])

          chk("S6", gi)
          dma("sp", gbb[:, :], dap(T["g4"], 0, [[0, 128], [1, 1024]]), [], ["gbb"])
          for blk in range(6):
              nj = 4 if blk < 5 else 2
              wi_ = rotW.next()
              if cur_gi[0] == 0:
                  dma("pool", wview4(wi_)[:, 0:nj, :], wdn_v[:, blk * 4:blk * 4 + nj, :], [], wkeys(wi_))
                  stash_store(wi_, ("dn", blk))
              else:
                  stash_load(wi_, ("dn", blk))
              for jj in range(nj):
                  j = blk * 4 + jj
                  for li in lts:
                      for hf in range(2):
                          b = 2 * li + hf
                          mm(psb[b][:, :], U[:, j, li * 128:(li + 1) * 128], wview4(wi_)[:, jj, hf * 512:(hf + 1) * 512],
                             j == 0, j == 21, [("U", j, li)] + wkeys(wi_), bk(b))
          for li, at in enumerate(tiles):
              smt, smk = sm[li], "sm%d" % li
              memset("dve", smt[:, 6:8], 0.0, [(smk, 6), (smk, 7)])
              for hf in range(2):
                  b = 2 * li + hf
                  act(ot[:, hf * 512:(hf + 1) * 512], psb[b][:, :], AF.Square, bk(b), ["ot", (smk, 6 + hf)],
                      accum_out=smt[:, 6 + hf:7 + hf])
              tt("dve", smt[:, 6:7], smt[:, 6:7], smt[:, 7:8], ALU.add, [(smk, 6), (smk, 7)], [(smk, 6)])
          for li in lts:
              smt, smk = sm[li], "sm%d" % li
              act(smt[:, 7:8], smt[:, 6:7], AF.Ln, [(smk, 6), "epsc"], [(smk, 7)], scale=1.0 / 1024.0, bias=epsc[:, 0:1])
          for li in lts:
              smt, smk = sm[li], "sm%d" % li
              act(smt[:, 7:8], smt[:, 7:8], AF.Exp, [(smk, 7)], [(smk, 7)], scale=-0.5)
          for li, at in enumerate(tiles):
              smt, smk = sm[li], "sm%d" % li
              for hf in range(2):
                  b = 2 * li + hf
                  stt("dve", yy[hf][1][:, :], psb[b][:, :], smt[:, 7:8], gbb[:, hf * 512:(hf + 1) * 512],
                      ALU.mult, ALU.mult, bk(b) + [(smk, 7), "gbb"], ["yy%d_1" % hf])
                  tt("dve", xres[:, li, hf * 512:(hf + 1) * 512], xres[:, li, hf * 512:(hf + 1) * 512], yy[hf][1][:, :], ALU.add,
                     [("xres", li), "yy%d_1" % hf], [("xres", li)])
              dst = T["ys"].ap() if isS else T["yp"].ap()[at * 128:(at + 1) * 128, :]
              dma("sp", dst, xres[:, li, :], [("xres", li)], [])
    except _Stop:
        pass
    if os.environ.get('DBG_MEM'):
        print('SBUF remaining', nc.sbuf_bytes_remaining, 'base', nc.sbuf_base, 'top', nc.sbuf_top)
    P.emit(nc)
    st.close()
    return nc


_NC = None


def kernel(x_prompt, x_sample, cache_k_a, cache_v_a, cache_k_b, cache_v_b, cache_logf_b, state_conv_ffn,
           g_pre_mix, g_post_mix, g_pre_ffn, g_post_ffn, w_in, b_f, rel_table, w_proj_a, w_proj_b, w_out,
           w_up, conv_w, conv_b, w_down):
    global _NC
    f = lambda a: np.ascontiguousarray(np.asarray(a, dtype=np.float32))
    if _NC is None:
        _NC = build()
    nc = _NC
    shared = {
        "g1": f(g_pre_mix), "g2": f(g_post_mix), "g3": f(g_pre_ffn), "g4": f(g_post_ffn),
        "w_in": f(w_in)[0], "b_f": f(b_f), "rel": f(rel_table)[0],
        "wpa": f(w_proj_a)[0], "wpb": f(w_proj_b)[0], "wout": f(w_out)[0], "wup": f(w_up)[0],
        "convw": f(conv_w)[0], "convb": f(conv_b), "wdn": f(w_down)[0],
    }
    xp, xs = f(x_prompt), f(x_sample)
    cka, cva, ckb, cvb = f(cache_k_a)[0], f(cache_v_a)[0], f(cache_k_b)[0], f(cache_v_b)[0]
    clf, scv = f(cache_logf_b)[0], f(state_conv_ffn)[0]
    in_maps = []
    for c in range(8):
        m = dict(shared)
        m["xp"] = xp[c]
        m["xs"] = xs[2 * c:2 * c + 2].reshape(128, 1024)
        m["cka"] = cka[2 * c:2 * c + 2].reshape(2, 512, 512)
        m["cva"] = cva[2 * c:2 * c + 2].reshape(2, 512, 512)
        m["ckb"] = ckb[2 * c:2 * c + 2].reshape(2, 2048, 512)
        m["cvb"] = cvb[2 * c:2 * c + 2].reshape(2, 2048, 512)
        m["clf"] = clf[2 * c:2 * c + 2]
        m["scv"] = scv[2 * c:2 * c + 2].reshape(4, 5632)
        in_maps.append(m)
    res = run_bass_kernel_spmd(nc, in_maps, core_ids=list(range(8)))
    R = res.results
    cat = lambda n: np.stack([np.asarray(R[c][n], dtype=np.float32) for c in range(8)])
    y_p = cat("yp")
    y_s = cat("ys").reshape(16, 64, 1024)
    ka_p = cat("kap").reshape(1, 8, 512, 8, 64)
    va_p = cat("vap").reshape(1, 8, 512, 8, 64)
    kb_p = cat("kbp").reshape(1, 8, 2048, 8, 64)
    vb_p = cat("vbp").reshape(1, 8, 2048, 8, 64)
    lf_p = cat("lfp").reshape(1, 8, 2048, 8)
    cv_p = cat("cvp").reshape(1, 8, 2, 5632)
    ka_s = cat("kas").reshape(1, 16, 64, 8, 64)
    va_s = cat("vas").reshape(1, 16, 64, 8, 64)
    kb_s = cat("kbs").reshape(1, 16, 64, 8, 64)
    vb_s = cat("vbs").reshape(1, 16, 64, 8, 64)
    lf_s = cat("lfs").reshape(1, 16, 64, 8)
    cv_s = cat("cvs").reshape(1, 16, 2, 5632)
    return (y_p, y_s, ka_p, va_p, kb_p, vb_p, lf_p, cv_p, ka_s, va_s, kb_s, vb_s, lf_s, cv_s)
```

```python
import contextlib
import numpy as np
import concourse.bass as bass
import concourse.mybir as mybir
from concourse.bass_utils import run_bass_kernel_spmd

F32 = mybir.dt.float32
BF16 = mybir.dt.bfloat16
AF = mybir.ActivationFunctionType
ALU = mybir.AluOpType

ENGS = ("pe", "act", "dve", "pool", "sp")
N_DMA_SEMS = 40
N_POOL_SEMS = 8


class Op:
    __slots__ = ("eng", "fn", "dma", "deps", "has_dep", "ticket", "sem")

    def __init__(self, eng, fn, dma):
        self.eng = eng
        self.fn = fn
        self.dma = dma
        self.deps = []
        self.has_dep = False
        self.ticket = None
        self.sem = None


class Prog:
    def __init__(self):
        self.ops = {e: [] for e in ENGS}
        self.last_w = {}
        self.readers = {}
        self.dma_count = 0
        self.pool_dma_count = 0
        self.dma_last = {}

    def op(self, eng, fn, reads=(), writes=(), dma=False):
        o = Op(eng, fn, dma)
        deps = {}

        def add(d, kind):
            if d is o:
                return
            if d.eng == eng and not d.dma and not dma:
                if eng == "pe":
                    return
            deps[id(d)] = d

        for k in reads:
            if isinstance(k, str) and k.startswith("ps"):
                k = k.split(".")[0]
                w = self.last_w.get(k)
                if w is not None:
                    add(w, "raw")
                for rd in self.readers.get(k, ()):
                    if rd.eng != eng:
                        add(rd, "war")
                self.readers.setdefault(k, []).append(o)
            else:
                w = self.last_w.get(k)
                if w is not None:
                    add(w, "raw")
                self.readers.setdefault(k, []).append(o)
        for k in writes:
            if isinstance(k, str) and k.startswith("ps"):
                k = k.split(".")[0]
            w = self.last_w.get(k)
            if w is not None:
                add(w, "waw")
            for rd in self.readers.get(k, ()):
                add(rd, "war")
            self.last_w[k] = o
            self.readers[k] = []
        if dma:
            if eng == "pool":
                slot = self.pool_dma_count % N_POOL_SEMS
                self.pool_dma_count += 1
            else:
                slot = N_POOL_SEMS + self.dma_count % (N_DMA_SEMS - N_POOL_SEMS)
                self.dma_count += 1
            prev = self.dma_last.get(slot)
            if prev is not None:
                deps[id(prev)] = prev
            self.dma_last[slot] = o
            o.sem = slot
        for d in deps.values():
            o.deps.append(d)
            d.has_dep = True
        self.ops[eng].append(o)
        return o

    def emit(self, nc, final_wait_eng="sp"):
        counts = {e: 0 for e in ENGS}
        dma_vals = [0] * N_DMA_SEMS
        for e in ENGS:
            for o in self.ops[e]:
                if o.dma:
                    dma_vals[o.sem] += 16
                    o.ticket = dma_vals[o.sem]
                elif o.has_dep:
                    counts[e] += 1
                    o.ticket = counts[e]
        with contextlib.ExitStack() as st:
            esem = {e: st.enter_context(nc.semaphore("s_" + e)) for e in ENGS if e != "sp"}
            dsem = [st.enter_context(nc.semaphore("d%d" % i)) for i in range(N_DMA_SEMS)]
            block = st.enter_context(nc.Block())

            def run(e, eng):
                waited = {}
                for o in self.ops[e]:
                    need = {}
                    for d in o.deps:
                        key = ("d", d.sem) if d.dma else ("e", d.eng)
                        if d.ticket > need.get(key, 0):
                            need[key] = d.ticket
                    for key, tk in need.items():
                        if waited.get(key, 0) >= tk:
                            continue
                        s = dsem[key[1]] if key[0] == "d" else esem[key[1]]
                        eng.wait_ge(s, tk)
                        waited[key] = tk
                    ins = o.fn(eng)
                    if o.dma:
                        ins.then_inc(dsem[o.sem], 16)
                    elif o.has_dep:
                        ins.then_inc(esem[e], 1)
                if e == final_wait_eng:
                    for i in range(N_DMA_SEMS):
                        if dma_vals[i] and waited.get(("d", i), 0) < dma_vals[i]:
                            eng.wait_ge(dsem[i], dma_vals[i])
                    for e2 in esem:
                        if counts[e2]:
                            eng.wait_ge(esem[e2], counts[e2])

            block.tensor(lambda eng: run("pe", eng))
            block.scalar(lambda eng: run("act", eng))
            block.vector(lambda eng: run("dve", eng))
            block.gpsimd(lambda eng: run("pool", eng))
            block.sync(lambda eng: run("sp", eng))


D = 1024
KD = 8
S = 2048
NTP = 16
INC = 5128
FF = 2816
NJ = 22
QA, KA, VA, QB, KB, VB, FB, GA, GB = 0, 512, 1024, 1536, 2048, 2560, 3072, 3080, 4104
EPS = 1e-6
NWB = 4

IN_SPECS = [
    ("xp", [2048, 1024]), ("xs", [128, 1024]),
    ("cka", [2, 512, 512]), ("cva", [2, 512, 512]), ("ckb", [2, 2048, 512]), ("cvb", [2, 2048, 512]),
    ("clf", [2, 2048, 8]), ("scv", [4, 5632]),
    ("g1", [1, 1024]), ("g2", [1, 1024]), ("g3", [1, 1024]), ("g4", [1, 1024]),
    ("w_in", [1024, INC]), ("b_f", [1, 8]), ("rel", [8, 257]),
    ("wpa", [512, 1024]), ("wpb", [512, 1024]), ("wout", [1024, 1024]),
    ("wup", [1024, 2 * FF]), ("convw", [3, 2 * FF]), ("convb", [1, 2 * FF]), ("wdn", [FF, 1024]),
]
OUT_SPECS = [
    ("yp", [2048, 1024]), ("ys", [128, 1024]),
    ("kap", [512, 512]), ("vap", [512, 512]), ("kbp", [2048, 512]), ("vbp", [2048, 512]),
    ("lfp", [2048, 8]), ("cvp", [2, 2 * FF]),
    ("kas", [128, 512]), ("vas", [128, 512]), ("kbs", [128, 512]), ("vbs", [128, 512]),
    ("lfs", [128, 8]), ("cvs", [4, 2 * FF]),
]


class _Stop(Exception):
    pass


def build(stop=None, ngroups=5):
    nc = bass.Bass("TRN2", target_bir_lowering=False)
    T = {}
    for n, sh in IN_SPECS:
        T[n] = nc.dram_tensor(n, sh, F32, kind="ExternalInput")
    for n, sh in OUT_SPECS:
        T[n] = nc.dram_tensor(n, sh, F32, kind="ExternalOutput")
    eext = nc.dram_tensor("eext", [8, 384], F32)
    stash = nc.dram_tensor("wstash", [40, 128, 4096], BF16)

    P = Prog()
    st = contextlib.ExitStack()

    def sb(name, shape, dt):
        return st.enter_context(nc.sbuf_tensor(name, shape, dt))

    def dap(t, off, ap):
        return bass.AP(tensor=t, offset=off, ap=ap)

    def mm(out, lhsT, rhs, start, stop, reads, writes, skip=False):
        if skip:
            P.op("pe", lambda e: e.matmul(out, lhsT=lhsT, rhs=rhs, start=start, stop=stop, skip_group_check=True), reads, writes)
        else:
            P.op("pe", lambda e: e.matmul(out, lhsT=lhsT, rhs=rhs, start=start, stop=stop), reads, writes)

    def tr(out, in_, ident, reads, writes):
        P.op("pe", lambda e: e.transpose(out, in_, ident), reads, writes)

    def act(out, in_, func, reads, writes, **kw):
        P.op("act", lambda e: e.activation(out=out, in_=in_, func=func, **kw), reads, writes)

    def tt(eng, out, in0, in1, op, reads, writes):
        P.op(eng, lambda e: e.tensor_tensor(out=out, in0=in0, in1=in1, op=op), reads, writes)

    def ts(eng, out, in0, s1, s2, op0, op1, reads, writes):
        if s2 is None:
            P.op(eng, lambda e: e.tensor_scalar(out=out, in0=in0, scalar1=s1, scalar2=None, op0=op0), reads, writes)
        else:
            P.op(eng, lambda e: e.tensor_scalar(out=out, in0=in0, scalar1=s1, scalar2=s2, op0=op0, op1=op1), reads, writes)

    def stt(eng, out, in0, scalar, in1, op0, op1, reads, writes):
        P.op(eng, lambda e: e.scalar_tensor_tensor(out=out, in0=in0, scalar=scalar, in1=in1, op0=op0, op1=op1), reads, writes)

    def cp(eng, out, in_, reads, writes):
        if eng == "act":
            act(out, in_, AF.Copy, reads, writes)
        else:
            P.op(eng, lambda e: e.tensor_copy(out=out, in_=in_), reads, writes)

    def memset(eng, ap, val, writes):
        P.op(eng, lambda e: e.memset(ap, val), (), writes)

    def asel(out, pattern, cmp, fill, base, cm, key):
        P.op("pool", lambda e: e.affine_select(out=out, in_=out, pattern=pattern, compare_op=cmp, fill=fill,
                                               base=base, channel_multiplier=cm), [key], [key])

    import os
    SKIP = os.environ.get("DBG_SKIP", "")

    def dma(eng, out, in_, reads, writes, slow=False):
        if "dmaout" in SKIP and not writes and not slow:
            return
        if slow:
            P.op(eng, lambda e: e.dma_start(out=out, in_=in_, allow_slow_non_contiguous=True), reads, writes, dma=True)
        else:
            P.op(eng, lambda e: e.dma_start(out=out, in_=in_), reads, writes, dma=True)

    def bc_last(ap, n):
        return bass.AP(tensor=ap.tensor, offset=ap.offset, ap=[list(x) for x in ap.ap] + [[0, n]])

    psb = [st.enter_context(nc.psum_tensor("ps%d" % i, [128, 512], F32)) for i in range(8)]

    def bk(b):
        return ["ps%d" % b]

    class Rot:
        def __init__(self, items):
            self.items = list(items)
            self.i = 0

        def next(self):
            x = self.items[self.i % len(self.items)]
            self.i += 1
            return x

    rotT = Rot([0, 1])
    rotM = Rot([2, 3, 4, 5, 6, 7])
    rotAll = Rot(range(8))

    def psbf(b):
        return psb[b][:, :].bitcast(BF16)

    kbT = sb("kbT", [128, 4, S], BF16)
    kaT = sb("kaT", [128, 4, 1024], BF16)
    vbA = sb("vbA", [128, 16, 8, 65], BF16)
    vaA = sb("vaA", [128, 8, 8, 65], BF16)
    cc = sb("cc", [128, 16, 16], F32)
    rc = sb("rc", [128, 17, 8], F32)
    identb = sb("identb", [128, 128], BF16)
    identf = sb("identf", [128, 128], F32)
    tri = sb("tri", [128, 128], F32)
    onesf = sb("onesf", [128, 128], F32)
    Jm = sb("Jm", [128, 128], F32)
    SU = sb("SU", [128, 128], F32)
    SUb = sb("SUb", [128, 128], F32)
    oseq = [sb("oseq%d" % s, [128, 128], F32) for s in range(2)]
    cmask = sb("cmask", [128, 128], BF16)
    cmask_s = sb("cmask_s", [128, 128], BF16)
    expB = sb("expB", [128, 8, 256], F32)
    expBs = sb("expBs", [128, 8, 128], F32)
    cbias = sb("cbias", [128, 8], F32)
    gbb = sb("gbb", [128, 1024], F32)
    g1T = sb("g1T", [128, 8], F32)
    g3T = sb("g3T", [128, 8], F32)
    bfb = sb("bfb", [128, 8], F32)
    zt = sb("zt", [128, 16], F32)
    onec = sb("onec", [128, 1], F32)
    epsc = sb("epsc", [128, 1], F32)
    cw = sb("cw", [128, 44, 4], F32)
    sprev = sb("sprev", [128, 44, 4], F32)
    carry = [sb("carry%d" % i, [128, 44, 4], F32) for i in range(2)]
    clfT = sb("clfT", [128, 2, 16, 8], F32)
    rs = sb("rs", [128, 2, 17, 8], F32)
    bC = sb("bC", [128, 2, 16, 8], F32)
    bN = sb("bN", [128, 8], F32)
    xres = sb("xres", [128, 4, 1024], F32)
    actT = sb("actT", [128, 8, 512], BF16)
    U = sb("U", [128, 24, 512], BF16)
    kTn = [sb("kTn%d" % i, [128, 4, 128], BF16) for i in range(2)]
    vAn = [sb("vAn%d" % i, [128, 8, 65], BF16) for i in range(2)]
    lfg = sb("lfg", [128, 4, 8], F32)
    sm = [sb("sm%d" % i, [128, 16], F32) for i in range(4)]
    wbuf = [sb("wb%d" % i, [128, 4096], BF16) for i in range(NWB)]
    rotW = Rot(range(NWB))
    xh = [sb("xh%d" % i, [128, 1024], BF16) for i in range(2)]
    rotXH = Rot(range(2))
    stg = [sb("stg%d" % i, [128, 512], F32) for i in range(2)]
    rotStg = Rot(range(2))
    ptl = [sb("pt%d" % i, [128, 128], BF16) for i in range(16)]
    rotPt = Rot(range(16))
    ptw = [sb("ptw%d" % i, [128, 384], BF16) for i in range(3)]
    rotPtw = Rot(range(3))
    pn32 = [sb("pn32_%d" % i, [128, 256], F32) for i in range(2)]
    rotPn = Rot(range(2))
    ptn = [sb("ptn%d" % i, [128, 256], BF16) for i in range(4)]
    rotPtn = Rot(range(4))
    ot = sb("ot", [128, 1024], BF16)
    ptq = [sb("ptq%d" % i, [128, 512], BF16) for i in range(5)]
    rotPq = Rot(range(5))
    facT = sb("facT", [128, 4, 8], F32)
    nb = [sb("nb%d" % i, [128, 16, 8], F32) for i in range(2)]
    rotNb = Rot(range(2))
    mt = [sb("mt%d" % i, [128, 512], BF16) for i in range(2)]
    rotMt = Rot(range(2))
    uext = [[sb("uext%d_%d" % (a, i), [128, 516], F32) for i in range(1)] for a in range(2)]
    yy = [[sb("yy%d_%d" % (a, i), [128, 512], F32) for i in range(2)] for a in range(2)]
    rotC = Rot(range(2))
    kcT = [sb("kcT%d" % i, [128, 4, 128], BF16) for i in range(2)]
    rotKc = Rot(range(2))
    hk = [sb("hk%d" % i, [128, 128], F32) for i in range(2)]

    def Ukeys(rows, tiles):
        return [("U", r, t) for r in rows for t in tiles]

    memset("pool", identb[:, :], 0.0, ["identb"])
    asel(identb[:, :], [[-1, 128]], ALU.not_equal, 1.0, 0, 1, "identb")
    memset("pool", identf[:, :], 0.0, ["identf"])
    asel(identf[:, :], [[-1, 128]], ALU.not_equal, 1.0, 0, 1, "identf")
    memset("pool", onesf[:, :], 1.0, ["onesf"])
    memset("pool", tri[:, :], 1.0, ["tri"])
    asel(tri[:, :], [[1, 128]], ALU.is_ge, 0.0, 0, -1, "tri")
    memset("pool", Jm[:, :], 0.0, ["Jm"])
    asel(Jm[:, :], [[1, 128]], ALU.not_equal, 1.0, -127, 1, "Jm")
    memset("pool", SU[:, :], 1.0, ["SU"])
    asel(SU[:, :], [[-1, 128]], ALU.is_gt, 0.0, 0, 1, "SU")
    memset("pool", SUb[:, :], 1.0, ["SUb"])
    asel(SUb[:, :], [[-1, 128]], ALU.is_gt, 0.0, 0, 1, "SUb")
    memset("pool", SUb[64:128, 0:64], 0.0, ["SUb"])
    for s in range(2):
        memset("pool", oseq[s][:, :], 0.0, ["oseq%d" % s])
        memset("pool", oseq[s][s * 64:(s + 1) * 64, :], 1.0, ["oseq%d" % s])
    memset("pool", cmask[:, :], 1.0, ["cmask"])
    asel(cmask[:, :], [[1, 128]], ALU.is_ge, 0.0, 0, -1, "cmask")
    memset("pool", cmask_s[:, :], 1.0, ["cmask_s"])
    asel(cmask_s[:, :], [[1, 128]], ALU.is_ge, 0.0, 0, -1, "cmask_s")
    memset("pool", cmask_s[0:64, 64:128], 0.0, ["cmask_s"])
    memset("pool", onec[:, :], 1.0, ["onec"])
    memset("pool", epsc[:, :], EPS, ["epsc"])
    memset("pool", vbA[:, :, :, :], 1.0, [("vbA", j) for j in range(16)])
    memset("pool", vaA[:, :, :, :], 1.0, [("vaA", j) for j in range(8)])
    for i in range(2):
        memset("pool", vAn[i][:, :, :], 1.0, ["vAn%d" % i])
        memset("pool", carry[i][:, :, :], 0.0, ["carry%d" % i])
    memset("pool", rc[:, 0, :], 0.0, [("rc", 0)])

    misc8 = sb("misc8", [8, 640], F32)
    dma("sp", bfb[:, :], dap(T["b_f"], 0, [[0, 128], [1, 8]]), [], ["bfb"])
    dma("sp", misc8[:, 0:257], T["rel"].ap(), [], ["m8rel"])
    dma("sp", misc8[:, 384:512], T["g1"].ap().rearrange("o (c p) -> (o c) p", p=128), [], ["m8g1"])
    dma("sp", misc8[:, 512:640], T["g3"].ap().rearrange("o (c p) -> (o c) p", p=128), [], ["m8g3"])
    m8c = misc8[:, 256:257]
    cp("dve", misc8[:, 257:384], bass.AP(tensor=m8c.tensor, offset=m8c.offset, ap=[list(m8c.ap[0]), [0, 127]]), ["m8rel"], ["m8ext"])
    for gi_, (gt_, col_) in enumerate(((g1T, 384), (g3T, 512))):
        b = rotM.next()
        mm(psb[b][:, 0:8], misc8[0:8, col_:col_ + 128], identf[0:8, 0:8], True, True, ["m8g1", "m8g3", "identf"], bk(b))
        cp("dve", gt_[:, :], psb[b][:, 0:8], bk(b), ["g1T" if gi_ == 0 else "g3T"])
    ts("dve", zt[0:8, 0:8], identf[0:8, 0:8], misc8[0:8, 256:257], None, ALU.mult, None, ["identf", "m8rel"], ["zt"])
    b = rotM.next()
    mm(psb[b][:, 0:8], onesf[0:8, :], zt[0:8, 0:8], True, True, ["onesf", "zt"], bk(b))
    cp("dve", cbias[:, :], psb[b][:, 0:8], bk(b), ["cbias"])
    dma("sp", clfT[:, 0, :, :], T["clf"].ap()[0].rearrange("(t p) h -> p t h", p=128), [], ["clfT"])
    dma("sp", clfT[:, 1, :, :], T["clf"].ap()[1].rearrange("(t p) h -> p t h", p=128), [], ["clfT1"])
    for s_ in range(2):
        ck = "clfT" if s_ == 0 else "clfT1"
        memset("dve", rs[:, s_, 16, :], 0.0, [("rs", s_, 16)])
        for j in range(15, -1, -1):
            tt("dve", rs[:, s_, j, :], rs[:, s_, j + 1, :], clfT[:, s_, j, :], ALU.add, [("rs", s_, j + 1), ck], [("rs", s_, j)])
        b2 = rotM.next()
        for j in range(16):
            mm(psb[b2][:, j * 8:(j + 1) * 8], SU[:, :], clfT[:, s_, j, :], True, False, ["SU", ck], bk(b2))
            mm(psb[b2][:, j * 8:(j + 1) * 8], onesf[:, :], rs[:, s_, j + 1, :], False, True, ["onesf", ("rs", s_, j + 1)], bk(b2))
        cp("dve", bC[:, s_, :, :], psb[b2][:, 0:128].rearrange("p (j h) -> p j h", h=8), bk(b2), [("bC", s_)])
    dma("sp", eext.ap(), misc8[:, 0:384], ["m8rel", "m8ext"], ["eext0", "eext1"])
    for h in range(8):
        for dlt in range(2):
            base = 129 if dlt == 0 else 1
            hb = (h * 2 + dlt) % 2
            dma("sp", hk[hb][:, :], dap(eext, h * 384 + base, [[1, 128], [1, 128]]), ["eext0", "eext1"], ["hk%d" % hb])
            b = rotM.next()
            mm(psb[b][:, 0:128], Jm[:, :], hk[hb][:, :], True, True, ["Jm", "hk%d" % hb], bk(b))
            act(expB[:, h, dlt * 128:(dlt + 1) * 128], psb[b][:, 0:128], AF.Exp, bk(b), ["expB"])
    cp("dve", expBs[:, :, :], expB[:, :, 128:256], ["expB"], ["expBs"])
    memset("pool", expB[64:128, :, 128:192], 0.0, ["expB"])
    memset("pool", expBs[64:128, :, 0:64], 0.0, ["expBs"])
    memset("pool", expBs[0:64, :, 64:128], 0.0, ["expBs"])
    Uf = U[:, :, :].bitcast(F32)
    Uflat = Uf.rearrange("p a b -> p (a b)")
    allU = Ukeys(range(24), range(4))
    dma("sp", Uflat[0:3, 0:5632], T["convw"].ap(), [], allU)
    dma("sp", Uflat[3:4, 0:5632], T["convb"].ap(), [], ["ustg1"])
    dma("sp", Uflat[32:36, 0:5632], T["scv"].ap(), [], ["ustg2"])
    bA = rotM.next()
    for blk in range(44):
        mm(psb[bA][:, blk * 4:blk * 4 + 4], Uflat[0:4, blk * 128:(blk + 1) * 128], identf[0:4, 0:4], True, True,
           allU + ["ustg1", "identf"], bk(bA))
    cp("dve", cw[:, :, :], psb[bA][:, 0:176].rearrange("p (a b) -> p a b", b=4), bk(bA), ["cw"])
    bA = rotM.next()
    for blk in range(44):
        mm(psb[bA][:, blk * 4:blk * 4 + 4], Uflat[32:36, blk * 128:(blk + 1) * 128], identf[32:36, 32:36], True, True,
           allU + ["ustg2", "identf"], bk(bA))
    cp("dve", sprev[:, :, :], psb[bA][:, 0:176].rearrange("p (a b) -> p a b", b=4), bk(bA), ["sprev"])

    w_in_v = T["w_in"].ap().rearrange("(kc p) n -> p kc n", p=128)
    wup_v = T["wup"].ap().rearrange("(kc p) n -> p kc n", p=128)
    wout_v = T["wout"].ap().rearrange("(kc p) n -> p kc n", p=128)
    wpa_v = T["wpa"].ap().rearrange("(kc p) n -> p kc n", p=128)
    wpb_v = T["wpb"].ap().rearrange("(kc p) n -> p kc n", p=128)
    wdn_v = T["wdn"].ap().rearrange("(j p) n -> p j n", p=128)

    def wview8(i):
        return wbuf[i][:, :].rearrange("p (a b) -> p a b", a=8)

    def wview4(i):
        return wbuf[i][:, :].rearrange("p (a b) -> p a b", a=4)

    def wkeys(i):
        return [("wb", i, 0), ("wb", i, 1)]

    stash_ids = {}
    cur_gi = [0]

    def stash_idx(bid):
        if bid not in stash_ids:
            stash_ids[bid] = len(stash_ids)
        return stash_ids[bid]

    def stash_store(i, bid):
        idx = stash_idx(bid)
        dma("sp", stash.ap()[idx], wbuf[i][:, :], wkeys(i), [("stash", idx)])

    def stash_load(i, bid):
        idx = stash_idx(bid)
        dma("pool", wbuf[i][:, :], stash.ap()[idx], [("stash", idx)], wkeys(i))

    def load_w8(src, ncols, bid):
        i = rotW.next()
        if cur_gi[0] == 0:
            dma("pool", wview8(i)[:, :, 0:ncols], src, [], wkeys(i))
            stash_store(i, bid)
        else:
            stash_load(i, bid)
        return i

    evac_flip = [0]

    def evac_eng():
        evac_flip[0] ^= 1
        return "act" if evac_flip[0] else "dve"

    def rstd_from_ss(smt, smk, c_ss, c_out, n):
        act(smt[:, c_out:c_out + 1], smt[:, c_ss:c_ss + 1], AF.Ln, [(smk, c_ss), "epsc"], [(smk, c_out)],
            scale=1.0 / n, bias=epsc[:, 0:1])
        act(smt[:, c_out:c_out + 1], smt[:, c_out:c_out + 1], AF.Exp, [(smk, c_out)], [(smk, c_out)], scale=-0.5)

    def norm_tiles(lis, gT, gkey, c0):
        for li in lis:
            memset("dve", sm[li][:, c0:c0 + 1], 0.0, [("sm%d" % li, c0)])
        for li in lis:
            act(ot[:, :], xres[:, li, :], AF.Square, [("xres", li)], ["ot", ("sm%d" % li, c0)], accum_out=sm[li][:, c0:c0 + 1])
        for li in lis:
            act(sm[li][:, c0 + 1:c0 + 2], sm[li][:, c0:c0 + 1], AF.Ln, [("sm%d" % li, c0), "epsc"], [("sm%d" % li, c0 + 1)],
                scale=1.0 / 1024.0, bias=epsc[:, 0:1])
        for li in lis:
            act(sm[li][:, c0 + 1:c0 + 2], sm[li][:, c0 + 1:c0 + 2], AF.Exp, [("sm%d" % li, c0 + 1)], [("sm%d" % li, c0 + 1)], scale=-0.5)
        for li in lis:
            xb = rotXH.next()
            ts("dve", xh[xb][:, :], xres[:, li, :], sm[li][:, c0 + 1:c0 + 2], None, ALU.mult, None,
               [("xres", li), ("sm%d" % li, c0 + 1)], ["xh%d" % xb])
            b = rotT.next()
            pv = psbf(b)
            for kc in range(8):
                tr(pv[:, kc * 128:(kc + 1) * 128], xh[xb][:, kc * 128:(kc + 1) * 128], identb[:, :],
                   ["xh%d" % xb, "identb"], bk(b))
            tt("dve", actT[:, :, li * 128:(li + 1) * 128], pv[:, 0:1024].rearrange("p (c t) -> p c t", c=8),
               bc_last(gT[:, :], 128), ALU.mult, bk(b) + [gkey], [("actT", li)])

    def transpose_rows_to_U(src, srckey, nchunk, row0, li):
        b = rotT.next()
        pv = psbf(b)
        for c in range(nchunk):
            tr(pv[:, c * 128:(c + 1) * 128], src[:, c * 128:(c + 1) * 128], identb[:, :], [srckey, "identb"], bk(b))
        cp("dve" if row0 == 8 else evac_eng(), U[:, row0:row0 + nchunk, li * 128:(li + 1) * 128],
           pv[:, 0:nchunk * 128].rearrange("p (c t) -> p c t", c=nchunk), bk(b),
           Ukeys(range(row0, row0 + nchunk), [li]))

    groups = [("p", [0, 1, 2, 3]), ("p", [4, 5, 6, 7]), ("p", [8, 9, 10, 11]), ("p", [12, 13, 14, 15]), ("s", [0])]

    def chk(stage, gi):
        if stop == (stage, gi):
            raise _Stop()

    try:
      chk("setup", 0)
      for gi, (kind, tiles) in enumerate(groups[:ngroups] if ngroups > 0 else groups[ngroups:]):
          cur_gi[0] = gi
          NT = len(tiles)
          TK = NT * 128
          isS = kind == "s"
          lts = list(range(NT))

          for li, at in enumerate(tiles):
              src = T["xs"].ap() if isS else T["xp"].ap()[at * 128:(at + 1) * 128, :]
              dma("sp", xres[:, li, :], src, [], [("xres", li)])
          norm_tiles(lts, g1T, "g1T", 0)

          actT_keys = [("actT", li) for li in lts]

          chk("S1", gi)
          def fm_block(wi, dest_fn):
              w8 = wview8(wi)
              for c in range(4):
                  b = rotM.next()
                  for kc in range(8):
                      mm(psb[b][:, 0:TK], w8[:, kc, c * 128:(c + 1) * 128], actT[:, kc, 0:TK], kc == 0, kc == 7,
                         wkeys(wi) + actT_keys, bk(b))
                  out, wk = dest_fn(c)
                  cp(evac_eng(), out, psb[b][:, 0:TK], bk(b), wk)

          def tm_block(wi, li, ncols=512):
              b = rotM.next()
              w8 = wview8(wi)
              for kc in range(8):
                  mm(psb[b][:, 0:ncols], actT[:, kc, li * 128:(li + 1) * 128], w8[:, kc, 0:ncols], kc == 0, kc == 7,
                     wkeys(wi) + [("actT", li)], bk(b))
              return b

          def out_rows(name_p, name_s, at):
              if isS:
                  return T[name_s].ap()
              return T[name_p].ap()[at * 128:(at + 1) * 128, :]

          slot0 = (tiles[0] % 8)
          wi = load_w8(w_in_v[:, :, KA:KA + 512], 512, "KA")
          if isS:
              fm_block(wi, lambda c: (kTn[0][:, c, :], ["kTn0"]))
          else:
              fm_block(wi, lambda c: (kaT[:, c, slot0 * 128:slot0 * 128 + TK], [("kaT", (slot0 + t)) for t in lts]))
          for li, at in enumerate(tiles):
              if isS or at >= 12:
                  b = tm_block(wi, li)
                  s = rotStg.next()
                  cp(evac_eng(), stg[s][:, :], psb[b][:, :], bk(b), ["stg%d" % s])
                  dst = T["kas"].ap() if isS else T["kap"].ap()[(at - 12) * 128:(at - 11) * 128, :]
                  dma("sp", dst, stg[s][:, :], ["stg%d" % s], [])
          chk("S2a", gi)
          wi = load_w8(w_in_v[:, :, VA:VA + 512], 512, "VA")
          for li, at in enumerate(tiles):
              b = tm_block(wi, li)
              if isS:
                  vdst, vk = vAn[0][:, :, 0:64], "vAn0"
              else:
                  vdst, vk = vaA[:, at % 8, :, 0:64], ("vaA", at % 8)
              cp("dve", vdst, psb[b][:, :].rearrange("p (h d) -> p h d", h=8), bk(b), [vk])
              if isS or at >= 12:
                  s = rotStg.next()
                  cp("act", stg[s][:, :], psb[b][:, :], bk(b), ["stg%d" % s])
                  dst = T["vas"].ap() if isS else T["vap"].ap()[(at - 12) * 128:(at - 11) * 128, :]
                  dma("sp", dst, stg[s][:, :], ["stg%d" % s], [])
          chk("S2b", gi)
          wi = load_w8(w_in_v[:, :, KB:KB + 512], 512, "KB")
          if isS:
              fm_block(wi, lambda c: (kTn[1][:, c, :], ["kTn1"]))
          else:
              t0 = tiles[0] * 128
              fm_block(wi, lambda c: (kbT[:, c, t0:t0 + TK], [("kbT", at) for at in tiles]))
          chk("S2b1", gi)
          for li, at in enumerate(tiles):
              b = tm_block(wi, li)
              s = rotStg.next()
              cp(evac_eng(), stg[s][:, :], psb[b][:, :], bk(b), ["stg%d" % s])
              dma("sp", out_rows("kbp", "kbs", at), stg[s][:, :], ["stg%d" % s], [])
          chk("S2b2", gi)
          wi = load_w8(w_in_v[:, :, VB:VB + 512], 512, "VB")
          for li, at in enumerate(tiles):
              b = tm_block(wi, li)
              if isS:
                  vdst, vk = vAn[1][:, :, 0:64], "vAn1"
              else:
                  vdst, vk = vbA[:, at, :, 0:64], ("vbA", at)
              cp("dve", vdst, psb[b][:, :].rearrange("p (h d) -> p h d", h=8), bk(b), [vk])
              s = rotStg.next()
              cp("act", stg[s][:, :], psb[b][:, :], bk(b), ["stg%d" % s])
              dma("sp", out_rows("vbp", "vbs", at), stg[s][:, :], ["stg%d" % s], [])
          chk("S2c", gi)
          wi = load_w8(w_in_v[:, :, QA:QA + 512], 512, "QA")
          chk("S2c0", gi)
          fm_block(wi, lambda c: (U[:, c, 0:TK], Ukeys([c], lts)))
          chk("S2c1", gi)
          wi = load_w8(w_in_v[:, :, QB:QB + 512], 512, "QB")
          fm_block(wi, lambda c: (U[:, 4 + c, 0:TK], Ukeys([4 + c], lts)))
          chk("S2d", gi)
          wi = load_w8(w_in_v[:, :, FB:FB + 8], 8, "FB")
          for li, at in enumerate(tiles):
              b = tm_block(wi, li, ncols=8)
              tt("dve", zt[:, 0:8], psb[b][:, 0:8], bfb[:, :], ALU.add, bk(b) + ["bfb"], ["zt"])
              act(zt[:, 0:8], zt[:, 0:8], AF.Exp, ["zt"], ["zt"], scale=-1.0)
              act(zt[:, 0:8], zt[:, 0:8], AF.Ln, ["zt", "onec"], ["zt"], bias=onec[:, 0:1])
              ts("dve", lfg[:, li, :], zt[:, 0:8], -1.0, None, ALU.mult, None, ["zt"], [("lfg", li)])
              if not isS:
                  b2 = rotM.next()
                  mm(psb[b2][:, 0:8], tri[:, :], lfg[:, li, :], True, at == 0, ["tri", ("lfg", li)], bk(b2))
                  if at > 0:
                      mm(psb[b2][:, 0:8], onesf[:, :], rc[:, at, :], False, True, ["onesf", ("rc", at)], bk(b2))
                  tt("dve", rc[:, at + 1, :], rc[:, at, :], lfg[:, li, :], ALU.add, [("rc", at), ("lfg", li)], [("rc", at + 1)])
                  mm(psb[b2][:, 8:16], onesf[:, :], rc[:, at + 1, :], True, True, ["onesf", ("rc", at + 1)], bk(b2))
                  cp("dve", cc[:, at, :], psb[b2][:, 0:16], bk(b2), [("cc", at)])
          lf_dst = T["lfs"].ap() if isS else T["lfp"].ap()[tiles[0] * 128:tiles[0] * 128 + TK, :].rearrange("(t p) h -> p t h", p=128)
          if isS:
              dma("sp", lf_dst, lfg[:, 0, :], [("lfg", 0)], [])
          else:
              dma("sp", lf_dst, lfg[:, 0:NT, :], [("lfg", li) for li in lts], [])

          chk("S2", gi)
          rotST = {0: Rot([2, 3]), 64: Rot([4, 5])}

          def oreg(bank, r):
              return psb[bank][:, r * 65:(r + 1) * 65], ["ps%d" % bank]

          def normalize(h, o, ok, col0, smt, smk):
              P.op("dve", lambda e: e.reciprocal(out=smt[:, 8 + (h % 8):9 + (h % 8)], in_=o[:, 64:65]), ok, [(smk, 8 + h % 8)])
              ts("dve", ot[:, col0 + h * 64:col0 + (h + 1) * 64], o[:, 0:64], smt[:, 8 + (h % 8):9 + (h % 8)], None,
                 ALU.mult, None, ok + [(smk, 8 + h % 8)], ["ot"])

          def run_pipeline(units, skew=2):
              nun = len(units)
              for t_ in range(nun + skew):
                  if t_ < nun:
                      units[t_][0]()
                  if t_ - skew >= 0:
                      units[t_ - skew][1]()

          if not isS:
              units = []
              for li, at in enumerate(tiles):
                  smt, smk = sm[li], "sm%d" % li
                  qc = slice(li * 128, (li + 1) * 128)
                  n = rotNb.next()
                  nbv = nb[n]

                  def mk_bias(at=at, n=n, nbv=nbv, g=gi):
                      csrc = cc[:, at, 8:16]
                      j0 = 4 * g
                      tt("dve", nbv[:, j0:at + 1, :],
                         bass.AP(tensor=csrc.tensor, offset=csrc.offset, ap=[list(csrc.ap[0]), [0, at + 1 - j0], list(csrc.ap[1])]),
                         cc[:, j0:at + 1, 0:8], ALU.subtract, [("cc", j) for j in range(at + 1)], ["nb%d" % n])
                      if g > 0:
                          esrc = cc[:, 3, 8:16]
                          tt("dve", nbv[:, 0:g, :],
                             bass.AP(tensor=csrc.tensor, offset=csrc.offset, ap=[list(csrc.ap[0]), [0, g], list(csrc.ap[1])]),
                             bass.AP(tensor=esrc.tensor, offset=esrc.offset, ap=[list(esrc.ap[0]), [64, g], list(esrc.ap[1])]),
                             ALU.subtract, [("cc", j) for j in range(at + 1)] + ["nb%d" % n], ["nb%d" % n])

                  firstB = True
                  for hp in range(4):
                      bo = 6 + hp % 2
                      chunks = [(c0, hh) for c0 in range(0, at + 1, 4) for hh in range(2)]
                      for ci, (c0, hh) in enumerate(chunks):
                          js = list(range(c0, min(c0 + 4, at + 1)))
                          st_ = {}

                          def front(js=js, hh=hh, hp=hp, li=li, at=at, qc=qc, n=n, nbv=nbv, st_=st_, need_bias=firstB, mk_bias=mk_bias, gi_=gi):
                              if need_bias:
                                  mk_bias()
                              h = 2 * hp + hh
                              rlo = hh * 64
                              b = rotST[rlo].next()
                              for x, j in enumerate(js):
                                  mm(psb[b][:, x * 128:(x + 1) * 128], kbT[rlo:rlo + 64, hp, j * 128:(j + 1) * 128],
                                     U[rlo:rlo + 64, 4 + hp, qc], True, True, [("kbT", j), ("U", 4 + hp, li)], ["ps%d" % b])
                              if js[0] < 4 * gi_:
                                  w_ = rotPq.next()
                                  act(ptq[w_][:, :], psb[b][:, :], AF.Exp, ["ps%d" % b, "nb%d" % n], ["ptq%d" % w_],
                                      scale=0.125, bias=nbv[:, js[0] // 4, h:h + 1])
                                  st_["wide"] = w_
                              else:
                                  pl = []
                                  for x, j in enumerate(js):
                                      p = rotPt.next()
                                      pl.append(p)
                                      act(ptl[p][:, :], psb[b][:, x * 128:(x + 1) * 128], AF.Exp, ["ps%d" % b, "nb%d" % n], ["pt%d" % p],
                                          scale=0.125, bias=nbv[:, j, h:h + 1])
                                      if j == at:
                                          tt("dve", ptl[p][:, :], ptl[p][:, :], cmask[:, :], ALU.mult, ["pt%d" % p, "cmask"], ["pt%d" % p])
                                  st_["pl"] = pl

                          def back(js=js, hh=hh, hp=hp, bo=bo, st_=st_, first=(ci == 0), last=(ci == len(chunks) - 1),
                                   smt=smt, smk=smk):
                              h = 2 * hp + hh
                              o, ok = oreg(bo, hh)
                              for x, j in enumerate(js):
                                  if "wide" in st_:
                                      w_ = st_["wide"]
                                      mm(o, ptq[w_][:, x * 128:(x + 1) * 128], vbA[:, j, h, :], first and x == 0, False,
                                         ["ptq%d" % w_, ("vbA", j)], ok, skip=True)
                                  else:
                                      mm(o, ptl[st_["pl"][x]][:, :], vbA[:, j, h, :], first and x == 0, False,
                                         ["pt%d" % st_["pl"][x], ("vbA", j)], ok, skip=True)
                              if last:
                                  for h2 in range(2):
                                      o2, ok2 = oreg(bo, h2)
                                      normalize(2 * hp + h2, o2, ok2, 512, smt, smk)

                          units.append((front, back))
                          firstB = False
                  far = [j for j in (at - 4, at - 3, at - 2) if j >= 0]
                  near = [j for j in (at - 1, at) if j >= 0]
                  for hp in range(4):
                      bo = 6 + hp % 2
                      sub = []
                      for hh in range(2):
                          if far:
                              sub.append((hh, "far"))
                          sub.append((hh, "near"))
                      for ci, (hh, kind_) in enumerate(sub):
                          st_ = {}

                          def front(hh=hh, hp=hp, li=li, at=at, qc=qc, kind_=kind_, st_=st_, far=far, near=near):
                              h = 2 * hp + hh
                              rlo = hh * 64
                              jl = far if kind_ == "far" else near
                              b = rotST[rlo].next()
                              for x, j in enumerate(jl):
                                  mm(psb[b][:, x * 128:(x + 1) * 128], kaT[rlo:rlo + 64, hp, (j % 8) * 128:(j % 8 + 1) * 128],
                                     U[rlo:rlo + 64, hp, qc], True, True, [("kaT", j % 8), ("U", hp, li)], ["ps%d" % b])
                              nl = len(jl)
                              if kind_ == "far":
                                  w = rotPtw.next()
                                  act(ptw[w][:, 0:nl * 128], psb[b][:, 0:nl * 128], AF.Exp, ["ps%d" % b, "cbias"], ["ptw%d" % w],
                                      scale=0.125, bias=cbias[:, h:h + 1])
                                  if at - 4 >= 0:
                                      memset("dve", ptw[w][0:64, 64:128], 0.0, ["ptw%d" % w])
                                  st_["src"] = (ptw[w], "ptw%d" % w)
                              else:
                                  q = rotPn.next()
                                  act(pn32[q][:, 0:nl * 128], psb[b][:, 0:nl * 128], AF.Exp, ["ps%d" % b], ["pn32_%d" % q], scale=0.125)
                                  r = rotPtn.next()
                                  tt("dve", ptn[r][:, 0:nl * 128], pn32[q][:, 0:nl * 128], expB[:, h, 256 - nl * 128:256], ALU.mult,
                                     ["pn32_%d" % q, "expB"], ["ptn%d" % r])
                                  st_["src"] = (ptn[r], "ptn%d" % r)

                          def back(hh=hh, hp=hp, bo=bo, kind_=kind_, st_=st_, far=far, near=near, first=(ci == 0),
                                   last=(ci == len(sub) - 1), smt=smt, smk=smk, li=li, lastpair=(hp == 3)):
                              h = 2 * hp + hh
                              jl = far if kind_ == "far" else near
                              o, ok = oreg(bo, hh)
                              srcT, srck = st_["src"]
                              for x, j in enumerate(jl):
                                  mm(o, srcT[:, x * 128:(x + 1) * 128], vaA[:, j % 8, h, :], first and x == 0, False,
                                     [srck, ("vaA", j % 8)], ok, skip=True)
                              if last:
                                  for h2 in range(2):
                                      o2, ok2 = oreg(bo, h2)
                                      normalize(2 * hp + h2, o2, ok2, 0, smt, smk)
                                  if lastpair:
                                      transpose_rows_to_U(ot, "ot", 8, 8, li)

                          units.append((front, back))
              run_pipeline(units, skew=3)
              if gi < 3:
                  j0 = 4 * gi
                  esrc = cc[:, j0 + 3, 8:16]
                  tt("dve", facT[:, :, :],
                     bass.AP(tensor=esrc.tensor, offset=esrc.offset, ap=[list(esrc.ap[0]), [0, 4], list(esrc.ap[1])]),
                     cc[:, j0:j0 + 4, 0:8], ALU.subtract, [("cc", j0 + t) for t in range(4)], ["facT"])
                  act(facT[:, :, :], facT[:, :, :], AF.Exp, ["facT"], ["facT"])
                  for t in range(4):
                      tt("dve", vbA[:, j0 + t, :, :], vbA[:, j0 + t, :, :], bc_last(facT[:, t, :], 65), ALU.mult,
                         [("vbA", j0 + t), "facT"], [("vbA", j0 + t)])
          else:
              smt, smk = sm[0], "sm0"
              b2 = rotM.next()
              for s in range(2):
                  mm(psb[b2][:, s * 8:(s + 1) * 8], oseq[s][:, :], lfg[:, 0, :], True, True, ["oseq%d" % s, ("lfg", 0)], bk(b2))
              for s in range(2):
                  tsrc = psb[b2][:, s * 8:(s + 1) * 8]
                  tt("dve", bC[:, s, :, :], bC[:, s, :, :],
                     bass.AP(tensor=tsrc.tensor, offset=tsrc.offset, ap=[list(tsrc.ap[0]), [0, 16], list(tsrc.ap[1])]),
                     ALU.add, bk(b2) + [("bC", s)], [("bC", s)])
              b2 = rotM.next()
              mm(psb[b2][:, 0:8], SUb[:, :], lfg[:, 0, :], True, True, ["SUb", ("lfg", 0)], bk(b2))
              cp("dve", bN[:, :], psb[b2][:, 0:8], bk(b2), ["bN"])

              def sreg(h):
                  return oreg(6 + h // 4, h % 4)

              def sample_branch(br):
                  ncache = 4 if br == 0 else 16
                  ksrc = T["cka"] if br == 0 else T["ckb"]
                  vsrc = T["cva"] if br == 0 else T["cvb"]
                  qrow0 = 0 if br == 0 else 4
                  col0 = 0 if br == 0 else 512
                  if br == 1:
                      kcv = kbT[:, :, :].rearrange("p a b -> p (a b)").rearrange("p (j c) -> p j c", j=16)
                      vv, vname = vbA, "vbA"
                      allk = [("kbT", t) for t in range(16)]
                  else:
                      kcv = kaT[:, :, :].rearrange("p a b -> p (a b)").rearrange("p (j c) -> p j c", j=8)
                      vv, vname = vaA, "vaA"
                      allk = [("kaT", t) for t in range(8)]

                  def tile_of(s, j):
                      return j if br == 1 else 4 * s + j

                  def load_chunk(s, cj):
                      jj0 = tile_of(s, 4 * cj)
                      r0 = 4 * cj * 128
                      dma("pool", kcv[:, jj0:jj0 + 4, :], ksrc.ap()[s, r0:r0 + 512, :].rearrange("(j p) c -> p j c", p=128),
                          [], allk + [("skc", br, jj0 + x) for x in range(4)])
                      for x in range(4):
                          dma("pool", vv[:, jj0 + x, :, 0:64],
                              vsrc.ap()[s, r0 + x * 128:r0 + (x + 1) * 128, :].rearrange("p (h d) -> p h d", h=8),
                              [], [(vname, jj0 + x)])

                  if br == 1:
                      for cj in range(4):
                          load_chunk(0, cj)
                  else:
                      load_chunk(0, 0)
                      load_chunk(1, 0)
                  for s in range(2):
                      units = []
                      started = {6: False, 7: False}
                      kbst = {}
                      for j in range(ncache + 1):
                          for half in range(2):
                              st_ = {}

                              def front(j=j, half=half, s=s, st_=st_, kbst=kbst):
                                  isnew = (j == ncache)
                                  if half == 0 and not isnew:
                                      kb_ = rotKc.next()
                                      kbst[j] = kb_
                                      jj = tile_of(s, j)
                                      b = rotT.next()
                                      pv = psbf(b)
                                      for c in range(4):
                                          tr(pv[:, c * 128:(c + 1) * 128], kcv[:, jj, c * 128:(c + 1) * 128], identb[:, :],
                                             [("skc", br, jj), "identb"], bk(b))
                                      cp(evac_eng(), kcT[kb_][:, :, :], pv[:, 0:512].rearrange("p (c t) -> p c t", c=4), bk(b),
                                         ["kcT%d" % kb_])
                                  hs = list(range(half * 4, half * 4 + 4))
                                  bl = {0: rotST[0].next(), 64: rotST[64].next()}
                                  info = []
                                  for h in hs:
                                      hp, rlo = h // 2, (h % 2) * 64
                                      x = (h - half * 4) // 2
                                      b = bl[rlo]
                                      if isnew:
                                          stv = psb[b][:, x * 128:(x + 1) * 128]
                                          mm(stv, kTn[br][rlo:rlo + 64, hp, :], U[rlo:rlo + 64, qrow0 + hp, 0:128], True, True,
                                             ["kTn%d" % br, ("U", qrow0 + hp, 0)], ["ps%d" % b])
                                      else:
                                          stv = psb[b][:, x * 64:(x + 1) * 64]
                                          kb_ = kbst[j]
                                          mm(stv, kcT[kb_][rlo:rlo + 64, hp, :], U[rlo:rlo + 64, qrow0 + hp, s * 64:(s + 1) * 64], True, True,
                                             ["kcT%d" % kb_, ("U", qrow0 + hp, 0)], ["ps%d" % b])
                                      info.append((h, b, stv))
                                  pl = []
                                  for h, b, stv in info:
                                      p = rotPt.next()
                                      pk = "pt%d" % p
                                      if isnew:
                                          if br == 1:
                                              act(ptl[p][:, :], stv, AF.Exp, ["ps%d" % b, "bN"], [pk], scale=0.125, bias=bN[:, h:h + 1])
                                              tt("dve", ptl[p][:, :], ptl[p][:, :], cmask_s[:, :], ALU.mult, [pk, "cmask_s"], [pk])
                                          else:
                                              q = rotPn.next()
                                              act(pn32[q][:, 0:128], stv, AF.Exp, ["ps%d" % b], ["pn32_%d" % q], scale=0.125)
                                              tt("dve", ptl[p][:, :], pn32[q][:, 0:128], expBs[:, h, :], ALU.mult,
                                                 ["pn32_%d" % q, "expBs"], [pk])
                                      elif br == 1:
                                          act(ptl[p][:, s * 64:(s + 1) * 64], stv, AF.Exp, ["ps%d" % b, ("bC", s)], [pk], scale=0.125,
                                              bias=bC[:, s, j, h:h + 1])
                                      elif j < 3:
                                          act(ptl[p][:, s * 64:(s + 1) * 64], stv, AF.Exp, ["ps%d" % b, "cbias"], [pk], scale=0.125,
                                              bias=cbias[:, h:h + 1])
                                      else:
                                          q = rotPn.next()
                                          act(pn32[q][:, 0:64], stv, AF.Exp, ["ps%d" % b], ["pn32_%d" % q], scale=0.125)
                                          tt("dve", ptl[p][:, s * 64:(s + 1) * 64], pn32[q][:, 0:64], expB[:, h, 0:64],
                                             ALU.mult, ["pn32_%d" % q, "expB"], [pk])
                                      pl.append((h, p))
                                  st_["pl"] = pl

                              def back(j=j, half=half, s=s, st_=st_, kbst=kbst, started=started):
                                  isnew = (j == ncache)
                                  for h, p in st_["pl"]:
                                      o, ok = sreg(h)
                                      bo = 6 + h // 4
                                      if isnew:
                                          mm(o, ptl[p][:, :], vAn[br][:, h, :], not started[bo], False, ["pt%d" % p, "vAn%d" % br], ok, skip=True)
                                      else:
                                          jj = tile_of(s, j)
                                          mm(o, ptl[p][:, :], vv[:, jj, h, :], not started[bo], False, ["pt%d" % p, (vname, jj)], ok, skip=True)
                                      started[bo] = True
                                  if br == 1 and s == 0 and half == 1 and (not isnew) and j % 4 == 3:
                                      load_chunk(1, j // 4)

                              units.append((front, back))
                      run_pipeline(units, skew=2)
                      rs_ = slice(s * 64, (s + 1) * 64)
                      for h in range(8):
                          o, ok = sreg(h)
                          P.op("dve", lambda e, o=o, h=h, rs_=rs_: e.reciprocal(out=smt[rs_, 8 + h:9 + h], in_=o[rs_, 64:65]),
                               ok, [(smk, 8 + h)])
                          ts("dve", ot[rs_, col0 + h * 64:col0 + (h + 1) * 64], o[rs_, 0:64], smt[rs_, 8 + h:9 + h], None,
                             ALU.mult, None, ok + [(smk, 8 + h)], ["ot"])

              sample_branch(1)
              sample_branch(0)
              transpose_rows_to_U(ot, "ot", 8, 8, 0)

          chk("S3", gi)
          s4units = []
          for cb in range(2):
              wst = {}
              for li in lts:
                  st_ = {}

                  def front(cb=cb, li=li, wst=wst, st_=st_, first=(li == 0)):
                      if first:
                          wab_ = rotW.next()
                          if cur_gi[0] == 0:
                              dma("pool", wview8(wab_)[:, 0:4, :], wpa_v[:, :, cb * 512:(cb + 1) * 512], [], [("wb", wab_, 0)])
                              dma("pool", wview8(wab_)[:, 4:8, :], wpb_v[:, :, cb * 512:(cb + 1) * 512], [], [("wb", wab_, 1)])
                              stash_store(wab_, ("wab", cb))
                          else:
                              stash_load(wab_, ("wab", cb))
                          wst["wab"] = wab_
                          wst["wga"] = load_w8(w_in_v[:, :, GA + cb * 512:GA + (cb + 1) * 512], 512, ("wga", cb))
                          wst["wgb"] = load_w8(w_in_v[:, :, GB + cb * 512:GB + (cb + 1) * 512], 512, ("wgb", cb))
                      wab, wga, wgb = wst["wab"], wst["wga"], wst["wgb"]
                      tsl = slice(li * 128, (li + 1) * 128)
                      bpa, bpb, bga, bgb = rotAll.next(), rotAll.next(), rotAll.next(), rotAll.next()
                      for c in range(4):
                          mm(psb[bpa][:, :], U[:, 8 + c, tsl], wview8(wab)[:, c, :], c == 0, c == 3,
                             [("U", 8 + c, li), ("wb", wab, 0)], bk(bpa))
                      for c in range(4):
                          mm(psb[bpb][:, :], U[:, 12 + c, tsl], wview8(wab)[:, 4 + c, :], c == 0, c == 3,
                             [("U", 12 + c, li), ("wb", wab, 1)], bk(bpb))
                      for kc in range(8):
                          mm(psb[bga][:, :], actT[:, kc, tsl], wview8(wga)[:, kc, :], kc == 0, kc == 7,
                             [("actT", li)] + wkeys(wga), bk(bga))
                      for kc in range(8):
                          mm(psb[bgb][:, :], actT[:, kc, tsl], wview8(wgb)[:, kc, :], kc == 0, kc == 7,
                             [("actT", li)] + wkeys(wgb), bk(bgb))
                      sA, sB = yy[0][0], yy[1][0]
                      act(sA[:, :], psb[bga][:, :], AF.Sigmoid, bk(bga), ["yy0_0"])
                      act(sB[:, :], psb[bgb][:, :], AF.Sigmoid, bk(bgb), ["yy1_0"])
                      tt("dve", sA[:, :], psb[bpa][:, :], sA[:, :], ALU.mult, bk(bpa) + ["yy0_0"], ["yy0_0"])
                      tt("dve", sB[:, :], psb[bpb][:, :], sB[:, :], ALU.mult, bk(bpb) + ["yy1_0"], ["yy1_0"])
                      m = rotMt.next()
                      tt("dve", mt[m][:, :], sA[:, :], sB[:, :], ALU.add, ["yy0_0", "yy1_0"], ["mt%d" % m])
                      st_["m"] = m

                  def back(cb=cb, li=li, st_=st_):
                      m = st_["m"]
                      transpose_rows_to_U(mt[m], "mt%d" % m, 4, 16 + cb * 4, li)

                  s4units.append((front, back))
          for t_ in range(len(s4units) + 1):
              if t_ < len(s4units):
                  s4units[t_][0]()
              if t_ >= 1:
                  s4units[t_ - 1][1]()

          chk("S4", gi)
          dma("sp", gbb[:, :], dap(T["g2"], 0, [[0, 128], [1, 1024]]), [], ["gbb"])
          wo = [load_w8(wout_v[:, :, hf * 512:(hf + 1) * 512], 512, ("wo", hf)) for hf in range(2)]
          bms = {}
          for li in lts:
              tsl = slice(li * 128, (li + 1) * 128)
              bms[li] = [2 * li, 2 * li + 1]
              for hf in range(2):
                  for kc in range(8):
                      mm(psb[bms[li][hf]][:, :], U[:, 16 + kc, tsl], wview8(wo[hf])[:, kc, :], kc == 0, kc == 7,
                         [("U", 16 + kc, li)] + wkeys(wo[hf]), bk(bms[li][hf]))
              smt, smk = sm[li], "sm%d" % li
              memset("dve", smt[:, 2:4], 0.0, [(smk, 2), (smk, 3)])
              for hf in range(2):
                  act(ot[:, hf * 512:(hf + 1) * 512], psb[bms[li][hf]][:, :], AF.Square, bk(bms[li][hf]), ["ot", (smk, 2 + hf)],
                      accum_out=smt[:, 2 + hf:3 + hf])
              tt("dve", smt[:, 2:3], smt[:, 2:3], smt[:, 3:4], ALU.add, [(smk, 2), (smk, 3)], [(smk, 2)])
          for li in lts:
              smt, smk = sm[li], "sm%d" % li
              act(smt[:, 3:4], smt[:, 2:3], AF.Ln, [(smk, 2), "epsc"], [(smk, 3)], scale=1.0 / 1024.0, bias=epsc[:, 0:1])
          for li in lts:
              smt, smk = sm[li], "sm%d" % li
              act(smt[:, 3:4], smt[:, 3:4], AF.Exp, [(smk, 3)], [(smk, 3)], scale=-0.5)
          for li in lts:
              smt, smk = sm[li], "sm%d" % li
              for hf in range(2):
                  stt("dve", yy[hf][1][:, :], psb[bms[li][hf]][:, :], smt[:, 3:4], gbb[:, hf * 512:(hf + 1) * 512],
                      ALU.mult, ALU.mult, bk(bms[li][hf]) + [(smk, 3), "gbb"], ["yy%d_1" % hf])
                  tt("dve", xres[:, li, hf * 512:(hf + 1) * 512], xres[:, li, hf * 512:(hf + 1) * 512], yy[hf][1][:, :], ALU.add,
                     [("xres", li), "yy%d_1" % hf], [("xres", li)])
              norm_tiles([li], g3T, "g3T", 4)

          chk("S5", gi)
          nseg = 2 if isS else 1
          seglen = 64 if isS else TK
          cr = carry[1] if isS else carry[0]
          crk = "carry1" if isS else "carry0"
          for blk in range(6):
              ncol = 512 if blk < 5 else 256
              wg = load_w8(wup_v[:, :, blk * 512:blk * 512 + ncol], ncol, ("upg", blk))
              wv = load_w8(wup_v[:, :, FF + blk * 512:FF + blk * 512 + ncol], ncol, ("upv", blk))
              for jj in range(ncol // 128):
                  j = blk * 4 + jj
                  ci = rotC.next()
                  ys = []
                  for a, wi_ in ((0, wg), (1, wv)):
                      fidx = j if a == 0 else 22 + j
                      b = rotAll.next()
                      for kc in range(8):
                          mm(psb[b][:, 0:TK], wview8(wi_)[:, kc, jj * 128:(jj + 1) * 128], actT[:, kc, 0:TK], kc == 0, kc == 7,
                             wkeys(wi_) + actT_keys, bk(b))
                      ue = uext[a][0]
                      uk = "uext%d_0" % a
                      y = yy[a][ci]
                      yk = "yy%d_%d" % (a, ci)
                      uev = ue[:, 0:nseg * (seglen + 2)].rearrange("p (s t) -> p s t", s=nseg)
                      psv = psb[b][:, 0:TK].rearrange("p (s t) -> p s t", s=nseg)
                      yv = y[:, 0:TK].rearrange("p (s t) -> p s t", s=nseg)
                      act(y[:, 0:TK], psb[b][:, 0:TK], AF.Identity, bk(b) + ["cw"], [yk], scale=cw[:, fidx, 2:3], bias=cw[:, fidx, 3:4])
                      cp("act", uev[:, :, 2:2 + seglen], psv, bk(b), [uk])
                      if isS:
                          cp("act", uev[:, :, 0:2], sprev[:, fidx, :].rearrange("p (s t) -> p s t", s=2), ["sprev"], [uk])
                      else:
                          cp("act", uev[:, :, 0:2], cr[:, fidx, 0:2].rearrange("p (s t) -> p s t", s=1), [crk], [uk])
                      ce = "dve"
                      stt(ce, yv, uev[:, :, 1:1 + seglen], cw[:, fidx, 1:2], yv, ALU.mult, ALU.add, [uk, "cw", yk], [yk])
                      stt(ce, yv, uev[:, :, 0:seglen], cw[:, fidx, 0:1], yv, ALU.mult, ALU.add, [uk, "cw", yk], [yk])
                      if isS:
                          cp("act", cr[:, fidx, :].rearrange("p (s t) -> p s t", s=2), uev[:, :, seglen:seglen + 2], [uk], [crk])
                      else:
                          cp("act", cr[:, fidx, 0:2].rearrange("p (s t) -> p s t", s=1), uev[:, :, seglen:seglen + 2], [uk], [crk])
                      ys.append((y, yk))
                  (yg, ygk), (yvv, yvk) = ys
                  act(yg[:, 0:TK], yg[:, 0:TK], AF.Gelu_apprx_tanh, [ygk], [ygk])
                  tt("dve", U[:, j, 0:TK], yg[:, 0:TK], yvv[:, 0:TK], ALU.mult, [ygk, yvk], Ukeys([j], lts))
          if isS or gi == 3:
              nr = 4 if isS else 2
              dstT = T["cvs"] if isS else T["cvp"]
              for q4 in range(11):
                  b = rotAll.next()
                  for x in range(4):
                      f = q4 * 4 + x
                      mm(psb[b][0:nr, x * 128:(x + 1) * 128], cr[:, f, 0:nr], identf[:, :], True, True, [crk, "identf"], bk(b))
                  cv = rotStg.next()
                  cp("dve", stg[cv][0:nr, :], psb[b][0:nr, :], bk(b), ["stg%d" % cv])
                  dma("sp", dstT.ap()[:, q4 * 512:(q4 + 1) * 512], stg[cv][0:nr, :], ["stg%d" % cv], [])

          chk("S6", gi)
          dma("sp", gbb[:, :], dap(T["g4"], 0, [[0, 128], [1, 1024]]), [], ["gbb"])
          for blk in range(6):
              nj = 4 if blk < 5 else 2
              wi_ = rotW.next()
              if cur_gi[0] == 0:
                  dma("pool", wview4(wi_)[:, 0:nj, :], wdn_v[:, blk * 4:blk * 4 + nj, :], [], wkeys(wi_))
                  stash_store(wi_, ("dn", blk))
              else:
                  stash_load(wi_, ("dn", blk))
              for jj in range(nj):
                  j = blk * 4 + jj
                  for li in lts:
                      for hf in range(2):
                          b = 2 * li + hf
                          mm(psb[b][:, :], U[:, j, li * 128:(li + 1) * 128], wview4(wi_)[:, jj, hf * 512:(hf + 1) * 512],
                             j == 0, j == 21, [("U", j, li)] + wkeys(wi_), bk(b))
          for li, at in enumerate(tiles):
              smt, smk = sm[li], "sm%d" % li
              memset("dve", smt[:, 6:8], 0.0, [(smk, 6), (smk, 7)])
              for hf in range(2):
                  b = 2 * li + hf
                  act(ot[:, hf * 512:(hf + 1) * 512], psb[b][:, :], AF.Square, bk(b), ["ot", (smk, 6 + hf)],
                      accum_out=smt[:, 6 + hf:7 + hf])
              tt("dve", smt[:, 6:7], smt[:, 6:7], smt[:, 7:8], ALU.add, [(smk, 6), (smk, 7)], [(smk, 6)])
          for li in lts:
              smt, smk = sm[li], "sm%d" % li
              act(smt[:, 7:8], smt[:, 6:7], AF.Ln, [(smk, 6), "epsc"], [(smk, 7)], scale=1.0 / 1024.0, bias=epsc[:, 0:1])
          for li in lts:
              smt, smk = sm[li], "sm%d" % li
              act(smt[:, 7:8], smt[:, 7:8], AF.Exp, [(smk, 7)], [(smk, 7)], scale=-0.5)
          for li, at in enumerate(tiles):
              smt, smk = sm[li], "sm%d" % li
              for hf in range(2):
                  b = 2 * li + hf
                  stt("dve", yy[hf][1][:, :], psb[b][:, :], smt[:, 7:8], gbb[:, hf * 512:(hf + 1) * 512],
                      ALU.mult, ALU.mult, bk(b) + [(smk, 7), "gbb"], ["yy%d_1" % hf])
                  tt("dve", xres[:, li, hf * 512:(hf + 1) * 512], xres[:, li, hf * 512:(hf + 1) * 512], yy[hf][1][:, :], ALU.add,
                     [("xres", li), "yy%d_1" % hf], [("xres", li)])
              dst = T["ys"].ap() if isS else T["yp"].ap()[at * 128:(at + 1) * 128, :]
              dma("sp", dst, xres[:, li, :], [("xres", li)], [])
    except _Stop:
        pass
    if os.environ.get('DBG_MEM'):
        print('SBUF remaining', nc.sbuf_bytes_remaining, 'base', nc.sbuf_base, 'top', nc.sbuf_top)
    P.emit(nc)
    st.close()
    return nc


_NC = None


def kernel(x_prompt, x_sample, cache_k_a, cache_v_a, cache_k_b, cache_v_b, cache_logf_b, state_conv_ffn,
           g_pre_mix, g_post_mix, g_pre_ffn, g_post_ffn, w_in, b_f, rel_table, w_proj_a, w_proj_b, w_out,
           w_up, conv_w, conv_b, w_down):
    global _NC
    f = lambda a: np.ascontiguousarray(np.asarray(a, dtype=np.float32))
    if _NC is None:
        _NC = build()
    nc = _NC
    shared = {
        "g1": f(g_pre_mix), "g2": f(g_post_mix), "g3": f(g_pre_ffn), "g4": f(g_post_ffn),
        "w_in": f(w_in)[0], "b_f": f(b_f), "rel": f(rel_table)[0],
        "wpa": f(w_proj_a)[0], "wpb": f(w_proj_b)[0], "wout": f(w_out)[0], "wup": f(w_up)[0],
        "convw": f(conv_w)[0], "convb": f(conv_b), "wdn": f(w_down)[0],
    }
    xp, xs = f(x_prompt), f(x_sample)
    cka, cva, ckb, cvb = f(cache_k_a)[0], f(cache_v_a)[0], f(cache_k_b)[0], f(cache_v_b)[0]
    clf, scv = f(cache_logf_b)[0], f(state_conv_ffn)[0]
    in_maps = []
    for c in range(8):
        m = dict(shared)
        m["xp"] = xp[c]
        m["xs"] = xs[2 * c:2 * c + 2].reshape(128, 1024)
        m["cka"] = cka[2 * c:2 * c + 2].reshape(2, 512, 512)
        m["cva"] = cva[2 * c:2 * c + 2].reshape(2, 512, 512)
        m["ckb"] = ckb[2 * c:2 * c + 2].reshape(2, 2048, 512)
        m["cvb"] = cvb[2 * c:2 * c + 2].reshape(2, 2048, 512)
        m["clf"] = clf[2 * c:2 * c + 2]
        m["scv"] = scv[2 * c:2 * c + 2].reshape(4, 5632)
        in_maps.append(m)
    res = run_bass_kernel_spmd(nc, in_maps, core_ids=list(range(8)))
    R = res.results
    cat = lambda n: np.stack([np.asarray(R[c][n], dtype=np.float32) for c in range(8)])
    y_p = cat("yp")
    y_s = cat("ys").reshape(16, 64, 1024)
    ka_p = cat("kap").reshape(1, 8, 512, 8, 64)
    va_p = cat("vap").reshape(1, 8, 512, 8, 64)
    kb_p = cat("kbp").reshape(1, 8, 2048, 8, 64)
    vb_p = cat("vbp").reshape(1, 8, 2048, 8, 64)
    lf_p = cat("lfp").reshape(1, 8, 2048, 8)
    cv_p = cat("cvp").reshape(1, 8, 2, 5632)
    ka_s = cat("kas").reshape(1, 16, 64, 8, 64)
    va_s = cat("vas").reshape(1, 16, 64, 8, 64)
    kb_s = cat("kbs").reshape(1, 16, 64, 8, 64)
    vb_s = cat("vbs").reshape(1, 16, 64, 8, 64)
    lf_s = cat("lfs").reshape(1, 16, 64, 8)
    cv_s = cat("cvs").reshape(1, 16, 2, 5632)
    return (y_p, y_s, ka_p, va_p, kb_p, vb_p, lf_p, cv_p, ka_s, va_s, kb_s, vb_s, lf_s, cv_s)
```

```python
import contextlib
import numpy as np
import concourse.bass as bass
import concourse.mybir as mybir
from concourse.bass_utils import run_bass_kernel_spmd

F32 = mybir.dt.float32
BF16 = mybir.dt.bfloat16
AF = mybir.ActivationFunctionType
ALU = mybir.AluOpType

ENGS = ("pe", "act", "dve", "pool", "sp")
N_DMA_SEMS = 40
N_POOL_SEMS = 8


class Op:
    __slots__ = ("eng", "fn", "dma", "deps", "has_dep", "ticket", "sem")

    def __init__(self, eng, fn, dma):
        self.eng = eng
        self.fn = fn
        self.dma = dma
        self.deps = []
        self.has_dep = False
        self.ticket = None
        self.sem = None


class Prog:
    def __init__(self):
        self.ops = {e: [] for e in ENGS}
        self.last_w = {}
        self.readers = {}
        self.dma_count = 0
        self.pool_dma_count = 0
        self.dma_last = {}

    def op(self, eng, fn, reads=(), writes=(), dma=False):
        o = Op(eng, fn, dma)
        deps = {}

        def add(d, kind):
            if d is o:
                return
            if d.eng == eng and not d.dma and not dma:
                if eng == "pe":
                    return
            deps[id(d)] = d

        for k in reads:
            if isinstance(k, str) and k.startswith("ps"):
                k = k.split(".")[0]
                w = self.last_w.get(k)
                if w is not None:
                    add(w, "raw")
                for rd in self.readers.get(k, ()):
                    if rd.eng != eng:
                        add(rd, "war")
                self.readers.setdefault(k, []).append(o)
            else:
                w = self.last_w.get(k)
                if w is not None:
                    add(w, "raw")
                self.readers.setdefault(k, []).append(o)
        for k in writes:
            if isinstance(k, str) and k.startswith("ps"):
                k = k.split(".")[0]
            w = self.last_w.get(k)
            if w is not None:
                add(w, "waw")
            for rd in self.readers.get(k, ()):
                add(rd, "war")
            self.last_w[k] = o
            self.readers[k] = []
        if dma:
            if eng == "pool":
                slot = self.pool_dma_count % N_POOL_SEMS
                self.pool_dma_count += 1
            else:
                slot = N_POOL_SEMS + self.dma_count % (N_DMA_SEMS - N_POOL_SEMS)
                self.dma_count += 1
            prev = self.dma_last.get(slot)
            if prev is not None:
                deps[id(prev)] = prev
            self.dma_last[slot] = o
            o.sem = slot
        for d in deps.values():
            o.deps.append(d)
            d.has_dep = True
        self.ops[eng].append(o)
        return o

    def emit(self, nc, final_wait_eng="sp"):
        counts = {e: 0 for e in ENGS}
        dma_vals = [0] * N_DMA_SEMS
        for e in ENGS:
            for o in self.ops[e]:
                if o.dma:
                    dma_vals[o.sem] += 16
                    o.ticket = dma_vals[o.sem]
                elif o.has_dep:
                    counts[e] += 1
                    o.ticket = counts[e]
        with contextlib.ExitStack() as st:
            esem = {e: st.enter_context(nc.semaphore("s_" + e)) for e in ENGS if e != "sp"}
            dsem = [st.enter_context(nc.semaphore("d%d" % i)) for i in range(N_DMA_SEMS)]
            block = st.enter_context(nc.Block())

            def run(e, eng):
                waited = {}
                for o in self.ops[e]:
                    need = {}
                    for d in o.deps:
                        key = ("d", d.sem) if d.dma else ("e", d.eng)
                        if d.ticket > need.get(key, 0):
                            need[key] = d.ticket
                    for key, tk in need.items():
                        if waited.get(key, 0) >= tk:
                            continue
                        s = dsem[key[1]] if key[0] == "d" else esem[key[1]]
                        eng.wait_ge(s, tk)
                        waited[key] = tk
                    ins = o.fn(eng)
                    if o.dma:
                        ins.then_inc(dsem[o.sem], 16)
                    elif o.has_dep:
                        ins.then_inc(esem[e], 1)
                if e == final_wait_eng:
                    for i in range(N_DMA_SEMS):
                        if dma_vals[i] and waited.get(("d", i), 0) < dma_vals[i]:
                            eng.wait_ge(dsem[i], dma_vals[i])
                    for e2 in esem:
                        if counts[e2]:
                            eng.wait_ge(esem[e2], counts[e2])

            block.tensor(lambda eng: run("pe", eng))
            block.scalar(lambda eng: run("act", eng))
            block.vector(lambda eng: run("dve", eng))
            block.gpsimd(lambda eng: run("pool", eng))
            block.sync(lambda eng: run("sp", eng))


D = 1024
KD = 8
S = 2048
NTP = 16
INC = 5128
FF = 2816
NJ = 22
QA, KA, VA, QB, KB, VB, FB, GA, GB = 0, 512, 1024, 1536, 2048, 2560, 3072, 3080, 4104
EPS = 1e-6
NWB = 4

IN_SPECS = [
    ("xp", [2048, 1024]), ("xs", [128, 1024]),
    ("cka", [2, 512, 512]), ("cva", [2, 512, 512]), ("ckb", [2, 2048, 512]), ("cvb", [2, 2048, 512]),
    ("clf", [2, 2048, 8]), ("scv", [4, 5632]),
    ("g1", [1, 1024]), ("g2", [1, 1024]), ("g3", [1, 1024]), ("g4", [1, 1024]),
    ("w_in", [1024, INC]), ("b_f", [1, 8]), ("rel", [8, 257]),
    ("wpa", [512, 1024]), ("wpb", [512, 1024]), ("wout", [1024, 1024]),
    ("wup", [1024, 2 * FF]), ("convw", [3, 2 * FF]), ("convb", [1, 2 * FF]), ("wdn", [FF, 1024]),
]
OUT_SPECS = [
    ("yp", [2048, 1024]), ("ys", [128, 1024]),
    ("kap", [512, 512]), ("vap", [512, 512]), ("kbp", [2048, 512]), ("vbp", [2048, 512]),
    ("lfp", [2048, 8]), ("cvp", [2, 2 * FF]),
    ("kas", [128, 512]), ("vas", [128, 512]), ("kbs", [128, 512]), ("vbs", [128, 512]),
    ("lfs", [128, 8]), ("cvs", [4, 2 * FF]),
]


class _Stop(Exception):
    pass


def build(stop=None, ngroups=5):
    nc = bass.Bass("TRN2", target_bir_lowering=False)
    T = {}
    for n, sh in IN_SPECS:
        T[n] = nc.dram_tensor(n, sh, F32, kind="ExternalInput")
    for n, sh in OUT_SPECS:
        T[n] = nc.dram_tensor(n, sh, F32, kind="ExternalOutput")
    eext = nc.dram_tensor("eext", [8, 384], F32)
    stash = nc.dram_tensor("wstash", [40, 128, 4096], BF16)

    P = Prog()
    st = contextlib.ExitStack()

    def sb(name, shape, dt):
        return st.enter_context(nc.sbuf_tensor(name, shape, dt))

    def dap(t, off, ap):
        return bass.AP(tensor=t, offset=off, ap=ap)

    def mm(out, lhsT, rhs, start, stop, reads, writes, skip=False):
        if skip:
            P.op("pe", lambda e: e.matmul(out, lhsT=lhsT, rhs=rhs, start=start, stop=stop, skip_group_check=True), reads, writes)
        else:
            P.op("pe", lambda e: e.matmul(out, lhsT=lhsT, rhs=rhs, start=start, stop=stop), reads, writes)

    def tr(out, in_, ident, reads, writes):
        P.op("pe", lambda e: e.transpose(out, in_, ident), reads, writes)

    def act(out, in_, func, reads, writes, **kw):
        P.op("act", lambda e: e.activation(out=out, in_=in_, func=func, **kw), reads, writes)

    def tt(eng, out, in0, in1, op, reads, writes):
        P.op(eng, lambda e: e.tensor_tensor(out=out, in0=in0, in1=in1, op=op), reads, writes)

    def ts(eng, out, in0, s1, s2, op0, op1, reads, writes):
        if s2 is None:
            P.op(eng, lambda e: e.tensor_scalar(out=out, in0=in0, scalar1=s1, scalar2=None, op0=op0), reads, writes)
        else:
            P.op(eng, lambda e: e.tensor_scalar(out=out, in0=in0, scalar1=s1, scalar2=s2, op0=op0, op1=op1), reads, writes)

    def stt(eng, out, in0, scalar, in1, op0, op1, reads, writes):
        P.op(eng, lambda e: e.scalar_tensor_tensor(out=out, in0=in0, scalar=scalar, in1=in1, op0=op0, op1=op1), reads, writes)

    def cp(eng, out, in_, reads, writes):
        if eng == "act":
            act(out, in_, AF.Copy, reads, writes)
        else:
            P.op(eng, lambda e: e.tensor_copy(out=out, in_=in_), reads, writes)

    def memset(eng, ap, val, writes):
        P.op(eng, lambda e: e.memset(ap, val), (), writes)

    def asel(out, pattern, cmp, fill, base, cm, key):
        P.op("pool", lambda e: e.affine_select(out=out, in_=out, pattern=pattern, compare_op=cmp, fill=fill,
                                               base=base, channel_multiplier=cm), [key], [key])

    import os
    SKIP = os.environ.get("DBG_SKIP", "")

    def dma(eng, out, in_, reads, writes, slow=False):
        if "dmaout" in SKIP and not writes and not slow:
            return
        if slow:
            P.op(eng, lambda e: e.dma_start(out=out, in_=in_, allow_slow_non_contiguous=True), reads, writes, dma=True)
        else:
            P.op(eng, lambda e: e.dma_start(out=out, in_=in_), reads, writes, dma=True)

    def bc_last(ap, n):
        return bass.AP(tensor=ap.tensor, offset=ap.offset, ap=[list(x) for x in ap.ap] + [[0, n]])

    psb = [st.enter_context(nc.psum_tensor("ps%d" % i, [128, 512], F32)) for i in range(8)]

    def bk(b):
        return ["ps%d" % b]

    class Rot:
        def __init__(self, items):
            self.items = list(items)
            self.i = 0

        def next(self):
            x = self.items[self.i % len(self.items)]
            self.i += 1
            return x

    rotT = Rot([0, 1])
    rotM = Rot([2, 3, 4, 5, 6, 7])
    rotAll = Rot(range(8))

    def psbf(b):
        return psb[b][:, :].bitcast(BF16)

    kbT = sb("kbT", [128, 4, S], BF16)
    kaT = sb("kaT", [128, 4, 1024], BF16)
    vbA = sb("vbA", [128, 16, 8, 65], BF16)
    vaA = sb("vaA", [128, 8, 8, 65], BF16)
    cc = sb("cc", [128, 16, 16], F32)
    rc = sb("rc", [128, 17, 8], F32)
    identb = sb("identb", [128, 128], BF16)
    identf = sb("identf", [128, 128], F32)
    tri = sb("tri", [128, 128], F32)
    onesf = sb("onesf", [128, 128], F32)
    Jm = sb("Jm", [128, 128], F32)
    SU = sb("SU", [128, 128], F32)
    SUb = sb("SUb", [128, 128], F32)
    oseq = [sb("oseq%d" % s, [128, 128], F32) for s in range(2)]
    cmask = sb("cmask", [128, 128], BF16)
    cmask_s = sb("cmask_s", [128, 128], BF16)
    expB = sb("expB", [128, 8, 256], F32)
    expBs = sb("expBs", [128, 8, 128], F32)
    cbias = sb("cbias", [128, 8], F32)
    gbb = sb("gbb", [128, 1024], F32)
    g1T = sb("g1T", [128, 8], F32)
    g3T = sb("g3T", [128, 8], F32)
    bfb = sb("bfb", [128, 8], F32)
    zt = sb("zt", [128, 16], F32)
    onec = sb("onec", [128, 1], F32)
    epsc = sb("epsc", [128, 1], F32)
    cw = sb("cw", [128, 44, 4], F32)
    sprev = sb("sprev", [128, 44, 4], F32)
    carry = [sb("carry%d" % i, [128, 44, 4], F32) for i in range(2)]
    clfT = sb("clfT", [128, 2, 16, 8], F32)
    rs = sb("rs", [128, 2, 17, 8], F32)
    bC = sb("bC", [128, 2, 16, 8], F32)
    bN = sb("bN", [128, 8], F32)
    xres = sb("xres", [128, 4, 1024], F32)
    actT = sb("actT", [128, 8, 512], BF16)
    U = sb("U", [128, 24, 512], BF16)
    kTn = [sb("kTn%d" % i, [128, 4, 128], BF16) for i in range(2)]
    vAn = [sb("vAn%d" % i, [128, 8, 65], BF16) for i in range(2)]
    lfg = sb("lfg", [128, 4, 8], F32)
    sm = [sb("sm%d" % i, [128, 16], F32) for i in range(4)]
    wbuf = [sb("wb%d" % i, [128, 4096], BF16) for i in range(NWB)]
    rotW = Rot(range(NWB))
    xh = [sb("xh%d" % i, [128, 1024], BF16) for i in range(2)]
    rotXH = Rot(range(2))
    stg = [sb("stg%d" % i, [128, 512], F32) for i in range(2)]
    rotStg = Rot(range(2))
    ptl = [sb("pt%d" % i, [128, 128], BF16) for i in range(16)]
    rotPt = Rot(range(16))
    ptw = [sb("ptw%d" % i, [128, 384], BF16) for i in range(3)]
    rotPtw = Rot(range(3))
    pn32 = [sb("pn32_%d" % i, [128, 256], F32) for i in range(2)]
    rotPn = Rot(range(2))
    ptn = [sb("ptn%d" % i, [128, 256], BF16) for i in range(4)]
    rotPtn = Rot(range(4))
    ot = sb("ot", [128, 1024], BF16)
    ptq = [sb("ptq%d" % i, [128, 512], BF16) for i in range(5)]
    rotPq = Rot(range(5))
    facT = sb("facT", [128, 4, 8], F32)
    nb = [sb("nb%d" % i, [128, 16, 8], F32) for i in range(2)]
    rotNb = Rot(range(2))
    mt = [sb("mt%d" % i, [128, 512], BF16) for i in range(2)]
    rotMt = Rot(range(2))
    uext = [[sb("uext%d_%d" % (a, i), [128, 516], F32) for i in range(1)] for a in range(2)]
    yy = [[sb("yy%d_%d" % (a, i), [128, 512], F32) for i in range(2)] for a in range(2)]
    rotC = Rot(range(2))
    kcT = [sb("kcT%d" % i, [128, 4, 128], BF16) for i in range(2)]
    rotKc = Rot(range(2))
    hk = [sb("hk%d" % i, [128, 128], F32) for i in range(2)]

    def Ukeys(rows, tiles):
        return [("U", r, t) for r in rows for t in tiles]

    memset("pool", identb[:, :], 0.0, ["identb"])
    asel(identb[:, :], [[-1, 128]], ALU.not_equal, 1.0, 0, 1, "identb")
    memset("pool", identf[:, :], 0.0, ["identf"])
    asel(identf[:, :], [[-1, 128]], ALU.not_equal, 1.0, 0, 1, "identf")
    memset("pool", onesf[:, :], 1.0, ["onesf"])
    memset("pool", tri[:, :], 1.0, ["tri"])
    asel(tri[:, :], [[1, 128]], ALU.is_ge, 0.0, 0, -1, "tri")
    memset("pool", Jm[:, :], 0.0, ["Jm"])
    asel(Jm[:, :], [[1, 128]], ALU.not_equal, 1.0, -127, 1, "Jm")
    memset("pool", SU[:, :], 1.0, ["SU"])
    asel(SU[:, :], [[-1, 128]], ALU.is_gt, 0.0, 0, 1, "SU")
    memset("pool", SUb[:, :], 1.0, ["SUb"])
    asel(SUb[:, :], [[-1, 128]], ALU.is_gt, 0.0, 0, 1, "SUb")
    memset("pool", SUb[64:128, 0:64], 0.0, ["SUb"])
    for s in range(2):
        memset("pool", oseq[s][:, :], 0.0, ["oseq%d" % s])
        memset("pool", oseq[s][s * 64:(s + 1) * 64, :], 1.0, ["oseq%d" % s])
    memset("pool", cmask[:, :], 1.0, ["cmask"])
    asel(cmask[:, :], [[1, 128]], ALU.is_ge, 0.0, 0, -1, "cmask")
    memset("pool", cmask_s[:, :], 1.0, ["cmask_s"])
    asel(cmask_s[:, :], [[1, 128]], ALU.is_ge, 0.0, 0, -1, "cmask_s")
    memset("pool", cmask_s[0:64, 64:128], 0.0, ["cmask_s"])
    memset("pool", onec[:, :], 1.0, ["onec"])
    memset("pool", epsc[:, :], EPS, ["epsc"])
    memset("pool", vbA[:, :, :, :], 1.0, [("vbA", j) for j in range(16)])
    memset("pool", vaA[:, :, :, :], 1.0, [("vaA", j) for j in range(8)])
    for i in range(2):
        memset("pool", vAn[i][:, :, :], 1.0, ["vAn%d" % i])
        memset("pool", carry[i][:, :, :], 0.0, ["carry%d" % i])
    memset("pool", rc[:, 0, :], 0.0, [("rc", 0)])

    misc8 = sb("misc8", [8, 640], F32)
    dma("sp", bfb[:, :], dap(T["b_f"], 0, [[0, 128], [1, 8]]), [], ["bfb"])
    dma("sp", misc8[:, 0:257], T["rel"].ap(), [], ["m8rel"])
    dma("sp", misc8[:, 384:512], T["g1"].ap().rearrange("o (c p) -> (o c) p", p=128), [], ["m8g1"])
    dma("sp", misc8[:, 512:640], T["g3"].ap().rearrange("o (c p) -> (o c) p", p=128), [], ["m8g3"])
    m8c = misc8[:, 256:257]
    cp("dve", misc8[:, 257:384], bass.AP(tensor=m8c.tensor, offset=m8c.offset, ap=[list(m8c.ap[0]), [0, 127]]), ["m8rel"], ["m8ext"])
    for gi_, (gt_, col_) in enumerate(((g1T, 384), (g3T, 512))):
        b = rotM.next()
        mm(psb[b][:, 0:8], misc8[0:8, col_:col_ + 128], identf[0:8, 0:8], True, True, ["m8g1", "m8g3", "identf"], bk(b))
        cp("dve", gt_[:, :], psb[b][:, 0:8], bk(b), ["g1T" if gi_ == 0 else "g3T"])
    ts("dve", zt[0:8, 0:8], identf[0:8, 0:8], misc8[0:8, 256:257], None, ALU.mult, None, ["identf", "m8rel"], ["zt"])
    b = rotM.next()
    mm(psb[b][:, 0:8], onesf[0:8, :], zt[0:8, 0:8], True, True, ["onesf", "zt"], bk(b))
    cp("dve", cbias[:, :], psb[b][:, 0:8], bk(b), ["cbias"])
    dma("sp", clfT[:, 0, :, :], T["clf"].ap()[0].rearrange("(t p) h -> p t h", p=128), [], ["clfT"])
    dma("sp", clfT[:, 1, :, :], T["clf"].ap()[1].rearrange("(t p) h -> p t h", p=128), [], ["clfT1"])
    for s_ in range(2):
        ck = "clfT" if s_ == 0 else "clfT1"
        memset("dve", rs[:, s_, 16, :], 0.0, [("rs", s_, 16)])
        for j in range(15, -1, -1):
            tt("dve", rs[:, s_, j, :], rs[:, s_, j + 1, :], clfT[:, s_, j, :], ALU.add, [("rs", s_, j + 1), ck], [("rs", s_, j)])
        b2 = rotM.next()
        for j in range(16):
            mm(psb[b2][:, j * 8:(j + 1) * 8], SU[:, :], clfT[:, s_, j, :], True, False, ["SU", ck], bk(b2))
            mm(psb[b2][:, j * 8:(j + 1) * 8], onesf[:, :], rs[:, s_, j + 1, :], False, True, ["onesf", ("rs", s_, j + 1)], bk(b2))
        cp("dve", bC[:, s_, :, :], psb[b2][:, 0:128].rearrange("p (j h) -> p j h", h=8), bk(b2), [("bC", s_)])
    dma("sp", eext.ap(), misc8[:, 0:384], ["m8rel", "m8ext"], ["eext0", "eext1"])
    for h in range(8):
        for dlt in range(2):
            base = 129 if dlt == 0 else 1
            hb = (h * 2 + dlt) % 2
            dma("sp", hk[hb][:, :], dap(eext, h * 384 + base, [[1, 128], [1, 128]]), ["eext0", "eext1"], ["hk%d" % hb])
            b = rotM.next()
            mm(psb[b][:, 0:128], Jm[:, :], hk[hb][:, :], True, True, ["Jm", "hk%d" % hb], bk(b))
            act(expB[:, h, dlt * 128:(dlt + 1) * 128], psb[b][:, 0:128], AF.Exp, bk(b), ["expB"])
    cp("dve", expBs[:, :, :], expB[:, :, 128:256], ["expB"], ["expBs"])
    memset("pool", expB[64:128, :, 128:192], 0.0, ["expB"])
    memset("pool", expBs[64:128, :, 0:64], 0.0, ["expBs"])
    memset("pool", expBs[0:64, :, 64:128], 0.0, ["expBs"])
    Uf = U[:, :, :].bitcast(F32)
    Uflat = Uf.rearrange("p a b -> p (a b)")
    allU = Ukeys(range(24), range(4))
    dma("sp", Uflat[0:3, 0:5632], T["convw"].ap(), [], allU)
    dma("sp", Uflat[3:4, 0:5632], T["convb"].ap(), [], ["ustg1"])
    dma("sp", Uflat[32:36, 0:5632], T["scv"].ap(), [], ["ustg2"])
    bA = rotM.next()
    for blk in range(44):
        mm(psb[bA][:, blk * 4:blk * 4 + 4], Uflat[0:4, blk * 128:(blk + 1) * 128], identf[0:4, 0:4], True, True,
           allU + ["ustg1", "identf"], bk(bA))
    cp("dve", cw[:, :, :], psb[bA][:, 0:176].rearrange("p (a b) -> p a b", b=4), bk(bA), ["cw"])
    bA = rotM.next()
    for blk in range(44):
        mm(psb[bA][:, blk * 4:blk * 4 + 4], Uflat[32:36, blk * 128:(blk + 1) * 128], identf[32:36, 32:36], True, True,
           allU + ["ustg2", "identf"], bk(bA))
    cp("dve", sprev[:, :, :], psb[bA][:, 0:176].rearrange("p (a b) -> p a b", b=4), bk(bA), ["sprev"])

    w_in_v = T["w_in"].ap().rearrange("(kc p) n -> p kc n", p=128)
    wup_v = T["wup"].ap().rearrange("(kc p) n -> p kc n", p=128)
    wout_v = T["wout"].ap().rearrange("(kc p) n -> p kc n", p=128)
    wpa_v = T["wpa"].ap().rearrange("(kc p) n -> p kc n", p=128)
    wpb_v = T["wpb"].ap().rearrange("(kc p) n -> p kc n", p=128)
    wdn_v = T["wdn"].ap().rearrange("(j p) n -> p j n", p=128)

    def wview8(i):
        return wbuf[i][:, :].rearrange("p (a b) -> p a b", a=8)

    def wview4(i):
        return wbuf[i][:, :].rearrange("p (a b) -> p a b", a=4)

    def wkeys(i):
        return [("wb", i, 0), ("wb", i, 1)]

    stash_ids = {}
    cur_gi = [0]

    def stash_idx(bid):
        if bid not in stash_ids:
            stash_ids[bid] = len(stash_ids)
        return stash_ids[bid]

    def stash_store(i, bid):
        idx = stash_idx(bid)
        dma("sp", stash.ap()[idx], wbuf[i][:, :], wkeys(i), [("stash", idx)])

    def stash_load(i, bid):
        idx = stash_idx(bid)
        dma("pool", wbuf[i][:, :], stash.ap()[idx], [("stash", idx)], wkeys(i))

    def load_w8(src, ncols, bid):
        i = rotW.next()
        if cur_gi[0] == 0:
            dma("pool", wview8(i)[:, :, 0:ncols], src, [], wkeys(i))
            stash_store(i, bid)
        else:
            stash_load(i, bid)
        return i

    evac_flip = [0]

    def evac_eng():
        evac_flip[0] ^= 1
        return "act" if evac_flip[0] else "dve"

    def rstd_from_ss(smt, smk, c_ss, c_out, n):
        act(smt[:, c_out:c_out + 1], smt[:, c_ss:c_ss + 1], AF.Ln, [(smk, c_ss), "epsc"], [(smk, c_out)],
            scale=1.0 / n, bias=epsc[:, 0:1])
        act(smt[:, c_out:c_out + 1], smt[:, c_out:c_out + 1], AF.Exp, [(smk, c_out)], [(smk, c_out)], scale=-0.5)

    def norm_tiles(lis, gT, gkey, c0):
        for li in lis:
            memset("dve", sm[li][:, c0:c0 + 1], 0.0, [("sm%d" % li, c0)])
        for li in lis:
            act(ot[:, :], xres[:, li, :], AF.Square, [("xres", li)], ["ot", ("sm%d" % li, c0)], accum_out=sm[li][:, c0:c0 + 1])
        for li in lis:
            act(sm[li][:, c0 + 1:c0 + 2], sm[li][:, c0:c0 + 1], AF.Ln, [("sm%d" % li, c0), "epsc"], [("sm%d" % li, c0 + 1)],
                scale=1.0 / 1024.0, bias=epsc[:, 0:1])
        for li in lis:
            act(sm[li][:, c0 + 1:c0 + 2], sm[li][:, c0 + 1:c0 + 2], AF.Exp, [("sm%d" % li, c0 + 1)], [("sm%d" % li, c0 + 1)], scale=-0.5)
        for li in lis:
            xb = rotXH.next()
            ts("dve", xh[xb][:, :], xres[:, li, :], sm[li][:, c0 + 1:c0 + 2], None, ALU.mult, None,
               [("xres", li), ("sm%d" % li, c0 + 1)], ["xh%d" % xb])
            b = rotT.next()
            pv = psbf(b)
            for kc in range(8):
                tr(pv[:, kc * 128:(kc + 1) * 128], xh[xb][:, kc * 128:(kc + 1) * 128], identb[:, :],
                   ["xh%d" % xb, "identb"], bk(b))
            tt("dve", actT[:, :, li * 128:(li + 1) * 128], pv[:, 0:1024].rearrange("p (c t) -> p c t", c=8),
               bc_last(gT[:, :], 128), ALU.mult, bk(b) + [gkey], [("actT", li)])

    def transpose_rows_to_U(src, srckey, nchunk, row0, li):
        b = rotT.next()
        pv = psbf(b)
        for c in range(nchunk):
            tr(pv[:, c * 128:(c + 1) * 128], src[:, c * 128:(c + 1) * 128], identb[:, :], [srckey, "identb"], bk(b))
        cp(evac_eng(), U[:, row0:row0 + nchunk, li * 128:(li + 1) * 128],
           pv[:, 0:nchunk * 128].rearrange("p (c t) -> p c t", c=nchunk), bk(b),
           Ukeys(range(row0, row0 + nchunk), [li]))

    groups = [("p", [0, 1, 2, 3]), ("p", [4, 5, 6, 7]), ("p", [8, 9, 10, 11]), ("p", [12, 13, 14, 15]), ("s", [0])]

    def chk(stage, gi):
        if stop == (stage, gi):
            raise _Stop()

    try:
      chk("setup", 0)
      for gi, (kind, tiles) in enumerate(groups[:ngroups] if ngroups > 0 else groups[ngroups:]):
          cur_gi[0] = gi
          NT = len(tiles)
          TK = NT * 128
          isS = kind == "s"
          lts = list(range(NT))

          for li, at in enumerate(tiles):
              src = T["xs"].ap() if isS else T["xp"].ap()[at * 128:(at + 1) * 128, :]
              dma("sp", xres[:, li, :], src, [], [("xres", li)])
          norm_tiles(lts, g1T, "g1T", 0)

          actT_keys = [("actT", li) for li in lts]

          chk("S1", gi)
          def fm_block(wi, dest_fn):
              w8 = wview8(wi)
              for c in range(4):
                  b = rotAll.next()
                  for kc in range(8):
                      mm(psb[b][:, 0:TK], w8[:, kc, c * 128:(c + 1) * 128], actT[:, kc, 0:TK], kc == 0, kc == 7,
                         wkeys(wi) + actT_keys, bk(b))
                  out, wk = dest_fn(c)
                  cp(evac_eng(), out, psb[b][:, 0:TK], bk(b), wk)

          def tm_block(wi, li, ncols=512):
              b = rotAll.next()
              w8 = wview8(wi)
              for kc in range(8):
                  mm(psb[b][:, 0:ncols], actT[:, kc, li * 128:(li + 1) * 128], w8[:, kc, 0:ncols], kc == 0, kc == 7,
                     wkeys(wi) + [("actT", li)], bk(b))
              return b

          def out_rows(name_p, name_s, at):
              if isS:
                  return T[name_s].ap()
              return T[name_p].ap()[at * 128:(at + 1) * 128, :]

          slot0 = (tiles[0] % 8)
          wi = load_w8(w_in_v[:, :, KA:KA + 512], 512, "KA")
          if isS:
              fm_block(wi, lambda c: (kTn[0][:, c, :], ["kTn0"]))
          else:
              fm_block(wi, lambda c: (kaT[:, c, slot0 * 128:slot0 * 128 + TK], [("kaT", (slot0 + t)) for t in lts]))
          for li, at in enumerate(tiles):
              if isS or at >= 12:
                  b = tm_block(wi, li)
                  s = rotStg.next()
                  cp(evac_eng(), stg[s][:, :], psb[b][:, :], bk(b), ["stg%d" % s])
                  dst = T["kas"].ap() if isS else T["kap"].ap()[(at - 12) * 128:(at - 11) * 128, :]
                  dma("sp", dst, stg[s][:, :], ["stg%d" % s], [])
          chk("S2a", gi)
          wi = load_w8(w_in_v[:, :, VA:VA + 512], 512, "VA")
          for li, at in enumerate(tiles):
              b = tm_block(wi, li)
              if isS:
                  vdst, vk = vAn[0][:, :, 0:64], "vAn0"
              else:
                  vdst, vk = vaA[:, at % 8, :, 0:64], ("vaA", at % 8)
              cp("dve", vdst, psb[b][:, :].rearrange("p (h d) -> p h d", h=8), bk(b), [vk])
              if isS or at >= 12:
                  s = rotStg.next()
                  cp("act", stg[s][:, :], psb[b][:, :], bk(b), ["stg%d" % s])
                  dst = T["vas"].ap() if isS else T["vap"].ap()[(at - 12) * 128:(at - 11) * 128, :]
                  dma("sp", dst, stg[s][:, :], ["stg%d" % s], [])
          chk("S2b", gi)
          wi = load_w8(w_in_v[:, :, KB:KB + 512], 512, "KB")
          if isS:
              fm_block(wi, lambda c: (kTn[1][:, c, :], ["kTn1"]))
          else:
              t0 = tiles[0] * 128
              fm_block(wi, lambda c: (kbT[:, c, t0:t0 + TK], [("kbT", at) for at in tiles]))
          chk("S2b1", gi)
          for li, at in enumerate(tiles):
              b = tm_block(wi, li)
              s = rotStg.next()
              cp(evac_eng(), stg[s][:, :], psb[b][:, :], bk(b), ["stg%d" % s])
              dma("sp", out_rows("kbp", "kbs", at), stg[s][:, :], ["stg%d" % s], [])
          chk("S2b2", gi)
          wi = load_w8(w_in_v[:, :, VB:VB + 512], 512, "VB")
          for li, at in enumerate(tiles):
              b = tm_block(wi, li)
              if isS:
                  vdst, vk = vAn[1][:, :, 0:64], "vAn1"
              else:
                  vdst, vk = vbA[:, at, :, 0:64], ("vbA", at)
              cp("dve", vdst, psb[b][:, :].rearrange("p (h d) -> p h d", h=8), bk(b), [vk])
              s = rotStg.next()
              cp("act", stg[s][:, :], psb[b][:, :], bk(b), ["stg%d" % s])
              dma("sp", out_rows("vbp", "vbs", at), stg[s][:, :], ["stg%d" % s], [])
          chk("S2c", gi)
          wi = load_w8(w_in_v[:, :, QA:QA + 512], 512, "QA")
          chk("S2c0", gi)
          fm_block(wi, lambda c: (U[:, c, 0:TK], Ukeys([c], lts)))
          chk("S2c1", gi)
          wi = load_w8(w_in_v[:, :, QB:QB + 512], 512, "QB")
          fm_block(wi, lambda c: (U[:, 4 + c, 0:TK], Ukeys([4 + c], lts)))
          chk("S2d", gi)
          wi = load_w8(w_in_v[:, :, FB:FB + 8], 8, "FB")
          for li, at in enumerate(tiles):
              b = tm_block(wi, li, ncols=8)
              tt("dve", zt[:, 0:8], psb[b][:, 0:8], bfb[:, :], ALU.add, bk(b) + ["bfb"], ["zt"])
              act(zt[:, 0:8], zt[:, 0:8], AF.Exp, ["zt"], ["zt"], scale=-1.0)
              act(zt[:, 0:8], zt[:, 0:8], AF.Ln, ["zt", "onec"], ["zt"], bias=onec[:, 0:1])
              ts("dve", lfg[:, li, :], zt[:, 0:8], -1.0, None, ALU.mult, None, ["zt"], [("lfg", li)])
              if not isS:
                  b2 = rotM.next()
                  mm(psb[b2][:, 0:8], tri[:, :], lfg[:, li, :], True, at == 0, ["tri", ("lfg", li)], bk(b2))
                  if at > 0:
                      mm(psb[b2][:, 0:8], onesf[:, :], rc[:, at, :], False, True, ["onesf", ("rc", at)], bk(b2))
                  tt("dve", rc[:, at + 1, :], rc[:, at, :], lfg[:, li, :], ALU.add, [("rc", at), ("lfg", li)], [("rc", at + 1)])
                  mm(psb[b2][:, 8:16], onesf[:, :], rc[:, at + 1, :], True, True, ["onesf", ("rc", at + 1)], bk(b2))
                  cp("dve", cc[:, at, :], psb[b2][:, 0:16], bk(b2), [("cc", at)])
          lf_dst = T["lfs"].ap() if isS else T["lfp"].ap()[tiles[0] * 128:tiles[0] * 128 + TK, :].rearrange("(t p) h -> p t h", p=128)
          if isS:
              dma("sp", lf_dst, lfg[:, 0, :], [("lfg", 0)], [])
          else:
              dma("sp", lf_dst, lfg[:, 0:NT, :], [("lfg", li) for li in lts], [])

          chk("S2", gi)
          rotST = {0: Rot([2, 3]), 64: Rot([4, 5])}

          def oreg(bank, r):
              return psb[bank][:, r * 65:(r + 1) * 65], ["ps%d" % bank]

          def normalize(h, o, ok, col0, smt, smk):
              P.op("dve", lambda e: e.reciprocal(out=smt[:, 8 + (h % 8):9 + (h % 8)], in_=o[:, 64:65]), ok, [(smk, 8 + h % 8)])
              ts("dve", ot[:, col0 + h * 64:col0 + (h + 1) * 64], o[:, 0:64], smt[:, 8 + (h % 8):9 + (h % 8)], None,
                 ALU.mult, None, ok + [(smk, 8 + h % 8)], ["ot"])

          def run_pipeline(units, skew=2):
              nun = len(units)
              for t_ in range(nun + skew):
                  if t_ < nun:
                      units[t_][0]()
                  if t_ - skew >= 0:
                      units[t_ - skew][1]()

          if not isS:
              units = []
              for li, at in enumerate(tiles):
                  smt, smk = sm[li], "sm%d" % li
                  qc = slice(li * 128, (li + 1) * 128)
                  n = rotNb.next()
                  nbv = nb[n]

                  def mk_bias(at=at, n=n, nbv=nbv, g=gi):
                      csrc = cc[:, at, 8:16]
                      j0 = 4 * g
                      tt("dve", nbv[:, j0:at + 1, :],
                         bass.AP(tensor=csrc.tensor, offset=csrc.offset, ap=[list(csrc.ap[0]), [0, at + 1 - j0], list(csrc.ap[1])]),
                         cc[:, j0:at + 1, 0:8], ALU.subtract, [("cc", j) for j in range(at + 1)], ["nb%d" % n])
                      if g > 0:
                          esrc = cc[:, 3, 8:16]
                          tt("dve", nbv[:, 0:g, :],
                             bass.AP(tensor=csrc.tensor, offset=csrc.offset, ap=[list(csrc.ap[0]), [0, g], list(csrc.ap[1])]),
                             bass.AP(tensor=esrc.tensor, offset=esrc.offset, ap=[list(esrc.ap[0]), [64, g], list(esrc.ap[1])]),
                             ALU.subtract, [("cc", j) for j in range(at + 1)] + ["nb%d" % n], ["nb%d" % n])

                  firstB = True
                  for hp in range(4):
                      bo = 6 + hp % 2
                      chunks = [(c0, hh) for c0 in range(0, at + 1, 4) for hh in range(2)]
                      for ci, (c0, hh) in enumerate(chunks):
                          js = list(range(c0, min(c0 + 4, at + 1)))
                          st_ = {}

                          def front(js=js, hh=hh, hp=hp, li=li, at=at, qc=qc, n=n, nbv=nbv, st_=st_, need_bias=firstB, mk_bias=mk_bias, gi_=gi):
                              if need_bias:
                                  mk_bias()
                              h = 2 * hp + hh
                              rlo = hh * 64
                              b = rotST[rlo].next()
                              for x, j in enumerate(js):
                                  mm(psb[b][:, x * 128:(x + 1) * 128], kbT[rlo:rlo + 64, hp, j * 128:(j + 1) * 128],
                                     U[rlo:rlo + 64, 4 + hp, qc], True, True, [("kbT", j), ("U", 4 + hp, li)], ["ps%d" % b])
                              if js[0] < 4 * gi_:
                                  w_ = rotPq.next()
                                  act(ptq[w_][:, :], psb[b][:, :], AF.Exp, ["ps%d" % b, "nb%d" % n], ["ptq%d" % w_],
                                      scale=0.125, bias=nbv[:, js[0] // 4, h:h + 1])
                                  st_["wide"] = w_
                              else:
                                  pl = []
                                  for x, j in enumerate(js):
                                      p = rotPt.next()
                                      pl.append(p)
                                      act(ptl[p][:, :], psb[b][:, x * 128:(x + 1) * 128], AF.Exp, ["ps%d" % b, "nb%d" % n], ["pt%d" % p],
                                          scale=0.125, bias=nbv[:, j, h:h + 1])
                                      if j == at:
                                          tt("dve", ptl[p][:, :], ptl[p][:, :], cmask[:, :], ALU.mult, ["pt%d" % p, "cmask"], ["pt%d" % p])
                                  st_["pl"] = pl

                          def back(js=js, hh=hh, hp=hp, bo=bo, st_=st_, first=(ci == 0), last=(ci == len(chunks) - 1),
                                   smt=smt, smk=smk):
                              h = 2 * hp + hh
                              o, ok = oreg(bo, hh)
                              for x, j in enumerate(js):
                                  if "wide" in st_:
                                      w_ = st_["wide"]
                                      mm(o, ptq[w_][:, x * 128:(x + 1) * 128], vbA[:, j, h, :], first and x == 0, False,
                                         ["ptq%d" % w_, ("vbA", j)], ok, skip=True)
                                  else:
                                      mm(o, ptl[st_["pl"][x]][:, :], vbA[:, j, h, :], first and x == 0, False,
                                         ["pt%d" % st_["pl"][x], ("vbA", j)], ok, skip=True)
                              if last:
                                  for h2 in range(2):
                                      o2, ok2 = oreg(bo, h2)
                                      normalize(2 * hp + h2, o2, ok2, 512, smt, smk)

                          units.append((front, back))
                          firstB = False
                  far = [j for j in (at - 4, at - 3, at - 2) if j >= 0]
                  near = [j for j in (at - 1, at) if j >= 0]
                  for hp in range(4):
                      bo = 6 + hp % 2
                      sub = []
                      for hh in range(2):
                          if far:
                              sub.append((hh, "far"))
                          sub.append((hh, "near"))
                      for ci, (hh, kind_) in enumerate(sub):
                          st_ = {}

                          def front(hh=hh, hp=hp, li=li, at=at, qc=qc, kind_=kind_, st_=st_, far=far, near=near):
                              h = 2 * hp + hh
                              rlo = hh * 64
                              jl = far if kind_ == "far" else near
                              b = rotST[rlo].next()
                              for x, j in enumerate(jl):
                                  mm(psb[b][:, x * 128:(x + 1) * 128], kaT[rlo:rlo + 64, hp, (j % 8) * 128:(j % 8 + 1) * 128],
                                     U[rlo:rlo + 64, hp, qc], True, True, [("kaT", j % 8), ("U", hp, li)], ["ps%d" % b])
                              nl = len(jl)
                              if kind_ == "far":
                                  w = rotPtw.next()
                                  act(ptw[w][:, 0:nl * 128], psb[b][:, 0:nl * 128], AF.Exp, ["ps%d" % b, "cbias"], ["ptw%d" % w],
                                      scale=0.125, bias=cbias[:, h:h + 1])
                                  if at - 4 >= 0:
                                      memset("dve", ptw[w][0:64, 64:128], 0.0, ["ptw%d" % w])
                                  st_["src"] = (ptw[w], "ptw%d" % w)
                              else:
                                  q = rotPn.next()
                                  act(pn32[q][:, 0:nl * 128], psb[b][:, 0:nl * 128], AF.Exp, ["ps%d" % b], ["pn32_%d" % q], scale=0.125)
                                  r = rotPtn.next()
                                  tt("dve", ptn[r][:, 0:nl * 128], pn32[q][:, 0:nl * 128], expB[:, h, 256 - nl * 128:256], ALU.mult,
                                     ["pn32_%d" % q, "expB"], ["ptn%d" % r])
                                  st_["src"] = (ptn[r], "ptn%d" % r)

                          def back(hh=hh, hp=hp, bo=bo, kind_=kind_, st_=st_, far=far, near=near, first=(ci == 0),
                                   last=(ci == len(sub) - 1), smt=smt, smk=smk, li=li, lastpair=(hp == 3)):
                              h = 2 * hp + hh
                              jl = far if kind_ == "far" else near
                              o, ok = oreg(bo, hh)
                              srcT, srck = st_["src"]
                              for x, j in enumerate(jl):
                                  mm(o, srcT[:, x * 128:(x + 1) * 128], vaA[:, j % 8, h, :], first and x == 0, False,
                                     [srck, ("vaA", j % 8)], ok, skip=True)
                              if last:
                                  for h2 in range(2):
                                      o2, ok2 = oreg(bo, h2)
                                      normalize(2 * hp + h2, o2, ok2, 0, smt, smk)
                                  if lastpair:
                                      transpose_rows_to_U(ot, "ot", 8, 8, li)

                          units.append((front, back))
              run_pipeline(units, skew=3)
              if gi < 3:
                  j0 = 4 * gi
                  esrc = cc[:, j0 + 3, 8:16]
                  tt("dve", facT[:, :, :],
                     bass.AP(tensor=esrc.tensor, offset=esrc.offset, ap=[list(esrc.ap[0]), [0, 4], list(esrc.ap[1])]),
                     cc[:, j0:j0 + 4, 0:8], ALU.subtract, [("cc", j0 + t) for t in range(4)], ["facT"])
                  act(facT[:, :, :], facT[:, :, :], AF.Exp, ["facT"], ["facT"])
                  for t in range(4):
                      tt("dve", vbA[:, j0 + t, :, :], vbA[:, j0 + t, :, :], bc_last(facT[:, t, :], 65), ALU.mult,
                         [("vbA", j0 + t), "facT"], [("vbA", j0 + t)])
          else:
              smt, smk = sm[0], "sm0"
              b2 = rotM.next()
              for s in range(2):
                  mm(psb[b2][:, s * 8:(s + 1) * 8], oseq[s][:, :], lfg[:, 0, :], True, True, ["oseq%d" % s, ("lfg", 0)], bk(b2))
              for s in range(2):
                  tsrc = psb[b2][:, s * 8:(s + 1) * 8]
                  tt("dve", bC[:, s, :, :], bC[:, s, :, :],
                     bass.AP(tensor=tsrc.tensor, offset=tsrc.offset, ap=[list(tsrc.ap[0]), [0, 16], list(tsrc.ap[1])]),
                     ALU.add, bk(b2) + [("bC", s)], [("bC", s)])
              b2 = rotM.next()
              mm(psb[b2][:, 0:8], SUb[:, :], lfg[:, 0, :], True, True, ["SUb", ("lfg", 0)], bk(b2))
              cp("dve", bN[:, :], psb[b2][:, 0:8], bk(b2), ["bN"])

              def sreg(h):
                  return oreg(6 + h // 4, h % 4)

              def sample_branch(br):
                  ncache = 4 if br == 0 else 16
                  ksrc = T["cka"] if br == 0 else T["ckb"]
                  vsrc = T["cva"] if br == 0 else T["cvb"]
                  qrow0 = 0 if br == 0 else 4
                  col0 = 0 if br == 0 else 512
                  if br == 1:
                      kcv = kbT[:, :, :].rearrange("p a b -> p (a b)").rearrange("p (j c) -> p j c", j=16)
                      vv, vname = vbA, "vbA"
                      allk = [("kbT", t) for t in range(16)]
                  else:
                      kcv = kaT[:, :, :].rearrange("p a b -> p (a b)").rearrange("p (j c) -> p j c", j=8)
                      vv, vname = vaA, "vaA"
                      allk = [("kaT", t) for t in range(8)]

                  def tile_of(s, j):
                      return j if br == 1 else 4 * s + j

                  def load_chunk(s, cj):
                      jj0 = tile_of(s, 4 * cj)
                      r0 = 4 * cj * 128
                      dma("pool", kcv[:, jj0:jj0 + 4, :], ksrc.ap()[s, r0:r0 + 512, :].rearrange("(j p) c -> p j c", p=128),
                          [], allk + [("skc", br, jj0 + x) for x in range(4)])
                      for x in range(4):
                          dma("pool", vv[:, jj0 + x, :, 0:64],
                              vsrc.ap()[s, r0 + x * 128:r0 + (x + 1) * 128, :].rearrange("p (h d) -> p h d", h=8),
                              [], [(vname, jj0 + x)])

                  if br == 1:
                      for cj in range(4):
                          load_chunk(0, cj)
                  else:
                      load_chunk(0, 0)
                      load_chunk(1, 0)
                  for s in range(2):
                      units = []
                      started = {6: False, 7: False}
                      kbst = {}
                      for j in range(ncache + 1):
                          for half in range(2):
                              st_ = {}

                              def front(j=j, half=half, s=s, st_=st_, kbst=kbst):
                                  isnew = (j == ncache)
                                  if half == 0 and not isnew:
                                      kb_ = rotKc.next()
                                      kbst[j] = kb_
                                      jj = tile_of(s, j)
                                      b = rotT.next()
                                      pv = psbf(b)
                                      for c in range(4):
                                          tr(pv[:, c * 128:(c + 1) * 128], kcv[:, jj, c * 128:(c + 1) * 128], identb[:, :],
                                             [("skc", br, jj), "identb"], bk(b))
                                      cp(evac_eng(), kcT[kb_][:, :, :], pv[:, 0:512].rearrange("p (c t) -> p c t", c=4), bk(b),
                                         ["kcT%d" % kb_])
                                  hs = list(range(half * 4, half * 4 + 4))
                                  bl = {0: rotST[0].next(), 64: rotST[64].next()}
                                  info = []
                                  for h in hs:
                                      hp, rlo = h // 2, (h % 2) * 64
                                      x = (h - half * 4) // 2
                                      b = bl[rlo]
                                      if isnew:
                                          stv = psb[b][:, x * 128:(x + 1) * 128]
                                          mm(stv, kTn[br][rlo:rlo + 64, hp, :], U[rlo:rlo + 64, qrow0 + hp, 0:128], True, True,
                                             ["kTn%d" % br, ("U", qrow0 + hp, 0)], ["ps%d" % b])
                                      else:
                                          stv = psb[b][:, x * 64:(x + 1) * 64]
                                          kb_ = kbst[j]
                                          mm(stv, kcT[kb_][rlo:rlo + 64, hp, :], U[rlo:rlo + 64, qrow0 + hp, s * 64:(s + 1) * 64], True, True,
                                             ["kcT%d" % kb_, ("U", qrow0 + hp, 0)], ["ps%d" % b])
                                      info.append((h, b, stv))
                                  pl = []
                                  for h, b, stv in info:
                                      p = rotPt.next()
                                      pk = "pt%d" % p
                                      if isnew:
                                          if br == 1:
                                              act(ptl[p][:, :], stv, AF.Exp, ["ps%d" % b, "bN"], [pk], scale=0.125, bias=bN[:, h:h + 1])
                                              tt("dve", ptl[p][:, :], ptl[p][:, :], cmask_s[:, :], ALU.mult, [pk, "cmask_s"], [pk])
                                          else:
                                              q = rotPn.next()
                                              act(pn32[q][:, 0:128], stv, AF.Exp, ["ps%d" % b], ["pn32_%d" % q], scale=0.125)
                                              tt("dve", ptl[p][:, :], pn32[q][:, 0:128], expBs[:, h, :], ALU.mult,
                                                 ["pn32_%d" % q, "expBs"], [pk])
                                      elif br == 1:
                                          act(ptl[p][:, s * 64:(s + 1) * 64], stv, AF.Exp, ["ps%d" % b, ("bC", s)], [pk], scale=0.125,
                                              bias=bC[:, s, j, h:h + 1])
                                      elif j < 3:
                                          act(ptl[p][:, s * 64:(s + 1) * 64], stv, AF.Exp, ["ps%d" % b, "cbias"], [pk], scale=0.125,
                                              bias=cbias[:, h:h + 1])
                                      else:
                                          q = rotPn.next()
                                          act(pn32[q][:, 0:64], stv, AF.Exp, ["ps%d" % b], ["pn32_%d" % q], scale=0.125)
                                          tt("dve", ptl[p][:, s * 64:(s + 1) * 64], pn32[q][:, 0:64], expB[:, h, 0:64],
                                             ALU.mult, ["pn32_%d" % q, "expB"], [pk])
                                      pl.append((h, p))
                                  st_["pl"] = pl

                              def back(j=j, half=half, s=s, st_=st_, kbst=kbst, started=started):
                                  isnew = (j == ncache)
                                  for h, p in st_["pl"]:
                                      o, ok = sreg(h)
                                      bo = 6 + h // 4
                                      if isnew:
                                          mm(o, ptl[p][:, :], vAn[br][:, h, :], not started[bo], False, ["pt%d" % p, "vAn%d" % br], ok, skip=True)
                                      else:
                                          jj = tile_of(s, j)
                                          mm(o, ptl[p][:, :], vv[:, jj, h, :], not started[bo], False, ["pt%d" % p, (vname, jj)], ok, skip=True)
                                      started[bo] = True
                                  if br == 1 and s == 0 and half == 1 and (not isnew) and j % 4 == 3:
                                      load_chunk(1, j // 4)

                              units.append((front, back))
                      run_pipeline(units, skew=2)
                      rs_ = slice(s * 64, (s + 1) * 64)
                      for h in range(8):
                          o, ok = sreg(h)
                          P.op("dve", lambda e, o=o, h=h, rs_=rs_: e.reciprocal(out=smt[rs_, 8 + h:9 + h], in_=o[rs_, 64:65]),
                               ok, [(smk, 8 + h)])
                          ts("dve", ot[rs_, col0 + h * 64:col0 + (h + 1) * 64], o[rs_, 0:64], smt[rs_, 8 + h:9 + h], None,
                             ALU.mult, None, ok + [(smk, 8 + h)], ["ot"])

              sample_branch(1)
              sample_branch(0)
              transpose_rows_to_U(ot, "ot", 8, 8, 0)

          chk("S3", gi)
          s4units = []
          for cb in range(2):
              wst = {}
              for li in lts:
                  st_ = {}

                  def front(cb=cb, li=li, wst=wst, st_=st_, first=(li == 0)):
                      if first:
                          wab_ = rotW.next()
                          if cur_gi[0] == 0:
                              dma("pool", wview8(wab_)[:, 0:4, :], wpa_v[:, :, cb * 512:(cb + 1) * 512], [], [("wb", wab_, 0)])
                              dma("pool", wview8(wab_)[:, 4:8, :], wpb_v[:, :, cb * 512:(cb + 1) * 512], [], [("wb", wab_, 1)])
                              stash_store(wab_, ("wab", cb))
                          else:
                              stash_load(wab_, ("wab", cb))
                          wst["wab"] = wab_
                          wst["wga"] = load_w8(w_in_v[:, :, GA + cb * 512:GA + (cb + 1) * 512], 512, ("wga", cb))
                          wst["wgb"] = load_w8(w_in_v[:, :, GB + cb * 512:GB + (cb + 1) * 512], 512, ("wgb", cb))
                      wab, wga, wgb = wst["wab"], wst["wga"], wst["wgb"]
                      tsl = slice(li * 128, (li + 1) * 128)
                      bpa, bpb, bga, bgb = rotAll.next(), rotAll.next(), rotAll.next(), rotAll.next()
                      for c in range(4):
                          mm(psb[bpa][:, :], U[:, 8 + c, tsl], wview8(wab)[:, c, :], c == 0, c == 3,
                             [("U", 8 + c, li), ("wb", wab, 0)], bk(bpa))
                      for c in range(4):
                          mm(psb[bpb][:, :], U[:, 12 + c, tsl], wview8(wab)[:, 4 + c, :], c == 0, c == 3,
                             [("U", 12 + c, li), ("wb", wab, 1)], bk(bpb))
                      for kc in range(8):
                          mm(psb[bga][:, :], actT[:, kc, tsl], wview8(wga)[:, kc, :], kc == 0, kc == 7,
                             [("actT", li)] + wkeys(wga), bk(bga))
                      for kc in range(8):
                          mm(psb[bgb][:, :], actT[:, kc, tsl], wview8(wgb)[:, kc, :], kc == 0, kc == 7,
                             [("actT", li)] + wkeys(wgb), bk(bgb))
                      sA, sB = yy[0][0], yy[1][0]
                      act(sA[:, :], psb[bga][:, :], AF.Sigmoid, bk(bga), ["yy0_0"])
                      act(sB[:, :], psb[bgb][:, :], AF.Sigmoid, bk(bgb), ["yy1_0"])
                      tt("dve", sA[:, :], psb[bpa][:, :], sA[:, :], ALU.mult, bk(bpa) + ["yy0_0"], ["yy0_0"])
                      tt("dve", sB[:, :], psb[bpb][:, :], sB[:, :], ALU.mult, bk(bpb) + ["yy1_0"], ["yy1_0"])
                      m = rotMt.next()
                      tt("dve", mt[m][:, :], sA[:, :], sB[:, :], ALU.add, ["yy0_0", "yy1_0"], ["mt%d" % m])
                      st_["m"] = m

                  def back(cb=cb, li=li, st_=st_):
                      m = st_["m"]
                      transpose_rows_to_U(mt[m], "mt%d" % m, 4, 16 + cb * 4, li)

                  s4units.append((front, back))
          for t_ in range(len(s4units) + 1):
              if t_ < len(s4units):
                  s4units[t_][0]()
              if t_ >= 1:
                  s4units[t_ - 1][1]()

          chk("S4", gi)
          dma("sp", gbb[:, :], dap(T["g2"], 0, [[0, 128], [1, 1024]]), [], ["gbb"])
          wo = [load_w8(wout_v[:, :, hf * 512:(hf + 1) * 512], 512, ("wo", hf)) for hf in range(2)]
          bms = {}
          for li in lts:
              tsl = slice(li * 128, (li + 1) * 128)
              bms[li] = [2 * li, 2 * li + 1]
              for hf in range(2):
                  for kc in range(8):
                      mm(psb[bms[li][hf]][:, :], U[:, 16 + kc, tsl], wview8(wo[hf])[:, kc, :], kc == 0, kc == 7,
                         [("U", 16 + kc, li)] + wkeys(wo[hf]), bk(bms[li][hf]))
              smt, smk = sm[li], "sm%d" % li
              memset("dve", smt[:, 2:4], 0.0, [(smk, 2), (smk, 3)])
              for hf in range(2):
                  act(ot[:, hf * 512:(hf + 1) * 512], psb[bms[li][hf]][:, :], AF.Square, bk(bms[li][hf]), ["ot", (smk, 2 + hf)],
                      accum_out=smt[:, 2 + hf:3 + hf])
              tt("dve", smt[:, 2:3], smt[:, 2:3], smt[:, 3:4], ALU.add, [(smk, 2), (smk, 3)], [(smk, 2)])
          for li in lts:
              smt, smk = sm[li], "sm%d" % li
              act(smt[:, 3:4], smt[:, 2:3], AF.Ln, [(smk, 2), "epsc"], [(smk, 3)], scale=1.0 / 1024.0, bias=epsc[:, 0:1])
          for li in lts:
              smt, smk = sm[li], "sm%d" % li
              act(smt[:, 3:4], smt[:, 3:4], AF.Exp, [(smk, 3)], [(smk, 3)], scale=-0.5)
          for li in lts:
              smt, smk = sm[li], "sm%d" % li
              for hf in range(2):
                  stt("dve", yy[hf][1][:, :], psb[bms[li][hf]][:, :], smt[:, 3:4], gbb[:, hf * 512:(hf + 1) * 512],
                      ALU.mult, ALU.mult, bk(bms[li][hf]) + [(smk, 3), "gbb"], ["yy%d_1" % hf])
                  tt("dve", xres[:, li, hf * 512:(hf + 1) * 512], xres[:, li, hf * 512:(hf + 1) * 512], yy[hf][1][:, :], ALU.add,
                     [("xres", li), "yy%d_1" % hf], [("xres", li)])
              norm_tiles([li], g3T, "g3T", 4)

          chk("S5", gi)
          nseg = 2 if isS else 1
          seglen = 64 if isS else TK
          cr = carry[1] if isS else carry[0]
          crk = "carry1" if isS else "carry0"
          for blk in range(6):
              ncol = 512 if blk < 5 else 256
              wg = load_w8(wup_v[:, :, blk * 512:blk * 512 + ncol], ncol, ("upg", blk))
              wv = load_w8(wup_v[:, :, FF + blk * 512:FF + blk * 512 + ncol], ncol, ("upv", blk))
              for jj in range(ncol // 128):
                  j = blk * 4 + jj
                  ci = rotC.next()
                  ys = []
                  for a, wi_ in ((0, wg), (1, wv)):
                      fidx = j if a == 0 else 22 + j
                      b = rotAll.next()
                      for kc in range(8):
                          mm(psb[b][:, 0:TK], wview8(wi_)[:, kc, jj * 128:(jj + 1) * 128], actT[:, kc, 0:TK], kc == 0, kc == 7,
                             wkeys(wi_) + actT_keys, bk(b))
                      ue = uext[a][0]
                      uk = "uext%d_0" % a
                      y = yy[a][ci]
                      yk = "yy%d_%d" % (a, ci)
                      uev = ue[:, 0:nseg * (seglen + 2)].rearrange("p (s t) -> p s t", s=nseg)
                      psv = psb[b][:, 0:TK].rearrange("p (s t) -> p s t", s=nseg)
                      yv = y[:, 0:TK].rearrange("p (s t) -> p s t", s=nseg)
                      act(y[:, 0:TK], psb[b][:, 0:TK], AF.Identity, bk(b) + ["cw"], [yk], scale=cw[:, fidx, 2:3], bias=cw[:, fidx, 3:4])
                      cp("act", uev[:, :, 2:2 + seglen], psv, bk(b), [uk])
                      if isS:
                          cp("act", uev[:, :, 0:2], sprev[:, fidx, :].rearrange("p (s t) -> p s t", s=2), ["sprev"], [uk])
                      else:
                          cp("act", uev[:, :, 0:2], cr[:, fidx, 0:2].rearrange("p (s t) -> p s t", s=1), [crk], [uk])
                      ce = "dve"
                      stt(ce, yv, uev[:, :, 1:1 + seglen], cw[:, fidx, 1:2], yv, ALU.mult, ALU.add, [uk, "cw", yk], [yk])
                      stt(ce, yv, uev[:, :, 0:seglen], cw[:, fidx, 0:1], yv, ALU.mult, ALU.add, [uk, "cw", yk], [yk])
                      if isS:
                          cp("act", cr[:, fidx, :].rearrange("p (s t) -> p s t", s=2), uev[:, :, seglen:seglen + 2], [uk], [crk])
                      else:
                          cp("act", cr[:, fidx, 0:2].rearrange("p (s t) -> p s t", s=1), uev[:, :, seglen:seglen + 2], [uk], [crk])
                      ys.append((y, yk))
                  (yg, ygk), (yvv, yvk) = ys
                  act(yg[:, 0:TK], yg[:, 0:TK], AF.Gelu_apprx_tanh, [ygk], [ygk])
                  tt("dve", U[:, j, 0:TK], yg[:, 0:TK], yvv[:, 0:TK], ALU.mult, [ygk, yvk], Ukeys([j], lts))
          if isS or gi == 3:
              nr = 4 if isS else 2
              dstT = T["cvs"] if isS else T["cvp"]
              for q4 in range(11):
                  b = rotAll.next()
                  for x in range(4):
                      f = q4 * 4 + x
                      mm(psb[b][0:nr, x * 128:(x + 1) * 128], cr[:, f, 0:nr], identf[:, :], True, True, [crk, "identf"], bk(b))
                  cv = rotStg.next()
                  cp("dve", stg[cv][0:nr, :], psb[b][0:nr, :], bk(b), ["stg%d" % cv])
                  dma("sp", dstT.ap()[:, q4 * 512:(q4 + 1) * 512], stg[cv][0:nr, :], ["stg%d" % cv], [])

          chk("S6", gi)
          dma("sp", gbb[:, :], dap(T["g4"], 0, [[0, 128], [1, 1024]]), [], ["gbb"])
          for blk in range(6):
              nj = 4 if blk < 5 else 2
              wi_ = rotW.next()
              if cur_gi[0] == 0:
                  dma("pool", wview4(wi_)[:, 0:nj, :], wdn_v[:, blk * 4:blk * 4 + nj, :], [], wkeys(wi_))
                  stash_store(wi_, ("dn", blk))
              else:
                  stash_load(wi_, ("dn", blk))
              for jj in range(nj):
                  j = blk * 4 + jj
                  for li in lts:
                      for hf in range(2):
                          b = 2 * li + hf
                          mm(psb[b][:, :], U[:, j, li * 128:(li + 1) * 128], wview4(wi_)[:, jj, hf * 512:(hf + 1) * 512],
                             j == 0, j == 21, [("U", j, li)] + wkeys(wi_), bk(b))
          for li, at in enumerate(tiles):
              smt, smk = sm[li], "sm%d" % li
              memset("dve", smt[:, 6:8], 0.0, [(smk, 6), (smk, 7)])
              for hf in range(2):
                  b = 2 * li + hf
                  act(ot[:, hf * 512:(hf + 1) * 512], psb[b][:, :], AF.Square, bk(b), ["ot", (smk, 6 + hf)],
                      accum_out=smt[:, 6 + hf:7 + hf])
              tt("dve", smt[:, 6:7], smt[:, 6:7], smt[:, 7:8], ALU.add, [(smk, 6), (smk, 7)], [(smk, 6)])
          for li in lts:
              smt, smk = sm[li], "sm%d" % li
              act(smt[:, 7:8], smt[:, 6:7], AF.Ln, [(smk, 6), "epsc"], [(smk, 7)], scale=1.0 / 1024.0, bias=epsc[:, 0:1])
          for li in lts:
              smt, smk = sm[li], "sm%d" % li
              act(smt[:, 7:8], smt[:, 7:8], AF.Exp, [(smk, 7)], [(smk, 7)], scale=-0.5)
          for li, at in enumerate(tiles):
              smt, smk = sm[li], "sm%d" % li
              for hf in range(2):
                  b = 2 * li + hf
                  stt("dve", yy[hf][1][:, :], psb[b][:, :], smt[:, 7:8], gbb[:, hf * 512:(hf + 1) * 512],
                      ALU.mult, ALU.mult, bk(b) + [(smk, 7), "gbb"], ["yy%d_1" % hf])
                  tt("dve", xres[:, li, hf * 512:(hf + 1) * 512], xres[:, li, hf * 512:(hf + 1) * 512], yy[hf][1][:, :], ALU.add,
                     [("xres", li), "yy%d_1" % hf], [("xres", li)])
              dst = T["ys"].ap() if isS else T["yp"].ap()[at * 128:(at + 1) * 128, :]
              dma("sp", dst, xres[:, li, :], [("xres", li)], [])
    except _Stop:
        pass
    if os.environ.get('DBG_MEM'):
        print('SBUF remaining', nc.sbuf_bytes_remaining, 'base', nc.sbuf_base, 'top', nc.sbuf_top)
    P.emit(nc)
    st.close()
    return nc


_NC = None


def kernel(x_prompt, x_sample, cache_k_a, cache_v_a, cache_k_b, cache_v_b, cache_logf_b, state_conv_ffn,
           g_pre_mix, g_post_mix, g_pre_ffn, g_post_ffn, w_in, b_f, rel_table, w_proj_a, w_proj_b, w_out,
           w_up, conv_w, conv_b, w_down):
    global _NC
    f = lambda a: np.ascontiguousarray(np.asarray(a, dtype=np.float32))
    if _NC is None:
        _NC = build()
    nc = _NC
    shared = {
        "g1": f(g_pre_mix), "g2": f(g_post_mix), "g3": f(g_pre_ffn), "g4": f(g_post_ffn),
        "w_in": f(w_in)[0], "b_f": f(b_f), "rel": f(rel_table)[0],
        "wpa": f(w_proj_a)[0], "wpb": f(w_proj_b)[0], "wout": f(w_out)[0], "wup": f(w_up)[0],
        "convw": f(conv_w)[0], "convb": f(conv_b), "wdn": f(w_down)[0],
    }
    xp, xs = f(x_prompt), f(x_sample)
    cka, cva, ckb, cvb = f(cache_k_a)[0], f(cache_v_a)[0], f(cache_k_b)[0], f(cache_v_b)[0]
    clf, scv = f(cache_logf_b)[0], f(state_conv_ffn)[0]
    in_maps = []
    for c in range(8):
        m = dict(shared)
        m["xp"] = xp[c]
        m["xs"] = xs[2 * c:2 * c + 2].reshape(128, 1024)
        m["cka"] = cka[2 * c:2 * c + 2].reshape(2, 512, 512)
        m["cva"] = cva[2 * c:2 * c + 2].reshape(2, 512, 512)
        m["ckb"] = ckb[2 * c:2 * c + 2].reshape(2, 2048, 512)
        m["cvb"] = cvb[2 * c:2 * c + 2].reshape(2, 2048, 512)
        m["clf"] = clf[2 * c:2 * c + 2]
        m["scv"] = scv[2 * c:2 * c + 2].reshape(4, 5632)
        in_maps.append(m)
    res = run_bass_kernel_spmd(nc, in_maps, core_ids=list(range(8)))
    R = res.results
    cat = lambda n: np.stack([np.asarray(R[c][n], dtype=np.float32) for c in range(8)])
    y_p = cat("yp")
    y_s = cat("ys").reshape(16, 64, 1024)
    ka_p = cat("kap").reshape(1, 8, 512, 8, 64)
    va_p = cat("vap").reshape(1, 8, 512, 8, 64)
    kb_p = cat("kbp").reshape(1, 8, 2048, 8, 64)
    vb_p = cat("vbp").reshape(1, 8, 2048, 8, 64)
    lf_p = cat("lfp").reshape(1, 8, 2048, 8)
    cv_p = cat("cvp").reshape(1, 8, 2, 5632)
    ka_s = cat("kas").reshape(1, 16, 64, 8, 64)
    va_s = cat("vas").reshape(1, 16, 64, 8, 64)
    kb_s = cat("kbs").reshape(1, 16, 64, 8, 64)
    vb_s = cat("vbs").reshape(1, 16, 64, 8, 64)
    lf_s = cat("lfs").reshape(1, 16, 64, 8)
    cv_s = cat("cvs").reshape(1, 16, 2, 5632)
    return (y_p, y_s, ka_p, va_p, kb_p, vb_p, lf_p, cv_p, ka_s, va_s, kb_s, vb_s, lf_s, cv_s)
```
